# Optimizing a Trainium2 kernel written in Bass

```python
import jax
import jax.numpy as jnp
from jax import lax
import numpy as np

D_MODEL = 1024
BATCH = 8
SEQ = 2048
DEPTH = 2

D_MIX = D_MODEL
N_MIXERS = 4
GROUP_W = D_MIX // N_MIXERS
HEAD_DIM = 64
GROUP_HEADS = GROUP_W // HEAD_DIM
D_FF = 256 * ((8 * D_MODEL // 3 + 255) // 256)
ALPHA = (2.0 * DEPTH) ** 0.25
BETA = (8.0 * DEPTH) ** -0.25
MLA_HEADS = GROUP_HEADS
MLA_NOPE = HEAD_DIM
MLA_ROPE = HEAD_DIM // 2
MLA_V = GROUP_W // MLA_HEADS
Q_LORA = D_MODEL // 4
KV_LORA = D_MODEL // 8
ROPE_THETA = 10000.0
Q_BLOCK = 128
LRU_HEADS = GROUP_HEADS
LRU_BLOCK = GROUP_W // LRU_HEADS
CONV_W = 4
LRU_C = 8.0
SWA_HEADS = GROUP_HEADS
DILATED_PATTERNS = ((128, 1), (512, 4), (2048, 16))
ML_HEADS = GROUP_HEADS
ML_CHUNK = 64
IN_SIZES = (Q_LORA, KV_LORA, MLA_ROPE, GROUP_W, GROUP_W, GROUP_W, GROUP_W, GROUP_W, GROUP_W, GROUP_W, GROUP_W, GROUP_W, ML_HEADS, ML_HEADS)
N_IN = sum(IN_SIZES)

kernel_name = 'hybrid_parallel_mla_rglru_dilated_mlstm_macaron'


def _layernorm(x, g, b, eps=1e-5):
    xf = x.astype(jnp.float32)
    mu = jnp.mean(xf, axis=-1, keepdims=True)
    var = jnp.mean(jnp.square(xf - mu), axis=-1, keepdims=True)
    return ((xf - mu) * lax.rsqrt(var + eps) * g + b).astype(x.dtype)


def _rmsnorm(x, g, eps=1e-6):
    xf = x.astype(jnp.float32)
    return (xf * lax.rsqrt(jnp.mean(xf * xf, axis=-1, keepdims=True) + eps) * g).astype(x.dtype)


def _group_rmsnorm(y, g, eps=1e-6):
    B, S, _ = y.shape
    yf = y.astype(jnp.float32).reshape(B, S, N_MIXERS, GROUP_W)
    yf = yf * lax.rsqrt(jnp.mean(yf * yf, axis=-1, keepdims=True) + eps)
    return (yf.reshape(B, S, D_MIX) * g).astype(y.dtype)


def _swiglu(x, w1, w3, w2):
    return (jax.nn.silu(x @ w1) * (x @ w3)) @ w2


def _split_columns(z):
    parts, start = [], 0
    for size in IN_SIZES:
        parts.append(z[..., start:start + size])
        start += size
    return parts


def _rope(x, pos):
    half = x.shape[-1] // 2
    freqs = ROPE_THETA ** (-jnp.arange(half, dtype=jnp.float32) / half)
    ang = pos.astype(jnp.float32)[:, None] * freqs[None, :]
    cos, sin = jnp.cos(ang)[:, None, :], jnp.sin(ang)[:, None, :]
    x1, x2 = x[..., :half].astype(jnp.float32), x[..., half:].astype(jnp.float32)
    return jnp.concatenate([x1 * cos - x2 * sin, x1 * sin + x2 * cos], axis=-1).astype(x.dtype)


def _causal_block_attention(q, k, v, scale):
    B, S, H, E = q.shape
    nb = S // Q_BLOCK
    qb = jnp.moveaxis(q.reshape(B, nb, Q_BLOCK, H, E), 1, 0)
    kpos = jnp.arange(S)

    def one_block(args):
        qi, i = args
        s = jnp.einsum('bqhe,bkhe->bhqk', qi, k, preferred_element_type=jnp.float32) * scale
        qpos = i * Q_BLOCK + jnp.arange(Q_BLOCK)
        s = jnp.where(kpos[None, :] <= qpos[:, None], s, -jnp.inf)
        p = jax.nn.softmax(s, axis=-1)
        return jnp.einsum('bhqk,bkhe->bqhe', p.astype(v.dtype), v)

    out = lax.map(one_block, (qb, jnp.arange(nb)))
    return jnp.moveaxis(out, 0, 1).reshape(B, S, H, v.shape[-1])


def _mla(c_q, c_kv, k_rope, q_norm, kv_norm, w_uq, w_ukv):
    B, S, _ = c_q.shape
    pos = jnp.arange(S)
    q = (_rmsnorm(c_q, q_norm) @ w_uq).reshape(B, S, MLA_HEADS, MLA_NOPE + MLA_ROPE)
    kv = (_rmsnorm(c_kv, kv_norm) @ w_ukv).reshape(B, S, MLA_HEADS, MLA_NOPE + MLA_V)
    q = jnp.concatenate([q[..., :MLA_NOPE], _rope(q[..., MLA_NOPE:], pos)], axis=-1)
    k_r = jnp.broadcast_to(_rope(k_rope[:, :, None, :], pos), (B, S, MLA_HEADS, MLA_ROPE))
    k = jnp.concatenate([kv[..., :MLA_NOPE], k_r], axis=-1)
    v = kv[..., MLA_NOPE:]
    o = _causal_block_attention(q, k, v, (MLA_NOPE + MLA_ROPE) ** -0.5)
    return o.reshape(B, S, MLA_HEADS * MLA_V)


def _rglru(xb, gate, conv_w, conv_b, w_a, b_a, w_x, b_x, lam):
    B, S, W = xb.shape
    xc = lax.conv_general_dilated(xb, conv_w[:, None, :], window_strides=(1,), padding=((CONV_W - 1, 0),),
                                  dimension_numbers=('NWC', 'WIO', 'NWC'), feature_group_count=W) + conv_b
    xh = xc.reshape(B, S, LRU_HEADS, LRU_BLOCK)
    r = jax.nn.sigmoid(jnp.einsum('bshi,hij->bshj', xh, w_a).reshape(B, S, W) + b_a)
    i = jax.nn.sigmoid(jnp.einsum('bshi,hij->bshj', xh, w_x).reshape(B, S, W) + b_x)
    log_a = -LRU_C * r.astype(jnp.float32) * jax.nn.softplus(-lam.astype(jnp.float32))
    a = jnp.exp(log_a)
    u = jnp.sqrt(-jnp.expm1(2.0 * log_a)) * (i * xc).astype(jnp.float32)

    def combine(left, right):
        a1, b1 = left
        a2, b2 = right
        return a1 * a2, a2 * b1 + b2

    _, h = lax.associative_scan(combine, (a, u), axis=1)
    return (h * jax.nn.gelu(gate.astype(jnp.float32))).astype(xb.dtype)


def _band_attention(q, k, v, back):
    N, L, H, E = q.shape
    blk = back
    nb = -(-L // blk)
    pad = nb * blk - L
    padf = lambda t: jnp.pad(t, ((0, 0), (0, pad), (0, 0), (0, 0))).reshape(N, nb, blk, H, E)
    qb, kb, vb = padf(q), padf(k), padf(v)

    def with_prev(t):
        prev = jnp.concatenate([jnp.zeros_like(t[:, :1]), t[:, :-1]], axis=1)
        return jnp.concatenate([prev, t], axis=2)

    kk, vv = with_prev(kb), with_prev(vb)
    s = jnp.einsum('nbqhe,nbkhe->nbhqk', qb, kk, preferred_element_type=jnp.float32) * (E ** -0.5)
    a_idx = jnp.arange(blk)[:, None]
    c_idx = jnp.arange(2 * blk)[None, :]
    rel = a_idx + blk - c_idx
    blk_id = jnp.arange(nb)[:, None, None]
    mask = (rel >= 0) & (rel <= back) & ((blk_id > 0) | (c_idx >= blk))
    s = jnp.where(mask[None, :, None], s, -jnp.inf)
    m = jnp.max(s, axis=-1, keepdims=True)
    p = jnp.exp(s - m)
    l = jnp.sum(p, axis=-1, keepdims=True)
    o = jnp.einsum('nbhqk,nbkhe->nbqhe', p / l, vv.astype(jnp.float32))
    lse = (m + jnp.log(l))[..., 0]
    o = o.reshape(N, nb * blk, H, E)[:, :L]
    lse = lse.transpose(0, 1, 3, 2).reshape(N, nb * blk, H)[:, :L]
    return o, lse


def _dilated_attention(q, k, v):
    B, S, H, E = q.shape
    outs, lses = [], []
    for window, dil in DILATED_PATTERNS:
        L = S // dil
        fold = lambda t: t.reshape(B, L, dil, H, E).transpose(0, 2, 1, 3, 4).reshape(B * dil, L, H, E)
        o, lse = _band_attention(fold(q), fold(k), fold(v), window // dil)
        outs.append(o.reshape(B, dil, L, H, E).transpose(0, 2, 1, 3, 4).reshape(B, S, H, E))
        lses.append(lse.reshape(B, dil, L, H).transpose(0, 2, 1, 3).reshape(B, S, H))
    wts = jax.nn.softmax(jnp.stack(lses), axis=0)
    return jnp.einsum('pbsh,pbshe->bshe', wts, jnp.stack(outs)).astype(q.dtype)


def _mlstm_chunkwise(q, k, v, log_i, log_f):
    B, S, H, E = q.shape
    nc = S // ML_CHUNK

    def chunks(t):
        return jnp.moveaxis(t.reshape((B, nc, ML_CHUNK) + t.shape[2:]), 1, 0)

    tri = jnp.tril(jnp.ones((ML_CHUNK, ML_CHUNK), dtype=bool))[None, :, :, None]

    def step(carry, inp):
        C, n, m = carry
        qc, kc, vc, ic, fc = inp
        b = jnp.cumsum(fc, axis=1)
        g = b[:, -1]
        D = jnp.where(tri, b[:, :, None, :] - b[:, None, :, :] + ic[:, None, :, :], -jnp.inf)
        inter = b + m[:, None, :]
        mt = jnp.maximum(jnp.max(D, axis=2), inter)
        s = jnp.einsum('bthe,bshe->btsh', qc, kc) * jnp.exp(D - mt[:, :, None, :])
        w_inter = jnp.exp(inter - mt)
        num = jnp.einsum('btsh,bshe->bthe', s, vc) + w_inter[..., None] * jnp.einsum('bthe,bhef->bthf', qc, C)
        den = jnp.sum(s, axis=2) + w_inter * jnp.einsum('bthe,bhe->bth', qc, n)
        h = num / jnp.maximum(jnp.abs(den), jnp.exp(-mt))[..., None]
        wk = g[:, None, :] - b + ic
        m_new = jnp.maximum(g + m, jnp.max(wk, axis=1))
        decay = jnp.exp(g + m - m_new)
        wk = jnp.exp(wk - m_new[:, None, :])
        C_new = decay[..., None, None] * C + jnp.einsum('bsh,bshe,bshf->bhef', wk, kc, vc)
        n_new = decay[..., None] * n + jnp.einsum('bsh,bshe->bhe', wk, kc)
        return (C_new, n_new, m_new), h

    init = (jnp.zeros((B, H, E, E), jnp.float32), jnp.zeros((B, H, E), jnp.float32), jnp.zeros((B, H), jnp.float32))
    _, hs = lax.scan(step, init, (chunks(q), chunks(k), chunks(v), chunks(log_i), chunks(log_f)))
    return jnp.moveaxis(hs, 0, 1).reshape(B, S, H, E)


def _mlstm(q, k, v, o_gate, ig, fg, b_i, b_f):
    B, S, _ = q.shape
    heads = lambda t: t.reshape(B, S, ML_HEADS, HEAD_DIM).astype(jnp.float32)
    log_i = ig.astype(jnp.float32) + b_i
    log_f = jax.nn.log_sigmoid(fg.astype(jnp.float32) + b_f)
    h = _mlstm_chunkwise(heads(q), heads(k) * (HEAD_DIM ** -0.5), heads(v), log_i, log_f)
    return (jax.nn.sigmoid(o_gate.astype(jnp.float32)) * h.reshape(B, S, GROUP_W)).astype(q.dtype)


def _token_mixing(x, w_in, mla_q_norm, mla_kv_norm, mla_w_uq, mla_w_ukv, lru_conv_w, lru_conv_b, lru_w_a,
                  lru_b_a, lru_w_x, lru_b_x, lru_lambda, ml_b_i, ml_b_f, out_norm, w_out):
    B, S, _ = x.shape
    (c_q, c_kv, k_rope, lru_x, lru_gate, sw_q, sw_k, sw_v,
     ml_q, ml_k, ml_v, ml_o, ml_i, ml_f) = _split_columns(x @ w_in)
    heads = lambda t: t.reshape(B, S, SWA_HEADS, HEAD_DIM)
    y_a = _mla(c_q, c_kv, k_rope, mla_q_norm, mla_kv_norm, mla_w_uq, mla_w_ukv)
    y_b = _rglru(lru_x, lru_gate, lru_conv_w, lru_conv_b, lru_w_a, lru_b_a, lru_w_x, lru_b_x, lru_lambda)
    y_c = _dilated_attention(heads(sw_q), heads(sw_k), heads(sw_v)).reshape(B, S, GROUP_W)
    y_d = _mlstm(ml_q, ml_k, ml_v, ml_o, ml_i, ml_f, ml_b_i, ml_b_f)
    y = _group_rmsnorm(jnp.concatenate([y_a, y_b, y_c, y_d], axis=-1), out_norm)
    return y @ w_out


def setup_inputs(seed: int = 0) -> dict:
    key = jax.random.key(seed)
    ks = jax.random.split(key, 24)
    f32 = jnp.float32
    nrm = lambda k, shape, scale: jax.random.normal(k, shape, f32) * scale
    x = nrm(ks[0], (BATCH, SEQ, D_MODEL), 1.0)
    ln_g = 1.0 + nrm(ks[1], (DEPTH, 3, D_MODEL), 0.02)
    ln_b = nrm(ks[2], (DEPTH, 3, D_MODEL), 0.02)
    ffn_w1 = nrm(ks[3], (DEPTH, 2, D_MODEL, D_FF), D_MODEL ** -0.5)
    ffn_w3 = nrm(ks[4], (DEPTH, 2, D_MODEL, D_FF), D_MODEL ** -0.5)
    ffn_w2 = nrm(ks[5], (DEPTH, 2, D_FF, D_MODEL), D_FF ** -0.5 * BETA)
    w_in = nrm(ks[6], (DEPTH, D_MODEL, N_IN), D_MODEL ** -0.5)
    mla_q_norm = 1.0 + nrm(ks[7], (DEPTH, Q_LORA), 0.02)
    mla_kv_norm = 1.0 + nrm(ks[8], (DEPTH, KV_LORA), 0.02)
    mla_w_uq = nrm(ks[9], (DEPTH, Q_LORA, MLA_HEADS * (MLA_NOPE + MLA_ROPE)), Q_LORA ** -0.5)
    mla_w_ukv = nrm(ks[10], (DEPTH, KV_LORA, MLA_HEADS * (MLA_NOPE + MLA_V)), KV_LORA ** -0.5)
    lru_conv_w = nrm(ks[11], (DEPTH, CONV_W, GROUP_W), CONV_W ** -0.5)
    lru_conv_b = nrm(ks[12], (DEPTH, GROUP_W), 0.02)
    lru_w_a = nrm(ks[13], (DEPTH, LRU_HEADS, LRU_BLOCK, LRU_BLOCK), LRU_BLOCK ** -0.5)
    lru_b_a = nrm(ks[14], (DEPTH, GROUP_W), 0.02)
    lru_w_x = nrm(ks[15], (DEPTH, LRU_HEADS, LRU_BLOCK, LRU_BLOCK), LRU_BLOCK ** -0.5)
    lru_b_x = nrm(ks[16], (DEPTH, GROUP_W), 0.02)
    a0 = jax.random.uniform(ks[17], (DEPTH, GROUP_W), f32, 0.9, 0.999)
    s0 = a0 ** (1.0 / LRU_C)
    lru_lambda = jnp.log(s0) - jnp.log1p(-s0)
    ml_b_i = nrm(ks[18], (DEPTH, ML_HEADS), 0.1)
    ml_b_f = jnp.linspace(3.0, 6.0, ML_HEADS, dtype=f32)[None, :] + nrm(ks[19], (DEPTH, ML_HEADS), 0.02)
    out_norm = 1.0 + nrm(ks[20], (DEPTH, D_MIX), 0.02)
    w_out = nrm(ks[21], (DEPTH, D_MIX, D_MODEL), D_MIX ** -0.5 * BETA)
    return {'x': x, 'ln_g': ln_g, 'ln_b': ln_b, 'ffn_w1': ffn_w1, 'ffn_w3': ffn_w3, 'ffn_w2': ffn_w2,
            'w_in': w_in, 'mla_q_norm': mla_q_norm, 'mla_kv_norm': mla_kv_norm, 'mla_w_uq': mla_w_uq,
            'mla_w_ukv': mla_w_ukv, 'lru_conv_w': lru_conv_w, 'lru_conv_b': lru_conv_b, 'lru_w_a': lru_w_a,
            'lru_b_a': lru_b_a, 'lru_w_x': lru_w_x, 'lru_b_x': lru_b_x, 'lru_lambda': lru_lambda,
            'ml_b_i': ml_b_i, 'ml_b_f': ml_b_f, 'out_norm': out_norm, 'w_out': w_out}


def reference(x, ln_g, ln_b, ffn_w1, ffn_w3, ffn_w2, w_in, mla_q_norm, mla_kv_norm, mla_w_uq, mla_w_ukv,
              lru_conv_w, lru_conv_b, lru_w_a, lru_b_a, lru_w_x, lru_b_x, lru_lambda, ml_b_i, ml_b_f,
              out_norm, w_out):
    for l in range(DEPTH):
        x = _layernorm(ALPHA * x + 0.5 * _swiglu(x, ffn_w1[l, 0], ffn_w3[l, 0], ffn_w2[l, 0]), ln_g[l, 0], ln_b[l, 0])
        y = _token_mixing(x, w_in[l], mla_q_norm[l], mla_kv_norm[l], mla_w_uq[l], mla_w_ukv[l],
                          lru_conv_w[l], lru_conv_b[l], lru_w_a[l], lru_b_a[l], lru_w_x[l], lru_b_x[l],
                          lru_lambda[l], ml_b_i[l], ml_b_f[l], out_norm[l], w_out[l])
        x = _layernorm(ALPHA * x + y, ln_g[l, 1], ln_b[l, 1])
        x = _layernorm(ALPHA * x + 0.5 * _swiglu(x, ffn_w1[l, 1], ffn_w3[l, 1], ffn_w2[l, 1]), ln_g[l, 2], ln_b[l, 2])
    return x
```

```python
import numpy as np
from contextlib import ExitStack
import concourse.bass as bass
import concourse.mybir as mybir
from concourse.bass_utils import run_bass_kernel_spmd

F32 = mybir.dt.float32
BF16 = mybir.dt.bfloat16
AF = mybir.ActivationFunctionType
ALU = mybir.AluOpType

SEQ = 2048
DM = 1024
DFF = 2816
DEPTH = 2
NQ = 4
TQ = 512
KC = 8
FC = 22
ALPHA = (2.0 * DEPTH) ** 0.25
N_IN = 2728
O_CQ, O_CKV, O_KR, O_LX, O_LG, O_SQ, O_SK, O_SV, O_MQ, O_MK, O_MV, O_MO, O_MI, O_MF = (
    0, 256, 384, 416, 672, 928, 1184, 1440, 1696, 1952, 2208, 2464, 2720, 2724)
NEG = -30000.0


class Buf:
    __slots__ = ("name", "excl", "writers", "readers", "sem", "cnt")

    def __init__(self, name, excl=False):
        self.name = name
        self.excl = excl
        self.writers = {}
        self.readers = {}
        self.sem = None
        self.cnt = 0


class DSem:
    __slots__ = ("sem", "cnt")

    def __init__(self, sem):
        self.sem = sem
        self.cnt = 0


class Sched:
    ENG = ("pe", "act", "dve", "pool")

    def __init__(self, nc, stack):
        self.nc = nc
        self.stack = stack
        self.q = {k: [] for k in ("pe", "act", "dve", "pool", "sp")}
        self.tick = {k: 0 for k in self.ENG}
        self.sem = {k: stack.enter_context(nc.semaphore("s_" + k)) for k in self.ENG}
        self.seen = {k: {} for k in self.q}
        self.dsems = []
        self.dfree = []

    def _semof(self, key):
        return self.sem[key] if isinstance(key, str) else key.sem

    def _dsem(self):
        if self.dfree:
            return self.dfree.pop()
        ds = DSem(self.stack.enter_context(self.nc.semaphore("d%d" % len(self.dsems))))
        self.dsems.append(ds)
        return ds

    def release(self, bufs):
        for b in bufs:
            if b.sem is not None:
                self.dfree.append(b.sem)
                b.sem = None
            b.writers = {}
            b.readers = {}

    def _collect(self, queue, reads, writes, ownkey=None):
        need = {}

        def add(k, t):
            if k is ownkey:
                return
            if queue == "pe" and k == "pe":
                return
            if need.get(k, 0) < t:
                need[k] = t
        for b in reads:
            for k, t in b.writers.items():
                add(k, t)
            if b.excl:
                for k, t in b.readers.items():
                    add(k, t)
        for b in writes:
            for k, t in b.writers.items():
                add(k, t)
            for k, t in b.readers.items():
                add(k, t)
        waits = []
        seen = self.seen[queue]
        for k, t in need.items():
            if seen.get(k, 0) >= t:
                continue
            seen[k] = t
            waits.append((self._semof(k), t))
        return waits

    def _update(self, key, tick, reads, writes):
        for b in reads:
            if b.excl:
                b.writers = {key: tick}
                b.readers = {}
            elif b.readers.get(key, 0) < tick:
                b.readers[key] = tick
        for b in writes:
            b.writers = {key: tick}
            b.readers = {}

    def op(self, queue, fn, reads=(), writes=(), signal=True):
        waits = self._collect(queue, reads, writes)
        tick = self.tick[queue] + 1
        if signal:
            self.tick[queue] = tick
        self._update(queue, tick, reads, writes)
        self.q[queue].append((waits, fn, self.sem[queue] if signal else None, 1))

    def dma(self, queue, out, in_, sembuf, reads=(), writes=()):
        if sembuf.sem is None:
            sembuf.sem = self._dsem()
        ds = sembuf.sem
        waits = self._collect(queue, reads, writes, ownkey=ds)
        ds.cnt += 1
        tick = 16 * ds.cnt
        self._update(ds, tick, reads, writes)
        self.q[queue].append((waits, lambda e: e.dma_start(out=out, in_=in_), ds.sem, 16))

    def barrier(self):
        for queue in self.q:
            seen = self.seen[queue]
            waits = []
            for k in self.ENG:
                t = self.tick[k]
                if t > 0 and seen.get(k, 0) < t:
                    seen[k] = t
                    waits.append((self.sem[k], t))
            for b in self.dsems:
                t = 16 * b.cnt
                if seen.get(b, 0) < t:
                    seen[b] = t
                    waits.append((b.sem, t))
            if waits:
                self.q[queue].append((waits, None, None, 0))

    def emit(self):
        qs = self.q

        def replay(lst, e):
            for waits, fn, sem, val in lst:
                for s, t in waits:
                    e.wait_ge(s, t)
                if fn is None:
                    continue
                ins = fn(e)
                if sem is not None:
                    ins.then_inc(sem, val)
        with self.nc.Block() as block:
            @block.tensor
            def _(e):
                replay(qs["pe"], e)

            @block.scalar
            def _(e):
                replay(qs["act"], e)

            @block.vector
            def _(e):
                replay(qs["dve"], e)

            @block.gpsimd
            def _(e):
                replay(qs["pool"], e)

            @block.sync
            def _(e):
                replay(qs["sp"], e)


class Tl:
    def __init__(self, ap, name, nb=1):
        self.ap = ap
        self.b = [Buf("%s%d" % (name, i)) for i in range(nb)]

    @property
    def b0(self):
        return self.b[0]


def _consts():
    c = {}
    c["ident"] = np.eye(128, dtype=np.float32)
    c["ones"] = np.ones((128, 128), np.float32)
    k = np.arange(128)[:, None]
    q = np.arange(128)[None, :]
    c["tri01"] = (q >= k).astype(np.float32)
    c["trineg"] = np.where(q >= k, 0.0, NEG).astype(np.float32)
    x = np.arange(SEQ)[None, :]
    d = x - k
    cnt = ((d >= 0) & (d <= 128)).astype(np.float32)
    cnt += ((d >= 0) & (d % 4 == 0) & (d <= 512)).astype(np.float32)
    cnt += ((d >= 0) & (d % 16 == 0) & (d <= 2048)).astype(np.float32)
    c["ctab"] = cnt.astype(np.float32)
    sel = np.zeros((128, 128), np.float32)
    sel[64, :] = 1.0
    c["sel"] = sel
    selh = np.zeros((128, 4, 128), np.float32)
    for h in range(4):
        selh[h, h, :] = 1.0
    c["selh"] = np.ascontiguousarray(selh.reshape(128, 512))
    half = 16
    freqs = (np.float32(10000.0) ** (-np.arange(half, dtype=np.float32) / np.float32(half))).astype(np.float32)
    ang = (np.arange(SEQ, dtype=np.float32)[None, :] * freqs[:, None]).astype(np.float32)
    cos, sin = np.cos(ang).astype(np.float32), np.sin(ang).astype(np.float32)
    rope = np.zeros((2, 32, SEQ), np.float32)
    rope[0, :16] = cos
    rope[0, 16:] = cos
    rope[1, :16] = -sin
    rope[1, 16:] = sin
    c["rope"] = rope
    return c


def _pv_layout():
    off = {}
    n = 0

    def add(name, cols):
        nonlocal n
        off[name] = n
        n += cols
    for l in range(DEPTH):
        for i in range(3):
            add(("lng", l, i), 8)
            add(("lnb", l, i), 8)
        add(("qn", l), 2)
        add(("kvn", l), 1)
        add(("cw", l), 8)
        add(("cb", l), 2)
        add(("ba", l), 2)
        add(("bx", l), 2)
        add(("lam", l), 2)
        add(("on", l), 8)
        add(("bi", l), 1)
        add(("bf", l), 1)
    return off, n


PV_OFF, PV_N = _pv_layout()


def _pvec(inp):
    pv = np.zeros((128, PV_N), np.float32)

    def put(name, arr):
        a = np.asarray(arr, np.float32).reshape(-1, 128).T
        pv[:, PV_OFF[name]:PV_OFF[name] + a.shape[1]] = a
    for l in range(DEPTH):
        for i in range(3):
            put(("lng", l, i), inp["ln_g"][l, i])
            put(("lnb", l, i), inp["ln_b"][l, i])
        put(("qn", l), inp["mla_q_norm"][l])
        put(("kvn", l), inp["mla_kv_norm"][l])
        put(("cw", l), inp["lru_conv_w"][l].reshape(-1))
        put(("cb", l), inp["lru_conv_b"][l])
        put(("ba", l), inp["lru_b_a"][l])
        put(("bx", l), inp["lru_b_x"][l])
        put(("lam", l), inp["lru_lambda"][l])
        put(("on", l), inp["out_norm"][l])
        pv[0:4, PV_OFF[("bi", l)]] = inp["ml_b_i"][l]
        pv[0:4, PV_OFF[("bf", l)]] = inp["ml_b_f"][l]
    return pv


class Builder:
    def __init__(self, debug=None):
        self.debug = debug or {}
        self.nc = bass.Bass("TRN2", target_bir_lowering=False)
        self.dbg_out = []

    def dram_in(self, name, shape):
        return self.nc.dram_tensor(name, list(shape), F32, kind="ExternalInput").ap()

    def build(self):
        nc = self.nc
        d = {}
        d["x"] = self.dram_in("x", [SEQ, DM])
        d["w1"] = self.dram_in("ffn_w1", [DEPTH, 2, DM, DFF])
        d["w3"] = self.dram_in("ffn_w3", [DEPTH, 2, DM, DFF])
        d["w2"] = self.dram_in("ffn_w2", [DEPTH, 2, DFF, DM])
        d["win"] = self.dram_in("w_in", [DEPTH, DM, N_IN])
        d["wuq"] = self.dram_in("mla_w_uq", [DEPTH, 256, 384])
        d["wukv"] = self.dram_in("mla_w_ukv", [DEPTH, 128, 512])
        d["wa"] = self.dram_in("lru_w_a", [DEPTH, 4, 64, 64])
        d["wx"] = self.dram_in("lru_w_x", [DEPTH, 4, 64, 64])
        d["wout"] = self.dram_in("w_out", [DEPTH, DM, DM])
        d["pvec"] = self.dram_in("pvec", [128, PV_N])
        for nm, shp in (("ident", [128, 128]), ("ones", [128, 128]), ("tri01", [128, 128]),
                        ("trineg", [128, 128]), ("ctab", [128, SEQ]), ("sel", [128, 128]),
                        ("selh", [128, 512]), ("rope", [2, 32, SEQ])):
            d[nm] = self.dram_in("c_" + nm, shp)
        d["out"] = nc.dram_tensor("out", [SEQ, DM], F32, kind="ExternalOutput").ap()
        d["yscr"] = nc.dram_tensor("yscr", [DM, SEQ], F32).ap()
        self.d = d
        with ExitStack() as st:
            self.st = st
            self.S = Sched(nc, st)
            self._alloc()
            self._program()
            self.S.barrier()
            self.S.emit()
        return nc

    def sbt(self, name, shape):
        return self.st.enter_context(self.nc.sbuf_tensor(name, list(shape), F32))

    def _alloc(self):
        nc, st = self.nc, self.st
        self.XT = self.sbt("XT", [128, KC, SEQ])
        self.XTB = [[Buf("xt%d_%d" % (i, c)) for c in range(KC)] for i in range(NQ)]
        self.YSB = [Buf("yscr%d" % i) for i in range(NQ)]
        self.cst = {}
        for nm, cols in (("ident", 128), ("ones", 128), ("tri01", 128), ("trineg", 128),
                         ("sel", 128), ("selh", 512), ("pvec", PV_N)):
            self.cst[nm] = Tl(self.sbt("k_" + nm, [128, cols])[:], nm)
        for nm, cols in (("identb", 128), ("trinegb", 128), ("onesb", 128), ("ctabb", SEQ)):
            self.cst[nm] = Tl(self.st.enter_context(self.nc.sbuf_tensor("k_" + nm, [128, cols], BF16))[:], nm)
        self.ps = [st.enter_context(nc.psum_tensor("ps%d" % i, [128, 512], F32)) for i in range(8)]
        self.psb = [Buf("ps%d" % i, excl=True) for i in range(8)]
        self._rr = 0
        self._ra = 0
        rem = nc.sbuf_bytes_remaining
        self.AN = (rem - 1024) // 4
        self.arena = self.sbt("arena", [128, self.AN])
        self.aoff = 0
        self.live = []
        self.ghosts = []
        self.leaked = []
        self.attn_filler = None

    def reset_arena(self, soft=False):
        self.rewind(0, soft=soft)

    def rewind(self, base, soft=False):
        keep, dead = [], []
        for ent in self.live:
            (dead if ent[0] >= base else keep).append(ent)
        if soft:
            for off, n, tl in dead:
                users = {}
                for b in tl.b:
                    for src in (b.writers, b.readers):
                        for k, t in src.items():
                            if users.get(k, 0) < t:
                                users[k] = t
                    self.leaked.append(b)
                if users:
                    self.ghosts.append((off, off + n, users))
        else:
            self.S.barrier()
            for off, n, tl in dead:
                self.S.release(tl.b)
            self.S.release(self.leaked)
            self.leaked = []
            self.ghosts = []
        self.live = keep
        self.aoff = base

    def _ghost_users(self, lo, hi):
        users = {}
        for g0, g1, u in self.ghosts:
            if g0 < hi and lo < g1:
                for k, t in u.items():
                    if users.get(k, 0) < t:
                        users[k] = t
        return users

    def track(self, tl, lo, hi):
        u = self._ghost_users(lo, hi)
        if u:
            for b in tl.b:
                b.readers = dict(u)
        self.live.append((lo, hi - lo, tl))
        return tl

    def carve(self, name, shape, nb=1, dt=F32):
        n = int(np.prod(shape[1:]))
        if dt == BF16:
            assert n % 2 == 0
            n //= 2
        assert self.aoff + n <= self.AN, (name, self.aoff, n, self.AN)
        ap = self.arena[0:shape[0], self.aoff:self.aoff + n]
        if dt == BF16:
            ap = ap.bitcast(BF16)
        off0 = self.aoff
        self.aoff += n
        if len(shape) == 3:
            ap = ap.rearrange("p (a b) -> p a b", a=shape[1])
        elif len(shape) == 4:
            ap = ap.rearrange("p (a b c) -> p a b c", a=shape[1], b=shape[2])
        tl = Tl(ap, name, nb)
        u = self._ghost_users(off0, off0 + n)
        if u:
            for b in tl.b:
                b.readers = dict(u)
        self.live.append((off0, n, tl))
        return tl

    def bank(self):
        i = self._rr
        self._rr = (self._rr + 1) % 6
        return i

    def accbank(self):
        i = 6 + self._ra
        self._ra ^= 1
        return i

    def mmg(self, bank, out, pairs, reads, start=True, last=True, extra_w=()):
        S = self.S
        n = len(pairs)
        w = [self.psb[bank]] + list(extra_w)
        for i, (lt, rh) in enumerate(pairs):
            st_ = start and i == 0
            sp_ = last and i == n - 1
            sig = (i == n - 1)
            rr = reads if (i == 0 or i == n - 1) else ()
            ww = w if (i == 0 or i == n - 1) else ()
            S.op("pe", (lambda e, o=out, a=lt, b=rh, s=st_, p=sp_: e.matmul(o, a, b, start=s, stop=p)),
                 reads=rr, writes=ww, signal=sig)

    def act(self, out, in_, func, reads, writes, bias=0.0, scale=1.0):
        self.S.op("act", lambda e: e.activation(out, in_, func, bias=bias, scale=scale), reads=reads, writes=writes)

    def tt(self, eng, out, a, b, op, reads, writes):
        self.S.op(eng, lambda e: e.tensor_tensor(out, a, b, op), reads=reads, writes=writes)

    def ts(self, eng, out, a, s1, s2, op0, op1, reads, writes):
        self.S.op(eng, lambda e: e.tensor_scalar(out, a, s1, s2, op0, op1), reads=reads, writes=writes)

    def stt(self, out, a, sc, b, op0, op1, reads, writes):
        self.S.op("dve", lambda e: e.scalar_tensor_tensor(out, a, sc, b, op0, op1), reads=reads, writes=writes)

    def cp(self, eng, out, in_, reads, writes):
        if eng == "act":
            self.S.op("act", lambda e: e.copy(out, in_), reads=reads, writes=writes)
        else:
            self.S.op(eng, lambda e: e.tensor_copy(out, in_), reads=reads, writes=writes)

    def load(self, tl, src, bi=0, queue=None, ap=None):
        if queue is None:
            queue = "sp"
        self.S.dma(queue, ap if ap is not None else tl.ap, src, tl.b[bi], writes=[tl.b[bi]])

    def pv(self, key, rows=128):
        o = PV_OFF[key]
        return lambda c=0, r0=0, r1=rows: self.cst["pvec"].ap[r0:r1, o + c:o + c + 1]

    def dump(self, name, tl_ap, bufs, shape):
        if name not in self.debug:
            return
        o = self.nc.dram_tensor("dbg_" + name, list(shape), F32, kind="ExternalOutput").ap()
        self.dbg_out.append("dbg_" + name)
        self.S.dma("sp", o, tl_ap, bufs[0], reads=bufs)

    def dump_xt(self, name):
        if name not in self.debug:
            return
        o = self.nc.dram_tensor("dbg_" + name, [DM, SEQ], F32, kind="ExternalOutput").ap()
        self.dbg_out.append("dbg_" + name)
        for q in range(NQ):
            self.S.dma("sp", o.rearrange("(c p) s -> p c s", p=128)[:, :, q * TQ:(q + 1) * TQ],
                       self.XT[:, :, q * TQ:(q + 1) * TQ], self.XTB[q][0], reads=self.XTB[q])

    def _program(self):
        d, S = self.d, self.S
        for nm, t in self.cst.items():
            if not nm.endswith("b"):
                self.load(t, d[nm])
        self.load(self.cst["ctabb"], d["ctab"], queue="pool")
        self.cp("dve", self.cst["identb"].ap, self.cst["ident"].ap, reads=[self.cst["ident"].b0], writes=[self.cst["identb"].b0])
        self.cp("dve", self.cst["trinegb"].ap, self.cst["trineg"].ap, reads=[self.cst["trineg"].b0], writes=[self.cst["trinegb"].b0])
        self.cp("dve", self.cst["onesb"].ap, self.cst["ones"].ap, reads=[self.cst["ones"].b0], writes=[self.cst["onesb"].b0])
        self.phase_load_x()
        self.dump_xt("x0")
        stop = self.debug.get("stop")
        for l in range(DEPTH):
            self.phase_ffn(l, 0)
            self.dump_xt("ffn%d0" % l)
            if stop == ("ffn", l, 0):
                break
            self.phase_mixer(l)
            self.dump_xt("mix%d" % l)
            if stop == ("mix", l):
                break
            self.phase_ffn(l, 1)
            self.dump_xt("ffn%d1" % l)
        self.phase_store()

    def phase_load_x(self):
        d, S = self.d, self.S
        self.reset_arena()
        xin = self.carve("xin", [128, 2, DM], nb=2)
        ident = self.cst["ident"]
        for tt in range(16):
            bi = tt % 2
            self.load(xin, d["x"][tt * 128:(tt + 1) * 128, :], bi=bi, ap=xin.ap[:, bi, :])
            q = tt // 4
            for half in range(2):
                bk = self.bank()
                for j in range(4):
                    dc = half * 4 + j
                    S.op("pe", (lambda e, o=self.ps[bk][:, j * 128:(j + 1) * 128], i=xin.ap[:, bi, dc * 128:(dc + 1) * 128]:
                                e.transpose(o, i, ident.ap)),
                         reads=[xin.b[bi], ident.b0], writes=[self.psb[bk]], signal=(j == 3))
                eng = "act" if half == 0 else "dve"
                self.cp(eng, self.XT[:, half * 4:(half + 1) * 4, tt * 128:(tt + 1) * 128],
                        self.ps[bk][:, :].rearrange("p (a b) -> p a b", a=4),
                        reads=[self.psb[bk]], writes=self.XTB[q][half * 4:(half + 1) * 4])

    def phase_store(self):
        d, S = self.d, self.S
        self.reset_arena()
        xo = self.carve("xout", [128, 2, DM], nb=2)
        ident = self.cst["ident"]
        for tt in range(16):
            bi = tt % 2
            q = tt // 4
            for half in range(2):
                bk = self.bank()
                for j in range(4):
                    dc = half * 4 + j
                    S.op("pe", (lambda e, o=self.ps[bk][:, j * 128:(j + 1) * 128], i=self.XT[:, dc, tt * 128:(tt + 1) * 128]:
                                e.transpose(o, i, ident.ap)),
                         reads=[self.XTB[q][dc], ident.b0], writes=[self.psb[bk]], signal=(j == 3))
                eng = "act" if half == 0 else "dve"
                self.cp(eng, xo.ap[:, bi, half * 512:(half + 1) * 512], self.ps[bk][:, :],
                        reads=[self.psb[bk]], writes=[xo.b[bi]])
            S.dma("sp", d["out"][tt * 128:(tt + 1) * 128, :], xo.ap[:, bi, :], xo.b[bi], reads=[xo.b[bi]])

    def layernorm(self, q, gkey, bkey, eps, tmp):
        S = self.S
        ones = self.cst["ones"]
        sq, mean, var, rstd = tmp
        qs = slice(q * TQ, (q + 1) * TQ)
        XB = self.XTB[q]
        bs, bq = self.bank(), self.bank()
        onesb = self.cst["onesb"]
        for dc in range(KC):
            i = dc % 2
            self.act(sq.ap[:, i, 0:TQ], self.XT[:, dc, qs], AF.Square, reads=[XB[dc]], writes=[sq.b[i]])
            self.act(sq.ap[:, i, TQ:2 * TQ], self.XT[:, dc, qs], AF.Identity, reads=[XB[dc]], writes=[sq.b[i]])
            S.op("pe", (lambda e, o=self.ps[bq][:, :], r=sq.ap[:, i, 0:TQ], s=(dc == 0), p=(dc == KC - 1):
                        e.matmul(o, onesb.ap, r, start=s, stop=p)),
                 reads=[sq.b[i], onesb.b0], writes=[self.psb[bq]], signal=(dc == KC - 1))
            S.op("pe", (lambda e, o=self.ps[bs][:, :], r=sq.ap[:, i, TQ:2 * TQ], s=(dc == 0), p=(dc == KC - 1):
                        e.matmul(o, onesb.ap, r, start=s, stop=p)),
                 reads=[sq.b[i]], writes=[self.psb[bs]], signal=True)
        S.op("act", lambda e: e.mul(mean.ap, self.ps[bs][:, :], 1.0 / DM), reads=[self.psb[bs]], writes=[mean.b0])
        self.tt("dve", var.ap, mean.ap, mean.ap, ALU.mult, reads=[mean.b0], writes=[var.b0])
        self.stt(var.ap, self.ps[bq][:, :], 1.0 / DM, var.ap, ALU.mult, ALU.subtract,
                 reads=[self.psb[bq], var.b0], writes=[var.b0])
        self.act(rstd.ap, var.ap, AF.Ln, reads=[var.b0], writes=[rstd.b0], bias=eps, scale=1.0)
        self.act(rstd.ap, rstd.ap, AF.Exp, reads=[rstd.b0], writes=[rstd.b0], scale=-0.5)
        g, b = self.pv(gkey), self.pv(bkey)
        pvb = self.cst["pvec"].b0
        for dc in range(KC):
            x = self.XT[:, dc, qs]
            self.tt("dve", x, x, mean.ap, ALU.subtract, reads=[XB[dc], mean.b0], writes=[XB[dc]])
            self.tt("dve", x, x, rstd.ap, ALU.mult, reads=[XB[dc], rstd.b0], writes=[XB[dc]])
            self.act(x, x, AF.Identity, reads=[XB[dc], pvb], writes=[XB[dc]], bias=b(dc), scale=g(dc))

    def phase_ffn(self, l, i):
        d, S = self.d, self.S
        self.reset_arena(soft=(i == 1))
        NB13, NB2 = 4, 3
        HT = self.carve("HT", [128, FC, TQ], dt=BF16)
        w13t = [self.carve("w13_%d" % k, [128, 2, KC, 128], dt=BF16) for k in range(NB13)]
        w2t = [self.carve("w2_%d" % k, [128, FC, 128], dt=BF16) for k in range(NB2)]
        XBt = [self.carve("xb%d" % k, [128, KC, TQ], dt=BF16) for k in range(2)]
        sil = self.carve("sil", [128, 2, TQ], nb=2)
        sq = self.carve("sq", [128, 2, 2 * TQ], nb=2, dt=BF16)
        mean = self.carve("mean", [128, TQ])
        var = self.carve("var", [128, TQ])
        rstd = self.carve("rstd", [128, TQ])
        w1v = d["w1"][l, i].rearrange("(kc p) f -> p kc f", p=128)
        w3v = d["w3"][l, i].rearrange("(kc p) f -> p kc f", p=128)
        w2v = d["w2"][l, i].rearrange("(fc p) m -> p fc m", p=128)
        jobs13 = [(q, fc) for q in range(NQ) for fc in range(FC)]
        jobs2 = [(q, dc) for q in range(NQ) for dc in range(KC)]
        st13 = [0]
        st2 = [0]

        def issue13(upto):
            while st13[0] <= upto and st13[0] < len(jobs13):
                n = st13[0]
                st13[0] += 1
                q, fc = jobs13[n]
                t = w13t[n % NB13]
                S.dma("pool", t.ap[:, 0, :, :], w1v[:, :, fc * 128:(fc + 1) * 128], t.b0, writes=[t.b0])
                S.dma("pool", t.ap[:, 1, :, :], w3v[:, :, fc * 128:(fc + 1) * 128], t.b0, writes=[t.b0])

        def issue2(upto):
            while st2[0] <= upto and st2[0] < len(jobs2):
                n = st2[0]
                st2[0] += 1
                q, dc = jobs2[n]
                t = w2t[n % NB2]
                S.dma("pool", t.ap, w2v[:, :, dc * 128:(dc + 1) * 128], t.b0, writes=[t.b0])

        def mkxb(q):
            if q < NQ:
                xb = XBt[q % 2]
                self.cp("dve", xb.ap, self.XT[:, :, q * TQ:(q + 1) * TQ], reads=self.XTB[q], writes=[xb.b0])
        stream_lru = (i == 0) and not self.debug.get("no_lru_stream")
        if stream_lru:
            self.lru_setup(l)
        fill = []

        def filler(k):
            for _ in range(k):
                while fill:
                    try:
                        next(fill[0])
                        break
                    except StopIteration:
                        fill.pop(0)
                if not fill:
                    return
        issue13(NB13 - 2)
        issue2(0)
        mkxb(0)
        n13 = 0
        n2 = 0
        for q in range(NQ):
            qs = slice(q * TQ, (q + 1) * TQ)
            xb = XBt[q % 2]
            for fc in range(FC):
                issue13(n13 + NB13 - 1)
                t = w13t[n13 % NB13]
                n13 += 1
                ba, bb = self.bank(), self.bank()
                self.mmg(ba, self.ps[ba][:, :], [(t.ap[:, 0, kc, :], xb.ap[:, kc, :]) for kc in range(KC)],
                         reads=[t.b0, xb.b0])
                self.mmg(bb, self.ps[bb][:, :], [(t.ap[:, 1, kc, :], xb.ap[:, kc, :]) for kc in range(KC)],
                         reads=[t.b0, xb.b0])
                si = fc % 2
                self.act(sil.ap[:, si, :], self.ps[ba][:, :], AF.Silu, reads=[self.psb[ba]], writes=[sil.b[si]])
                self.tt("dve", HT.ap[:, fc, :], sil.ap[:, si, :], self.ps[bb][:, :], ALU.mult,
                        reads=[sil.b[si], self.psb[bb]], writes=[HT.b0])
                if fc == FC - 4:
                    issue2(n2 + NB2 - 1)
                filler(2)
            mkxb(q + 1)
            for dc in range(KC):
                issue2(n2 + NB2 - 1)
                t = w2t[n2 % NB2]
                n2 += 1
                bk = self.bank()
                self.mmg(bk, self.ps[bk][:, :], [(t.ap[:, fc, :], HT.ap[:, fc, :]) for fc in range(FC)],
                         reads=[t.b0, HT.b0])
                self.stt(self.XT[:, dc, qs], self.XT[:, dc, qs], 2.0 * ALPHA, self.ps[bk][:, :], ALU.mult, ALU.add,
                         reads=[self.XTB[q][dc], self.psb[bk]], writes=[self.XTB[q][dc]])
                if dc < KC - 2:
                    filler(5)
            self.layernorm(q, ("lng", l, 2 * i), ("lnb", l, 2 * i), 4.0e-5, (sq, mean, var, rstd))
            if stream_lru:
                fill.append(self.lru_tile(l, q))
        filler(100000)

    def proj_fm(self, wt, ncols, out_fn, evac="act", c0=0):
        XBF = self.XBF
        for q in range(NQ):
            qs = slice(q * TQ, (q + 1) * TQ)
            bk = self.bank()
            self.mmg(bk, self.ps[bk][0:ncols, :], [(wt.ap[:, kc, c0:c0 + ncols], XBF.ap[:, kc, qs]) for kc in range(KC)],
                     reads=[wt.b0, XBF.b[q]])
            oap, obufs = out_fn(q)
            self.cp(evac, oap, self.ps[bk][0:ncols, :], reads=[self.psb[bk]], writes=obufs)

    def rmsnorm_fm(self, X, Y, nch, rows_key, eps, nfeat, tmp):
        S = self.S
        ones = self.cst["onesb"]
        sq, rstd0 = tmp
        rot = [rstd0, self.ND, self.RD]
        g = self.pv(rows_key)
        pvb = self.cst["pvec"].b0
        for q in range(NQ):
            qs = slice(q * TQ, (q + 1) * TQ)
            rstd = rot[q % 3]
            bk = self.bank()
            for c in range(nch):
                i = c % 2
                src = X.ap[:, c, qs] if nch > 1 else X.ap[:, qs]
                self.act(sq.ap[:, i, :], src, AF.Square, reads=[X.b[q]], writes=[sq.b[i]])
                S.op("pe", (lambda e, o=self.ps[bk][:, :], r=sq.ap[:, i, :], s=(c == 0), p=(c == nch - 1):
                            e.matmul(o, ones.ap, r, start=s, stop=p)),
                     reads=[sq.b[i], ones.b0], writes=[self.psb[bk]], signal=True)
            self.act(rstd.ap, self.ps[bk][:, :], AF.Ln, reads=[self.psb[bk]], writes=[rstd.b0], bias=eps, scale=1.0 / nfeat)
            self.act(rstd.ap, rstd.ap, AF.Exp, reads=[rstd.b0], writes=[rstd.b0], scale=-0.5)
            for c in range(nch):
                src = X.ap[:, c, qs] if nch > 1 else X.ap[:, qs]
                dst = Y.ap[:, c, qs] if nch > 1 else Y.ap[:, qs]
                self.stt(dst, src, g(c), rstd.ap, ALU.mult, ALU.mult, reads=[X.b[q], rstd.b0, pvb], writes=[Y.b[q]])

    def attn_head(self, kfn, qfn, kdim, vfn, mode, scale, fin_fn, aux=None):
        S = self.S
        ident, trineg, tri01, ctab = (self.cst[k] for k in ("identb", "trinegb", "tri01", "ctabb"))
        PT = self.PT
        NPT = len(PT.b)
        LOOK = NPT - 2
        blocks = [(q, j) for q in range(NQ) for j in range(4 * q + 4)]
        accs = {}
        pts = {}
        pending = []

        def stage_a(n):
            q, j = blocks[n]
            if j == 0:
                accs[q] = self.accbank()
            r = j - 4 * q
            c0 = 128 * max(r, 0)
            lo, hi = q * TQ + c0, (q + 1) * TQ
            bk = self.bank()
            kap, kb = kfn(j)
            qap, qb = qfn(lo, hi)
            pso = self.ps[bk][:, c0:TQ]
            diag = (r >= 0)
            pi = self._pti
            self._pti = (self._pti + 1) % NPT
            pt = PT.ap[:, pi, c0:TQ]
            ptb = PT.b[pi]
            pts[n] = (pt, ptb, c0)
            if mode == "mla":
                self.mmg(bk, pso, [(kap, qap)], reads=kb + qb, last=not diag)
                if diag:
                    S.op("pe", (lambda e, o=self.ps[bk][:, c0:c0 + 128]: e.matmul(o, ident.ap, trineg.ap, start=False, stop=True)),
                         reads=[ident.b0, trineg.b0], writes=[self.psb[bk]], signal=True)
                self.act(pt, pso, AF.Exp, reads=[self.psb[bk]], writes=[ptb], scale=scale)
            elif mode == "dil":
                self.mmg(bk, pso, [(kap, qap)], reads=kb + qb)
                self.act(pt, pso, AF.Exp, reads=[self.psb[bk]], writes=[ptb], scale=scale)
                off = q * TQ - 128 * j
                self.tt("dve", pt, pt, ctab.ap[:, off + c0:off + TQ], ALU.mult, reads=[ptb, ctab.b0], writes=[ptb])
            else:
                csrow, biasfn = aux
                self.mmg(bk, pso, [(kap, qap)], reads=kb + qb)
                ei = self._ei
                self._ei = (self._ei + 1) % len(self.ET.b)
                et = self.ET.ap[:, ei, c0:TQ]
                etb = self.ET.b[ei]
                bap, bb = biasfn(j)
                self.act(et, csrow.ap[:, lo:hi], AF.Exp, reads=[csrow.b[q]] + bb, writes=[etb], bias=bap, scale=-1.0)
                if diag:
                    self.tt("pool", self.ET.ap[:, ei, c0:c0 + 128], self.ET.ap[:, ei, c0:c0 + 128], tri01.ap, ALU.mult,
                            reads=[etb, tri01.b0], writes=[etb])
                self.stt(pt, pso, scale, et, ALU.mult, ALU.mult, reads=[self.psb[bk], etb], writes=[ptb])

        def stage_b(n):
            q, j = blocks[n]
            nj = 4 * q + 4
            acc = accs[q]
            pt, ptb, c0 = pts.pop(n)
            vap, vb = vfn(j)
            S.op("pe", (lambda e, o=self.ps[acc][:, c0:TQ], a=vap, b=pt, s=(j == 0), p=(j == nj - 1):
                        e.matmul(o, a, b, start=s, stop=p)),
                 reads=vb + [ptb], writes=[self.psb[acc]], signal=True)
            if j == nj - 1:
                pending.append((n + 2, q, acc))
            while pending and pending[0][0] <= n:
                _, q_, acc_ = pending.pop(0)
                fin_fn(q_, acc_)
        nb = len(blocks)
        for n in range(min(LOOK, nb)):
            stage_a(n)
        for n in range(nb):
            if n + LOOK < nb:
                stage_a(n + LOOK)
            stage_b(n)
            if self.attn_filler is not None:
                self.attn_filler()
        for _, q_, acc_ in pending:
            fin_fn(q_, acc_)

    def finish_softmax(self, q, acc, out_ap, out_bufs, mlstm=False, gate=None, cpeng="act"):
        T, RD = self.ND, self.RD
        pa = self.psb[acc]
        if not mlstm:
            self.act(T.ap[0:64, :], self.ps[acc][64:128, :], AF.Ln, reads=[pa], writes=[T.b0])
            self.act(RD.ap[0:64, :], T.ap[0:64, :], AF.Exp, reads=[T.b0], writes=[RD.b0], scale=-1.0)
        else:
            self.act(T.ap[0:64, :], self.ps[acc][64:128, :], AF.Identity, reads=[pa], writes=[T.b0])
            self.tt("dve", T.ap[0:64, :], T.ap[0:64, :], T.ap[0:64, :], ALU.mult, reads=[T.b0], writes=[T.b0])
            self.ts("dve", T.ap[0:64, :], T.ap[0:64, :], 1.0, 0.0, ALU.max, ALU.add, reads=[T.b0], writes=[T.b0])
            self.act(T.ap[0:64, :], T.ap[0:64, :], AF.Ln, reads=[T.b0], writes=[T.b0])
            self.act(RD.ap[0:64, :], T.ap[0:64, :], AF.Exp, reads=[T.b0], writes=[RD.b0], scale=-0.5)
        if gate is not None:
            gap, gb = gate
            self.tt("pool", RD.ap[0:64, :], RD.ap[0:64, :], gap, ALU.mult, reads=[RD.b0] + gb, writes=[RD.b0])
        self.tt("dve", out_ap, self.ps[acc][0:64, :], RD.ap[0:64, :], ALU.mult, reads=[pa, RD.b0], writes=out_bufs)

    def group_finish(self, l, g, YG):
        S = self.S
        ones = self.cst["onesb"]
        sq = self.gsq
        rot = [self.grstd, self.ND, self.RD]
        on = self.pv(("on", l))
        pvb = self.cst["pvec"].b0
        for q in range(NQ):
            qs = slice(q * TQ, (q + 1) * TQ)
            rstd = rot[q % 3]
            bk = self.bank()
            for c in range(2):
                self.act(sq.ap[:, c, :], YG.ap[:, c, qs], AF.Square, reads=[YG.b[q]], writes=[sq.b[c]])
                S.op("pe", (lambda e, o=self.ps[bk][:, :], r=sq.ap[:, c, :], s=(c == 0), p=(c == 1):
                            e.matmul(o, ones.ap, r, start=s, stop=p)),
                     reads=[sq.b[c], ones.b0], writes=[self.psb[bk]], signal=True)
            self.act(rstd.ap, self.ps[bk][:, :], AF.Ln, reads=[self.psb[bk]], writes=[rstd.b0], bias=1e-6, scale=1.0 / 256)
            self.act(rstd.ap, rstd.ap, AF.Exp, reads=[rstd.b0], writes=[rstd.b0], scale=-0.5)
            for c in range(2):
                self.stt(YG.ap[:, c, qs], YG.ap[:, c, qs], on(g * 2 + c), rstd.ap, ALU.mult, ALU.mult,
                         reads=[YG.b[q], rstd.b0, pvb], writes=[YG.b[q]])
            S.dma("sp", self.d["yscr"].rearrange("(c p) s -> p c s", p=128)[:, g * 2:g * 2 + 2, qs],
                  YG.ap[:, :, qs], YG.b[q], reads=[YG.b[q]], writes=[self.YSB[q]])
        if ("yg%d%d" % (l, g)) in self.debug:
            self.dump("yg%d%d" % (l, g), YG.ap, YG.b, [128, 2, SEQ])

    def phase_mixer(self, l):
        self.reset_arena(soft=True)
        self.XBF = self.carve("XBF", [128, KC, SEQ], nb=NQ, dt=BF16)
        for q in range(NQ):
            self.cp("dve" if q % 2 else "act", self.XBF.ap[:, :, q * TQ:(q + 1) * TQ], self.XT[:, :, q * TQ:(q + 1) * TQ],
                    reads=self.XTB[q], writes=[self.XBF.b[q]])
        self.PT = self.carve("PT", [128, 5, TQ], nb=5, dt=BF16)
        self._pti = 0
        self.ET = self.carve("ET", [128, 4, TQ], nb=4)
        self._ei = 0
        self.ND = self.carve("ND", [128, TQ])
        self.RD = self.carve("RD", [128, TQ])
        self.gsq = self.carve("gsq", [128, 2, TQ], nb=2, dt=BF16)
        self.grstd = self.carve("grstd", [128, TQ])
        base = self.aoff
        for g, fn in enumerate((self.mixer_mla, self.mixer_lru, self.mixer_dil, self.mixer_mlstm)):
            if "only_g" in self.debug and g not in self.debug["only_g"]:
                continue
            if g == 1 and not self.debug.get("no_lru_stream"):
                continue
            self.rewind(base, soft=True)
            fn(l)
        self.phase_outproj(l)

    def win(self, l, c0, n):
        return self.d["win"][l].rearrange("(kc p) f -> p kc f", p=128)[:, :, c0:c0 + n]

    def mixer_mla(self, l):
        d, S = self.d, self.S
        CQ = self.carve("CQ", [128, 2, SEQ], nb=NQ, dt=BF16)
        CKV = self.carve("CKV", [128, SEQ], nb=NQ, dt=BF16)
        KRR = self.carve("KRR", [128, SEQ], nb=NQ, dt=BF16)
        RY = self.carve("RY", [128, 2, SEQ])
        ry0 = self.aoff - 2 * SEQ
        ROPE = self.track(Tl(RY.ap, "rope"), ry0, self.aoff)
        YH = [self.track(Tl(RY.ap[:, k, :], "YHa%d" % k, NQ), ry0, self.aoff) for k in range(2)]
        tmp = self.carve("tmpr", [128, TQ])
        tmp2 = self.carve("tmpr2", [128, TQ])
        base2 = self.aoff
        Wa = self.carve("Wa", [128, KC, 256], dt=BF16)
        Wb = self.carve("Wb", [128, KC, 160], dt=BF16)
        Wsw = self.carve("Wsw", [128, KC, 96], dt=BF16)
        CQr = self.carve("CQr", [128, 2, SEQ], nb=NQ)
        CKVr = self.carve("CKVr", [128, SEQ], nb=NQ)
        self.load(ROPE, d["rope"].rearrange("t r s -> r t s"), ap=ROPE.ap[64:96, :, :])
        self.load(Wa, self.win(l, O_CQ, 256), queue="pool")
        self.load(Wb, self.win(l, O_CKV, 160), queue="pool")
        self.load(Wsw, self.win(l, 320, 64), ap=Wsw.ap[:, :, 0:64], queue="pool")
        self.load(Wsw, self.win(l, O_KR + 16, 16), ap=Wsw.ap[:, :, 64:80], queue="pool")
        self.load(Wsw, self.win(l, O_KR, 16), ap=Wsw.ap[:, :, 80:96], queue="pool")
        for m in range(2):
            self.proj_fm(Wa, 128, lambda q, m=m: (CQr.ap[:, m, q * TQ:(q + 1) * TQ], [CQr.b[q]]), c0=m * 128,
                         evac="act" if m == 0 else "dve")
        self.proj_fm(Wb, 128, lambda q: (CKVr.ap[:, q * TQ:(q + 1) * TQ], [CKVr.b[q]]), evac="dve")
        self.rmsnorm_fm(CQr, CQ, 2, ("qn", l), 1e-6, 256, (self.gsq, self.grstd))
        self.rmsnorm_fm(CKVr, CKV, 1, ("kvn", l), 1e-6, 128, (self.gsq, self.grstd))
        self.dump("cq%d" % l, CQ.ap, CQ.b, [128, 2, SEQ])
        self.dump("ckv%d" % l, CKV.ap, CKV.b, [128, SEQ])
        for q in range(NQ):
            qs = slice(q * TQ, (q + 1) * TQ)
            ba, bb = self.bank(), self.bank()
            self.mmg(ba, self.ps[ba][0:96, :], [(Wb.ap[:, kc, 64:160], self.XBF.ap[:, kc, qs]) for kc in range(KC)],
                     reads=[Wb.b0, self.XBF.b[q]])
            self.mmg(bb, self.ps[bb][0:96, :], [(Wsw.ap[:, kc, :], self.XBF.ap[:, kc, qs]) for kc in range(KC)],
                     reads=[Wsw.b0, self.XBF.b[q]])
            self.tt("dve", tmp2.ap[64:96, :], self.ps[ba][64:96, :], ROPE.ap[64:96, 0, qs], ALU.mult,
                    reads=[self.psb[ba], ROPE.b0], writes=[tmp2.b0])
            self.tt("dve", tmp.ap[64:96, :], self.ps[bb][64:96, :], ROPE.ap[64:96, 1, qs], ALU.mult,
                    reads=[self.psb[bb], ROPE.b0], writes=[tmp.b0])
            self.tt("pool", KRR.ap[64:96, qs], tmp2.ap[64:96, :], tmp.ap[64:96, :], ALU.add,
                    reads=[tmp2.b0, tmp.b0], writes=[KRR.b[q]])
        self.dump("krr%d" % l, KRR.ap, KRR.b, [128, SEQ])
        self.rewind(base2, soft=True)
        YG = self.carve("YGa", [128, 2, SEQ], nb=NQ)
        QH = [self.carve("QH%d" % k, [128, SEQ], nb=NQ, dt=BF16) for k in range(2)]
        KH = [self.carve("KH%d" % k, [128, SEQ], nb=NQ, dt=BF16) for k in range(2)]
        VH = [self.carve("VH%d" % k, [128, 16, 128], dt=BF16) for k in range(2)]
        wuq = [self.carve("wuq%d" % k, [128, 2, 96], dt=BF16) for k in range(2)]
        wuqs = [self.carve("wuqs%d" % k, [128, 2, 96], dt=BF16) for k in range(2)]
        wukv = [self.carve("wukv%d" % k, [128, 128], dt=BF16) for k in range(2)]
        wuqv = d["wuq"][l].rearrange("(kc p) f -> p kc f", p=128)
        def head_proj(h):
            k = h % 2
            self.load(wuq[k], wuqv[:, :, h * 96:(h + 1) * 96], queue="pool")
            self.load(wuqs[k], wuqv[:, :, h * 96:h * 96 + 64], ap=wuqs[k].ap[:, :, 0:64], queue="pool")
            self.load(wuqs[k], wuqv[:, :, h * 96 + 80:h * 96 + 96], ap=wuqs[k].ap[:, :, 64:80], queue="pool")
            self.load(wuqs[k], wuqv[:, :, h * 96 + 64:h * 96 + 80], ap=wuqs[k].ap[:, :, 80:96], queue="pool")
            self.load(wukv[k], d["wukv"][l][:, h * 128:(h + 1) * 128], queue="pool")
            qh, kh, vh, yh = QH[k], KH[k], VH[k], YH[k]
            S.op("pool", lambda e, v=vh: e.memset(v.ap[:, :, 64:128], 1.0), writes=[vh.b0])
            for q in range(NQ):
                qs = slice(q * TQ, (q + 1) * TQ)
                ba, bb, bc = self.bank(), self.bank(), self.bank()
                self.mmg(ba, self.ps[ba][0:96, :], [(wuq[k].ap[:, c, :], CQ.ap[:, c, qs]) for c in range(2)],
                         reads=[wuq[k].b0, CQ.b[q]])
                self.mmg(bb, self.ps[bb][0:96, :], [(wuqs[k].ap[:, c, :], CQ.ap[:, c, qs]) for c in range(2)],
                         reads=[wuqs[k].b0, CQ.b[q]])
                self.cp("dve", qh.ap[0:64, qs], self.ps[ba][0:64, :], reads=[self.psb[ba]], writes=[qh.b[q]])
                self.tt("dve", tmp2.ap[64:96, :], self.ps[ba][64:96, :], ROPE.ap[64:96, 0, qs], ALU.mult,
                        reads=[self.psb[ba], ROPE.b0], writes=[tmp2.b0])
                self.tt("dve", tmp.ap[64:96, :], self.ps[bb][64:96, :], ROPE.ap[64:96, 1, qs], ALU.mult,
                        reads=[self.psb[bb], ROPE.b0], writes=[tmp.b0])
                self.tt("pool", qh.ap[64:96, qs], tmp2.ap[64:96, :], tmp.ap[64:96, :], ALU.add,
                        reads=[tmp2.b0, tmp.b0], writes=[qh.b[q]])
                yield
                self.mmg(bc, self.ps[bc][:, :], [(wukv[k].ap[:, :], CKV.ap[:, qs])], reads=[wukv[k].b0, CKV.b[q]])
                self.cp("dve", kh.ap[0:64, qs], self.ps[bc][0:64, :], reads=[self.psb[bc]], writes=[kh.b[q]])
                self.cp("pool", kh.ap[64:96, qs], KRR.ap[64:96, qs], reads=[KRR.b[q]], writes=[kh.b[q]])
                yield
            for half in range(2):
                bk = self.bank()
                for t8 in range(8):
                    tc = half * 8 + t8
                    S.op("pe", (lambda e, o=self.ps[bk][:, t8 * 64:(t8 + 1) * 64], a=CKV.ap[:, tc * 128:(tc + 1) * 128], b=wukv[k].ap[:, 64:128]:
                                e.matmul(o, a, b, start=True, stop=True)),
                         reads=[CKV.b[tc // 4], wukv[k].b0], writes=[self.psb[bk]], signal=(t8 == 7))
                self.cp("dve", vh.ap[:, half * 8:(half + 1) * 8, 0:64], self.ps[bk][:, :].rearrange("p (a b) -> p a b", a=8),
                        reads=[self.psb[bk]], writes=[vh.b0])
                yield
            if h == 0:
                self.dump("qh%d" % l, qh.ap, qh.b, [128, SEQ])
                self.dump("kh%d" % l, kh.ap, kh.b, [128, SEQ])
                pass


        def head_attn(h):
            k = h % 2
            qh, kh, vh, yh = QH[k], KH[k], VH[k], YH[k]
            def fin(q, acc, h=h, yh=yh):
                qs = slice(q * TQ, (q + 1) * TQ)
                if h % 2 == 0:
                    self.finish_softmax(q, acc, YG.ap[0:64, h // 2, qs], [YG.b[q]], cpeng="dve")
                else:
                    self.finish_softmax(q, acc, yh.ap[0:64, qs], [yh.b[q]], cpeng="dve")
                    S.dma("sp", YG.ap[64:128, h // 2, qs], yh.ap[0:64, qs], yh.b[q], reads=[yh.b[q]], writes=[YG.b[q]])
            self.attn_head(lambda j, kh=kh: (kh.ap[0:96, j * 128:(j + 1) * 128], [kh.b[j // 4]]),
                           lambda lo, hi, qh=qh: (qh.ap[0:96, lo:hi], [qh.b[lo // TQ]]), 96,
                           lambda j, vh=vh: (vh.ap[:, j, :], [vh.b0]), "mla", 96.0 ** -0.5, fin)

        for _ in head_proj(0):
            pass
        for h in range(4):
            gen = head_proj(h + 1) if h + 1 < 4 else iter(())
            self.attn_filler = lambda gen=gen: next(gen, None)
            head_attn(h)
            self.attn_filler = None
            for _ in gen:
                pass
        self.group_finish(l, 0, YG)

    def lru_setup(self, l):
        d, S = self.d, self.S
        L = {}
        L["XL"] = self.carve("XL", [128, KC, TQ], dt=BF16)
        W = L["W"] = self.carve("Wl", [128, KC, 512], dt=BF16)
        L["wbd"] = [[self.carve("wbd%d%d" % (c, k), [128, 128]) for k in range(2)] for c in range(2)]
        sp = L["sp"] = self.carve("spl", [128, 8])
        L["XB"] = [self.carve("XBl%d" % c, [128, TQ + 4]) for c in range(2)]
        L["H"] = self.carve("Hl", [128, 2])
        L["hb"] = self.carve("hbl", [128, 4])
        for nm in ("GT", "XC", "A", "I", "M", "Qp", "Tg"):
            L[nm] = self.carve(nm + "l", [128, TQ])
        L["YGt"] = self.carve("YGt", [128, 2, TQ])
        L["sq"] = self.carve("sql", [128, 2, TQ], nb=2, dt=BF16)
        L["rstd"] = self.carve("rstdl", [128, TQ])
        pvb = self.cst["pvec"].b0
        lam = self.pv(("lam", l))
        for c in range(2):
            self.load(W, self.win(l, O_LX + c * 128, 128), ap=W.ap[:, :, c * 128:(c + 1) * 128], queue="pool")
            self.load(W, self.win(l, O_LG + c * 128, 128), ap=W.ap[:, :, 256 + c * 128:256 + (c + 1) * 128], queue="pool")
            for k, wsrc in enumerate((d["wa"], d["wx"])):
                t = L["wbd"][c][k]
                S.op("dve", lambda e, t=t: e.memset(t.ap, 0.0), writes=[t.b0])
                self.load(t, wsrc[l, 2 * c], ap=t.ap[0:64, 0:64])
                self.load(t, wsrc[l, 2 * c + 1], ap=t.ap[64:128, 64:128])
            x_, t_, nsp = sp.ap[:, 4 * c:4 * c + 1], sp.ap[:, 4 * c + 1:4 * c + 2], sp.ap[:, 4 * c + 2:4 * c + 3]
            self.act(x_, lam(c), AF.Exp, reads=[pvb], writes=[sp.b0], scale=-1.0)
            self.ts("dve", t_, x_, -0.25, 1.0 / 3.0, ALU.mult, ALU.add, reads=[sp.b0], writes=[sp.b0])
            self.tt("dve", t_, t_, x_, ALU.mult, reads=[sp.b0], writes=[sp.b0])
            self.ts("dve", t_, t_, -1.0, 0.5, ALU.mult, ALU.add, reads=[sp.b0], writes=[sp.b0])
            self.tt("dve", t_, t_, x_, ALU.mult, reads=[sp.b0], writes=[sp.b0])
            self.ts("dve", t_, t_, -1.0, 1.0, ALU.mult, ALU.add, reads=[sp.b0], writes=[sp.b0])
            self.tt("dve", t_, t_, x_, ALU.mult, reads=[sp.b0], writes=[sp.b0])
            self.ts("dve", nsp, t_, -8.0, 0.0, ALU.mult, ALU.add, reads=[sp.b0], writes=[sp.b0])
            self.ts("dve", sp.ap[:, 4 * c + 3:4 * c + 4], t_, -4.0, 0.0, ALU.mult, ALU.add, reads=[sp.b0], writes=[sp.b0])
            hb = L["hb"]
            self.ts("dve", hb.ap[:, 2 * c:2 * c + 1], self.pv(("ba", l))(c), 0.5, 0.0, ALU.mult, ALU.add, reads=[pvb], writes=[hb.b0])
            self.ts("dve", hb.ap[:, 2 * c + 1:2 * c + 2], self.pv(("bx", l))(c), 0.5, 0.0, ALU.mult, ALU.add, reads=[pvb], writes=[hb.b0])
            xb = L["XB"][c]
            S.op("dve", lambda e, xb=xb: e.memset(xb.ap[:, 0:3], 0.0), writes=[xb.b0])
        self.L = L

    def lru_tile(self, l, q):
        S, L = self.S, self.L
        qs = slice(q * TQ, (q + 1) * TQ)
        pvb = self.cst["pvec"].b0
        ones = self.cst["onesb"]
        cw, cb, ba_, bx_ = (self.pv((k, l)) for k in ("cw", "cb", "ba", "bx"))
        on = self.pv(("on", l))
        XL, W, sp, H = L["XL"], L["W"], L["sp"], L["H"]
        GT, XC, A, I, M, Qp, Tg, YGt, sq, rstd = (L[k] for k in ("GT", "XC", "A", "I", "M", "Qp", "Tg", "YGt", "sq", "rstd"))
        self.cp("dve", XL.ap, self.XT[:, :, qs], reads=self.XTB[q], writes=[XL.b0])
        yield
        for c in range(2):
            XB = L["XB"][c]
            nsp = sp.ap[:, 4 * c + 2:4 * c + 3]
            bk = self.bank()
            self.mmg(bk, self.ps[bk][:, :], [(W.ap[:, kc, c * 128:(c + 1) * 128], XL.ap[:, kc, :]) for kc in range(KC)],
                     reads=[W.b0, XL.b0])
            self.cp("act", XB.ap[:, 3:3 + TQ], self.ps[bk][:, :], reads=[self.psb[bk]], writes=[XB.b0])
            yield
            bk = self.bank()
            self.mmg(bk, self.ps[bk][:, :], [(W.ap[:, kc, 256 + c * 128:256 + (c + 1) * 128], XL.ap[:, kc, :]) for kc in range(KC)],
                     reads=[W.b0, XL.b0])
            self.cp("dve", GT.ap, self.ps[bk][:, :], reads=[self.psb[bk]], writes=[GT.b0])
            yield
            self.ts("dve", XC.ap, XB.ap[:, 0:TQ], cw(0 * 2 + c), cb(c), ALU.mult, ALU.add, reads=[XB.b0, pvb], writes=[XC.b0])
            self.tt("dve", Tg.ap, GT.ap, GT.ap, ALU.mult, reads=[GT.b0], writes=[Tg.b0])
            yield
            self.stt(XC.ap, XB.ap[:, 1:1 + TQ], cw(1 * 2 + c), XC.ap, ALU.mult, ALU.add, reads=[XB.b0, XC.b0, pvb], writes=[XC.b0])
            self.ts("dve", Tg.ap, Tg.ap, 0.044715, 1.0, ALU.mult, ALU.add, reads=[Tg.b0], writes=[Tg.b0])
            yield
            self.stt(XC.ap, XB.ap[:, 2:2 + TQ], cw(2 * 2 + c), XC.ap, ALU.mult, ALU.add, reads=[XB.b0, XC.b0, pvb], writes=[XC.b0])
            self.tt("dve", Tg.ap, Tg.ap, GT.ap, ALU.mult, reads=[Tg.b0, GT.b0], writes=[Tg.b0])
            yield
            self.stt(XC.ap, XB.ap[:, 3:3 + TQ], cw(3 * 2 + c), XC.ap, ALU.mult, ALU.add, reads=[XB.b0, XC.b0, pvb], writes=[XC.b0])
            self.act(Tg.ap, Tg.ap, AF.Tanh, reads=[Tg.b0], writes=[Tg.b0], scale=0.7978845608028654)
            yield
            self.cp("dve", XB.ap[:, 0:3], XB.ap[:, TQ:TQ + 3], reads=[XB.b0], writes=[XB.b0])
            yield
            wbd = L["wbd"][c]
            b1, b2 = self.bank(), self.bank()
            self.mmg(b1, self.ps[b1][:, :], [(wbd[0].ap, XC.ap)], reads=[wbd[0].b0, XC.b0])
            self.mmg(b2, self.ps[b2][:, :], [(wbd[1].ap, XC.ap)], reads=[wbd[1].b0, XC.b0])
            hb = L["hb"]
            self.act(A.ap, self.ps[b1][:, :], AF.Tanh, reads=[self.psb[b1], hb.b0], writes=[A.b0], bias=hb.ap[:, 2 * c:2 * c + 1], scale=0.5)
            self.act(I.ap, self.ps[b2][:, :], AF.Tanh, reads=[self.psb[b2], hb.b0], writes=[I.b0], bias=hb.ap[:, 2 * c + 1:2 * c + 2], scale=0.5)
            yield
            self.stt(Tg.ap, Tg.ap, 1.0, GT.ap, ALU.add, ALU.mult, reads=[Tg.b0, GT.b0], writes=[Tg.b0])
            yield
            hnsp = sp.ap[:, 4 * c + 3:4 * c + 4]
            self.ts("dve", A.ap, A.ap, hnsp, hnsp, ALU.mult, ALU.add, reads=[A.b0, sp.b0], writes=[A.b0])
            yield
            self.ts("dve", Qp.ap, A.ap, -1.0 / 120.0, 0.0, ALU.mult, ALU.add, reads=[A.b0], writes=[Qp.b0])
            yield
            for cc in (-1.0 / 24.0, -1.0 / 6.0, -0.5, -1.0):
                self.stt(Qp.ap, Qp.ap, cc, A.ap, ALU.add, ALU.mult, reads=[Qp.b0, A.b0], writes=[Qp.b0])
                yield
            self.ts("dve", A.ap, Qp.ap, -1.0, 1.0, ALU.mult, ALU.add, reads=[Qp.b0], writes=[A.b0])
            yield
            self.stt(M.ap, Qp.ap, 2.0, Qp.ap, ALU.subtract, ALU.mult, reads=[Qp.b0], writes=[M.b0])
            yield
            self.act(M.ap, M.ap, AF.Sqrt, reads=[M.b0], writes=[M.b0], scale=-0.25)
            yield
            self.stt(I.ap, I.ap, 1.0, XC.ap, ALU.add, ALU.mult, reads=[I.b0, XC.b0], writes=[I.b0])
            yield
            self.tt("dve", I.ap, I.ap, M.ap, ALU.mult, reads=[I.b0, M.b0], writes=[I.b0])
            yield
            init = 0.0 if q == 0 else H.ap[:, c:c + 1]
            S.op("dve", lambda e, init=init: e.tensor_tensor_scan(M.ap, A.ap, I.ap, init, ALU.mult, ALU.add),
                 reads=[A.b0, I.b0, H.b0], writes=[M.b0])
            yield
            self.cp("dve", H.ap[:, c:c + 1], M.ap[:, TQ - 1:TQ], reads=[M.b0], writes=[H.b0])
            self.stt(YGt.ap[:, c, :], M.ap, 0.5, Tg.ap, ALU.mult, ALU.mult, reads=[M.b0, Tg.b0], writes=[YGt.b0])
            yield
        bk = self.bank()
        for c in range(2):
            self.act(sq.ap[:, c, :], YGt.ap[:, c, :], AF.Square, reads=[YGt.b0], writes=[sq.b[c]])
            S.op("pe", (lambda e, o=self.ps[bk][:, :], r=sq.ap[:, c, :], s=(c == 0), p=(c == 1):
                        e.matmul(o, ones.ap, r, start=s, stop=p)),
                 reads=[sq.b[c], ones.b0], writes=[self.psb[bk]], signal=True)
            yield
        self.act(rstd.ap, self.ps[bk][:, :], AF.Ln, reads=[self.psb[bk]], writes=[rstd.b0], bias=1e-6, scale=1.0 / 256)
        yield
        self.act(rstd.ap, rstd.ap, AF.Exp, reads=[rstd.b0], writes=[rstd.b0], scale=-0.5)
        yield
        for c in range(2):
            self.stt(YGt.ap[:, c, :], YGt.ap[:, c, :], on(2 + c), rstd.ap, ALU.mult, ALU.mult,
                     reads=[YGt.b0, rstd.b0, pvb], writes=[YGt.b0])
            yield
        S.dma("sp", self.d["yscr"].rearrange("(c p) s -> p c s", p=128)[:, 2:4, qs], YGt.ap, YGt.b0,
              reads=[YGt.b0], writes=[self.YSB[q]])
        yield

    def mixer_lru(self, l):
        d, S = self.d, self.S
        YG = self.carve("YGb", [128, 2, SEQ], nb=NQ)
        Wx = self.carve("Wlx", [128, KC, 128], dt=BF16)
        Wg = self.carve("Wlg", [128, KC, 128], dt=BF16)
        wbd = [self.carve("wbd%d" % k, [128, 128]) for k in range(2)]
        XB = self.carve("XB", [128, SEQ + 4], nb=NQ)
        GT = self.carve("GT", [128, SEQ], nb=NQ)
        XC = self.carve("XC", [128, SEQ])
        A = self.carve("A", [128, SEQ])
        I = self.carve("I", [128, SEQ])
        M = self.carve("M", [128, SEQ])
        Qp = self.carve("Qp", [128, SEQ])
        sp = self.carve("sp", [128, 8])
        pvb = self.cst["pvec"].b0
        cw, cb, ba_, bx_, lam = (self.pv((k, l)) for k in ("cw", "cb", "ba", "bx", "lam"))
        for c in range(2):
            self.load(Wx, self.win(l, O_LX + c * 128, 128), queue="pool")
            self.load(Wg, self.win(l, O_LG + c * 128, 128), queue="pool")
            for k, wsrc in enumerate((d["wa"], d["wx"])):
                S.op("pool", lambda e, t=wbd[k]: e.memset(t.ap, 0.0), writes=[wbd[k].b0])
                self.load(wbd[k], wsrc[l, 2 * c], ap=wbd[k].ap[0:64, 0:64])
                self.load(wbd[k], wsrc[l, 2 * c + 1], ap=wbd[k].ap[64:128, 64:128])
            x_, t_, nsp = sp.ap[:, 0:1], sp.ap[:, 1:2], sp.ap[:, 2:3]
            self.act(x_, lam(c), AF.Exp, reads=[pvb], writes=[sp.b0], scale=-1.0)
            self.ts("dve", t_, x_, -0.25, 1.0 / 3.0, ALU.mult, ALU.add, reads=[sp.b0], writes=[sp.b0])
            self.tt("dve", t_, t_, x_, ALU.mult, reads=[sp.b0], writes=[sp.b0])
            self.ts("dve", t_, t_, -1.0, 0.5, ALU.mult, ALU.add, reads=[sp.b0], writes=[sp.b0])
            self.tt("dve", t_, t_, x_, ALU.mult, reads=[sp.b0], writes=[sp.b0])
            self.ts("dve", t_, t_, -1.0, 1.0, ALU.mult, ALU.add, reads=[sp.b0], writes=[sp.b0])
            self.tt("dve", t_, t_, x_, ALU.mult, reads=[sp.b0], writes=[sp.b0])
            self.ts("dve", nsp, t_, -8.0, 0.0, ALU.mult, ALU.add, reads=[sp.b0], writes=[sp.b0])
            S.op("pool", lambda e: e.memset(XB.ap[:, 0:3], 0.0), writes=[XB.b[0]])
            self.proj_fm(Wx, 128, lambda q: (XB.ap[:, 3 + q * TQ:3 + (q + 1) * TQ], [XB.b[q]]), evac="act")
            self.proj_fm(Wg, 128, lambda q: (GT.ap[:, q * TQ:(q + 1) * TQ], [GT.b[q]]), evac="dve")
            xb_all = XB.b
            self.ts("dve", XC.ap, XB.ap[:, 0:SEQ], cw(0 * 2 + c), cb(c), ALU.mult, ALU.add, reads=xb_all + [pvb], writes=[XC.b0])
            for j in range(1, 4):
                self.stt(XC.ap, XB.ap[:, j:j + SEQ], cw(j * 2 + c), XC.ap, ALU.mult, ALU.add, reads=xb_all + [XC.b0, pvb], writes=[XC.b0])
            for q in range(NQ):
                qs = slice(q * TQ, (q + 1) * TQ)
                b1, b2 = self.bank(), self.bank()
                self.mmg(b1, self.ps[b1][:, :], [(wbd[0].ap, XC.ap[:, qs])], reads=[wbd[0].b0, XC.b0])
                self.mmg(b2, self.ps[b2][:, :], [(wbd[1].ap, XC.ap[:, qs])], reads=[wbd[1].b0, XC.b0])
                self.act(A.ap[:, qs], self.ps[b1][:, :], AF.Sigmoid, reads=[self.psb[b1], pvb], writes=[A.b0], bias=ba_(c))
                self.act(I.ap[:, qs], self.ps[b2][:, :], AF.Sigmoid, reads=[self.psb[b2], pvb], writes=[I.b0], bias=bx_(c))
            gall = GT.b
            Tg = XB.ap[:, 0:SEQ]
            tb = XB.b
            self.tt("pool", Tg, GT.ap, GT.ap, ALU.mult, reads=gall + [XC.b0], writes=tb)
            self.ts("pool", Tg, Tg, 0.044715, 1.0, ALU.mult, ALU.add, reads=tb, writes=tb)
            self.tt("pool", Tg, Tg, GT.ap, ALU.mult, reads=tb + gall, writes=tb)
            self.act(Tg, Tg, AF.Sigmoid, reads=tb, writes=tb, scale=1.5957691216057308)
            self.tt("pool", Tg, Tg, GT.ap, ALU.mult, reads=tb + gall, writes=tb)
            self.ts("dve", A.ap, A.ap, nsp, 0.0, ALU.mult, ALU.add, reads=[A.b0, sp.b0], writes=[A.b0])
            self.ts("dve", Qp.ap, A.ap, -1.0 / 120.0, 0.0, ALU.mult, ALU.add, reads=[A.b0], writes=[Qp.b0])
            for cc in (-1.0 / 24.0, -1.0 / 6.0, -0.5, -1.0):
                self.stt(Qp.ap, Qp.ap, cc, A.ap, ALU.add, ALU.mult, reads=[Qp.b0, A.b0], writes=[Qp.b0])
            self.ts("dve", A.ap, Qp.ap, -1.0, 1.0, ALU.mult, ALU.add, reads=[Qp.b0], writes=[A.b0])
            self.stt(M.ap, Qp.ap, 2.0, Qp.ap, ALU.subtract, ALU.mult, reads=[Qp.b0], writes=[M.b0])
            self.act(M.ap, M.ap, AF.Sqrt, reads=[M.b0], writes=[M.b0], scale=-1.0)
            self.tt("dve", I.ap, I.ap, XC.ap, ALU.mult, reads=[I.b0, XC.b0], writes=[I.b0])
            self.tt("dve", I.ap, I.ap, M.ap, ALU.mult, reads=[I.b0, M.b0], writes=[I.b0])
            S.op("dve", lambda e: e.tensor_tensor_scan(M.ap, A.ap, I.ap, 0.0, ALU.mult, ALU.add),
                 reads=[A.b0, I.b0], writes=[M.b0])
            self.tt("dve", YG.ap[:, c, :], M.ap, Tg, ALU.mult, reads=[M.b0] + tb, writes=YG.b)
        self.group_finish(l, 1, YG)

    def proj_qz(self, Wq, QZ):
        S = self.S
        S.op("pool", lambda e: e.memset(QZ[0].ap[64:128, :], 0.0), writes=QZ[0].b)
        S.op("pool", lambda e: e.memset(QZ[1].ap[0:64, :], 0.0), writes=QZ[1].b)
        XBF = self.XBF
        for q in range(NQ):
            qs = slice(q * TQ, (q + 1) * TQ)
            bk = self.bank()
            self.mmg(bk, self.ps[bk][:, :], [(Wq.ap[:, kc, :], XBF.ap[:, kc, qs]) for kc in range(KC)],
                     reads=[Wq.b0, XBF.b[q]])
            self.cp("act", QZ[0].ap[0:64, qs], self.ps[bk][0:64, :], reads=[self.psb[bk]], writes=[QZ[0].b[q]])
            self.cp("dve", QZ[1].ap[64:128, qs], self.ps[bk][64:128, :], reads=[self.psb[bk]], writes=[QZ[1].b[q]])

    def pair_proj(self, l, offs, hp, names):
        out = []
        for nm, o in zip(names, offs):
            t = self.carve(nm, [128, KC, 128], dt=BF16)
            self.load(t, self.win(l, o + hp * 128, 128), queue="pool")
            out.append(t)
        return out

    def v_tokmajor(self, Wv, VP):
        S = self.S
        S.op("pool", lambda e: e.memset(VP.ap[:, :, :, 64:128], 1.0), writes=[VP.b0])
        for g4 in range(4):
            bk = self.bank()
            for t4 in range(4):
                tc = g4 * 4 + t4
                self.mmg(bk, self.ps[bk][:, t4 * 128:(t4 + 1) * 128],
                         [(self.XBF.ap[:, kc, tc * 128:(tc + 1) * 128], Wv.ap[:, kc, :]) for kc in range(KC)],
                         reads=[self.XBF.b[g4], Wv.b0])
            self.cp("dve", VP.ap[:, g4 * 4:(g4 + 1) * 4, :, 0:64],
                    self.ps[bk][:, :].rearrange("p (a b c) -> p a b c", a=4, b=2),
                    reads=[self.psb[bk]], writes=[VP.b0])

    def mixer_dil(self, l):
        S = self.S
        YG = self.carve("YGc", [128, 2, SEQ], nb=NQ)
        YH = self.carve("YHc", [128, SEQ], nb=NQ)
        sets = []
        for hp in range(2):
            Wq, Wk, Wv = self.pair_proj(l, (O_SQ, O_SK, O_SV), hp, ("Wq%d" % hp, "Wk%d" % hp, "Wv%d" % hp))
            QZ = [self.carve("QZ%d_%d" % (hp, k), [128, SEQ], nb=NQ, dt=BF16) for k in range(2)]
            KP = self.carve("KP%d" % hp, [128, SEQ], nb=NQ, dt=BF16)
            VP = self.carve("VP%d" % hp, [128, 16, 2, 128], dt=BF16)
            sets.append((Wq, Wk, Wv, QZ, KP, VP))

        def proj(hp):
            Wq, Wk, Wv, QZ, KP, VP = sets[hp]
            self.proj_qz(Wq, QZ)
            self.proj_fm(Wk, 128, lambda q: (KP.ap[:, q * TQ:(q + 1) * TQ], [KP.b[q]]), evac="dve")
            self.v_tokmajor(Wv, VP)

        def attn(hp):
            Wq, Wk, Wv, QZ, KP, VP = sets[hp]
            for h2 in range(2):
                def fin(q, acc, h2=h2, hp=hp):
                    qs = slice(q * TQ, (q + 1) * TQ)
                    if h2 == 0:
                        self.finish_softmax(q, acc, YG.ap[0:64, hp, qs], [YG.b[q]])
                    else:
                        self.finish_softmax(q, acc, YH.ap[0:64, qs], [YH.b[q]])
                        S.dma("sp", YG.ap[64:128, hp, qs], YH.ap[0:64, qs], YH.b[q], reads=[YH.b[q]], writes=[YG.b[q]])
                self.attn_head(lambda j: (KP.ap[:, j * 128:(j + 1) * 128], [KP.b[j // 4]]),
                               lambda lo, hi, h2=h2: (QZ[h2].ap[:, lo:hi], [QZ[h2].b[lo // TQ]]), 128,
                               lambda j, h2=h2: (VP.ap[:, j, h2, :], [VP.b0]), "dil", 0.125, fin)
        proj(0)
        proj(1)
        attn(0)
        attn(1)
        self.group_finish(l, 2, YG)

    def mixer_mlstm(self, l):
        d, S = self.d, self.S
        YG = self.carve("YGd", [128, 2, SEQ], nb=NQ)
        YH = self.carve("YHd", [128, SEQ], nb=NQ)
        CS = self.carve("CS", [128, SEQ], nb=NQ)
        BIAS = self.carve("BIAS", [128, 64])
        small = self.carve("small", [128, 4])
        base1 = self.aoff
        Wg = self.carve("Wgt", [128, KC, 8], dt=BF16)
        DB = self.carve("DB", [128, SEQ], nb=NQ)
        E1 = self.carve("E1", [128, SEQ])
        ONE4 = self.carve("one4", [128, SEQ])
        pvb = self.cst["pvec"].b0
        ident, selh = self.cst["ident"], self.cst["selh"]
        self.load(Wg, self.win(l, O_MI, 8), queue="pool")
        bi, bf = self.pv(("bi", l)), self.pv(("bf", l))
        nbf = small.ap[0:4, 0:1]
        self.ts("dve", nbf, bf(0, 0, 4), -1.0, 0.0, ALU.mult, ALU.add, reads=[pvb], writes=[small.b0])
        S.op("pool", lambda e: e.memset(ONE4.ap[0:4, :], 1.0), writes=[ONE4.b0])
        S.op("pool", lambda e: e.memset(CS.ap, 0.0), writes=CS.b)
        for q in range(NQ):
            qs = slice(q * TQ, (q + 1) * TQ)
            b1, b2 = self.bank(), self.bank()
            self.mmg(b1, self.ps[b1][0:4, :], [(Wg.ap[:, kc, 0:4], self.XBF.ap[:, kc, qs]) for kc in range(KC)], reads=[Wg.b0, self.XBF.b[q]])
            self.mmg(b2, self.ps[b2][0:4, :], [(Wg.ap[:, kc, 4:8], self.XBF.ap[:, kc, qs]) for kc in range(KC)], reads=[Wg.b0, self.XBF.b[q]])
            self.act(E1.ap[0:4, qs], self.ps[b2][0:4, :], AF.Exp, reads=[self.psb[b2], small.b0], writes=[E1.b0], bias=nbf, scale=-1.0)
            self.act(E1.ap[0:4, qs], E1.ap[0:4, qs], AF.Ln, reads=[E1.b0], writes=[E1.b0], bias=1.0, scale=1.0)
            self.ts("dve", DB.ap[0:4, qs], self.ps[b1][0:4, :], bi(0, 0, 4), 0.0, ALU.add, ALU.add,
                    reads=[self.psb[b1], pvb], writes=[DB.b[q]])
        S.op("dve", lambda e: e.tensor_tensor_scan(CS.ap[0:4, :], ONE4.ap[0:4, :], E1.ap[0:4, :], 0.0, ALU.mult, ALU.add),
             reads=[ONE4.b0, E1.b0], writes=CS.b)
        self.tt("dve", DB.ap[0:4, :], DB.ap[0:4, :], CS.ap[0:4, :], ALU.add, reads=DB.b + CS.b, writes=DB.b)
        bk = self.bank()
        for tc in range(16):
            S.op("pe", (lambda e, o=self.ps[bk][:, tc * 4:(tc + 1) * 4], i=DB.ap[0:4, tc * 128:(tc + 1) * 128]:
                        e.transpose(o, i, ident.ap[0:4, 0:4])),
                 reads=DB.b + [ident.b0], writes=[self.psb[bk]], signal=(tc == 15))
        self.cp("dve", BIAS.ap[:, 0:64], self.ps[bk][:, 0:64], reads=[self.psb[bk]], writes=[BIAS.b0])
        self.dump("cs%d" % l, CS.ap, CS.b, [128, SEQ])
        self.dump("bias%d" % l, BIAS.ap, BIAS.b, [128, 64])
        self.rewind(base1, soft=True)
        CSR = [self.carve("CSR0", [128, SEQ], nb=NQ)] * 2
        OG = self.carve("OG", [128, SEQ], nb=NQ)
        Wo = self.carve("Wo", [128, KC, 64], dt=BF16)
        base = self.aoff
        for hp in range(2):
            self.rewind(base, soft=True)
            Wq, Wk, Wv = self.pair_proj(l, (O_MQ, O_MK, O_MV), hp, ("Wq", "Wk", "Wv"))
            QZ = [self.carve("QZ%d" % k, [128, SEQ], nb=NQ, dt=BF16) for k in range(2)]
            KP = self.carve("KP", [128, SEQ], nb=NQ, dt=BF16)
            VP = self.carve("VP", [128, 16, 2, 128], dt=BF16)
            self.proj_qz(Wq, QZ)
            self.proj_fm(Wk, 128, lambda q: (KP.ap[:, q * TQ:(q + 1) * TQ], [KP.b[q]]), evac="dve")
            self.v_tokmajor(Wv, VP)
            for h2 in range(2):
                h = hp * 2 + h2
                pb = 64 * h2
                csr = CSR[h2]
                for q in range(NQ):
                    qs = slice(q * TQ, (q + 1) * TQ)
                    bk = self.bank()
                    self.mmg(bk, self.ps[bk][:, :], [(selh.ap[:, h * 128:(h + 1) * 128], CS.ap[:, qs])], reads=[selh.b0, CS.b[q]])
                    self.cp("act", csr.ap[:, qs], self.ps[bk][:, :], reads=[self.psb[bk]], writes=[csr.b[q]])
                self.load(Wo, self.win(l, O_MO + h * 64, 64), queue="pool")
                for q in range(NQ):
                    qs = slice(q * TQ, (q + 1) * TQ)
                    bk = self.bank()
                    self.mmg(bk, self.ps[bk][0:64, :], [(Wo.ap[:, kc, :], self.XBF.ap[:, kc, qs]) for kc in range(KC)], reads=[Wo.b0, self.XBF.b[q]])
                    self.act(OG.ap[0:64, qs], self.ps[bk][0:64, :], AF.Sigmoid, reads=[self.psb[bk]], writes=[OG.b[q]])

                def fin(q, acc, h2=h2, hp=hp):
                    qs = slice(q * TQ, (q + 1) * TQ)
                    gate = (OG.ap[0:64, qs], [OG.b[q]])
                    if h2 == 0:
                        self.finish_softmax(q, acc, YG.ap[0:64, hp, qs], [YG.b[q]], mlstm=True, gate=gate)
                    else:
                        self.finish_softmax(q, acc, YH.ap[0:64, qs], [YH.b[q]], mlstm=True, gate=gate)
                        S.dma("sp", YG.ap[64:128, hp, qs], YH.ap[0:64, qs], YH.b[q], reads=[YH.b[q]], writes=[YG.b[q]])
                self.attn_head(lambda j: (KP.ap[:, j * 128:(j + 1) * 128], [KP.b[j // 4]]),
                               lambda lo, hi, h2=h2: (QZ[h2].ap[:, lo:hi], [QZ[h2].b[lo // TQ]]), 128,
                               lambda j, h2=h2: (VP.ap[:, j, h2, :], [VP.b0]), "mlstm", 0.125, fin,
                               aux=(csr, lambda j, h=h: (BIAS.ap[:, j * 4 + h:j * 4 + h + 1], [BIAS.b0])))
        self.group_finish(l, 3, YG)

    def phase_outproj(self, l):
        d, S = self.d, self.S
        self.reset_arena(soft=True)
        NBO = 3
        YT = self.carve("YT", [128, 2, KC, TQ], nb=2, dt=BF16)
        wo = [self.carve("wo%d" % k, [128, KC, 128], dt=BF16) for k in range(NBO)]
        sq = self.carve("sq", [128, 2, 2 * TQ], nb=2, dt=BF16)
        mean = self.carve("mean", [128, TQ])
        var = self.carve("var", [128, TQ])
        rstd = self.carve("rstd", [128, TQ])
        wov = d["wout"][l].rearrange("(kc p) m -> p kc m", p=128)
        ysv = d["yscr"].rearrange("(c p) s -> p c s", p=128)
        jobs = [(q, dc) for q in range(NQ) for dc in range(KC)]
        st = [0]

        def issue(upto):
            while st[0] <= upto and st[0] < len(jobs):
                n = st[0]
                st[0] += 1
                q, dc = jobs[n]
                if dc == 0:
                    yi = q % 2
                    S.dma("pool", YT.ap[:, yi, :, :], ysv[:, :, q * TQ:(q + 1) * TQ], YT.b[yi], reads=[self.YSB[q]], writes=[YT.b[yi]])
                t = wo[n % NBO]
                S.dma("pool", t.ap, wov[:, :, dc * 128:(dc + 1) * 128], t.b0, writes=[t.b0])
        n = 0
        for q in range(NQ):
            qs = slice(q * TQ, (q + 1) * TQ)
            yi = q % 2
            for dc in range(KC):
                issue(n + NBO - 1)
                t = wo[n % NBO]
                n += 1
                bk = self.bank()
                self.mmg(bk, self.ps[bk][:, :], [(t.ap[:, kc, :], YT.ap[:, yi, kc, :]) for kc in range(KC)], reads=[t.b0, YT.b[yi]])
                self.stt(self.XT[:, dc, qs], self.XT[:, dc, qs], ALPHA, self.ps[bk][:, :], ALU.mult, ALU.add,
                         reads=[self.XTB[q][dc], self.psb[bk]], writes=[self.XTB[q][dc]])
            self.layernorm(q, ("lng", l, 1), ("lnb", l, 1), 1.0e-5, (sq, mean, var, rstd))


_CACHE = {}


def _inmaps(inputs):
    cs = _consts()
    pv = _pvec(inputs)
    maps = []
    shared = {
        "ffn_w1": np.ascontiguousarray(inputs["ffn_w1"], np.float32),
        "ffn_w3": np.ascontiguousarray(inputs["ffn_w3"], np.float32),
        "ffn_w2": np.ascontiguousarray(inputs["ffn_w2"], np.float32),
        "w_in": np.ascontiguousarray(inputs["w_in"], np.float32),
        "mla_w_uq": np.ascontiguousarray(inputs["mla_w_uq"], np.float32),
        "mla_w_ukv": np.ascontiguousarray(inputs["mla_w_ukv"], np.float32),
        "lru_w_a": np.ascontiguousarray(inputs["lru_w_a"], np.float32),
        "lru_w_x": np.ascontiguousarray(inputs["lru_w_x"], np.float32),
        "w_out": np.ascontiguousarray(inputs["w_out"], np.float32),
        "pvec": pv,
    }
    for k, v in cs.items():
        shared["c_" + k] = v
    return shared


def kernel(**inputs):
    inputs = {k: np.asarray(v) for k, v in inputs.items()}
    if "nc" not in _CACHE:
        _CACHE["nc"] = Builder().build()
    nc = _CACHE["nc"]
    shared = _inmaps(inputs)
    x = np.ascontiguousarray(inputs["x"], np.float32)
    in_maps = []
    for b in range(8):
        m = dict(shared)
        m["x"] = x[b]
        in_maps.append(m)
    res = run_bass_kernel_spmd(nc, in_maps, core_ids=list(range(8)))
    return np.stack([r["out"] for r in res.results], axis=0).astype(np.float32)
```

```python
import numpy as np
from contextlib import ExitStack
import concourse.bass as bass
import concourse.mybir as mybir
from concourse.bass_utils import run_bass_kernel_spmd

F32 = mybir.dt.float32
BF16 = mybir.dt.bfloat16
AF = mybir.ActivationFunctionType
ALU = mybir.AluOpType

SEQ = 2048
DM = 1024
DFF = 2816
DEPTH = 2
NQ = 4
TQ = 512
KC = 8
FC = 22
ALPHA = (2.0 * DEPTH) ** 0.25
N_IN = 2728
O_CQ, O_CKV, O_KR, O_LX, O_LG, O_SQ, O_SK, O_SV, O_MQ, O_MK, O_MV, O_MO, O_MI, O_MF = (
    0, 256, 384, 416, 672, 928, 1184, 1440, 1696, 1952, 2208, 2464, 2720, 2724)
NEG = -30000.0


class Buf:
    __slots__ = ("name", "excl", "writers", "readers", "sem", "cnt")

    def __init__(self, name, excl=False):
        self.name = name
        self.excl = excl
        self.writers = {}
        self.readers = {}
        self.sem = None
        self.cnt = 0


class DSem:
    __slots__ = ("sem", "cnt")

    def __init__(self, sem):
        self.sem = sem
        self.cnt = 0


class Sched:
    ENG = ("pe", "act", "dve", "pool")

    def __init__(self, nc, stack):
        self.nc = nc
        self.stack = stack
        self.q = {k: [] for k in ("pe", "act", "dve", "pool", "sp")}
        self.tick = {k: 0 for k in self.ENG}
        self.sem = {k: stack.enter_context(nc.semaphore("s_" + k)) for k in self.ENG}
        self.seen = {k: {} for k in self.q}
        self.dsems = []
        self.dfree = []

    def _semof(self, key):
        return self.sem[key] if isinstance(key, str) else key.sem

    def _dsem(self):
        if self.dfree:
            return self.dfree.pop()
        ds = DSem(self.stack.enter_context(self.nc.semaphore("d%d" % len(self.dsems))))
        self.dsems.append(ds)
        return ds

    def release(self, bufs):
        for b in bufs:
            if b.sem is not None:
                self.dfree.append(b.sem)
                b.sem = None
            b.writers = {}
            b.readers = {}

    def _collect(self, queue, reads, writes, ownkey=None):
        need = {}

        def add(k, t):
            if k is ownkey:
                return
            if queue == "pe" and k == "pe":
                return
            if need.get(k, 0) < t:
                need[k] = t
        for b in reads:
            for k, t in b.writers.items():
                add(k, t)
            if b.excl:
                for k, t in b.readers.items():
                    add(k, t)
        for b in writes:
            for k, t in b.writers.items():
                add(k, t)
            for k, t in b.readers.items():
                add(k, t)
        waits = []
        seen = self.seen[queue]
        for k, t in need.items():
            if seen.get(k, 0) >= t:
                continue
            seen[k] = t
            waits.append((self._semof(k), t))
        return waits

    def _update(self, key, tick, reads, writes):
        for b in reads:
            if b.excl:
                b.writers = {key: tick}
                b.readers = {}
            elif b.readers.get(key, 0) < tick:
                b.readers[key] = tick
        for b in writes:
            b.writers = {key: tick}
            b.readers = {}

    def op(self, queue, fn, reads=(), writes=(), signal=True):
        waits = self._collect(queue, reads, writes)
        tick = self.tick[queue] + 1
        if signal:
            self.tick[queue] = tick
        self._update(queue, tick, reads, writes)
        self.q[queue].append((waits, fn, self.sem[queue] if signal else None, 1))

    def dma(self, queue, out, in_, sembuf, reads=(), writes=()):
        if sembuf.sem is None:
            sembuf.sem = self._dsem()
        ds = sembuf.sem
        waits = self._collect(queue, reads, writes, ownkey=ds)
        ds.cnt += 1
        tick = 16 * ds.cnt
        self._update(ds, tick, reads, writes)
        self.q[queue].append((waits, lambda e: e.dma_start(out=out, in_=in_), ds.sem, 16))

    def barrier(self):
        for queue in self.q:
            seen = self.seen[queue]
            waits = []
            for k in self.ENG:
                t = self.tick[k]
                if t > 0 and seen.get(k, 0) < t:
                    seen[k] = t
                    waits.append((self.sem[k], t))
            for b in self.dsems:
                t = 16 * b.cnt
                if seen.get(b, 0) < t:
                    seen[b] = t
                    waits.append((b.sem, t))
            if waits:
                self.q[queue].append((waits, None, None, 0))

    def emit(self):
        qs = self.q

        def replay(lst, e):
            for waits, fn, sem, val in lst:
                for s, t in waits:
                    e.wait_ge(s, t)
                if fn is None:
                    continue
                ins = fn(e)
                if sem is not None:
                    ins.then_inc(sem, val)
        with self.nc.Block() as block:
            @block.tensor
            def _(e):
                replay(qs["pe"], e)

            @block.scalar
            def _(e):
                replay(qs["act"], e)

            @block.vector
            def _(e):
                replay(qs["dve"], e)

            @block.gpsimd
            def _(e):
                replay(qs["pool"], e)

            @block.sync
            def _(e):
                replay(qs["sp"], e)


class Tl:
    def __init__(self, ap, name, nb=1):
        self.ap = ap
        self.b = [Buf("%s%d" % (name, i)) for i in range(nb)]

    @property
    def b0(self):
        return self.b[0]


def _consts():
    c = {}
    c["ident"] = np.eye(128, dtype=np.float32)
    c["ones"] = np.ones((128, 128), np.float32)
    k = np.arange(128)[:, None]
    q = np.arange(128)[None, :]
    c["tri01"] = (q >= k).astype(np.float32)
    c["trineg"] = np.where(q >= k, 0.0, NEG).astype(np.float32)
    x = np.arange(SEQ)[None, :]
    d = x - k
    cnt = ((d >= 0) & (d <= 128)).astype(np.float32)
    cnt += ((d >= 0) & (d % 4 == 0) & (d <= 512)).astype(np.float32)
    cnt += ((d >= 0) & (d % 16 == 0) & (d <= 2048)).astype(np.float32)
    c["ctab"] = cnt.astype(np.float32)
    sel = np.zeros((128, 128), np.float32)
    sel[64, :] = 1.0
    c["sel"] = sel
    selh = np.zeros((128, 4, 128), np.float32)
    for h in range(4):
        selh[h, h, :] = 1.0
    c["selh"] = np.ascontiguousarray(selh.reshape(128, 512))
    half = 16
    freqs = (np.float32(10000.0) ** (-np.arange(half, dtype=np.float32) / np.float32(half))).astype(np.float32)
    ang = (np.arange(SEQ, dtype=np.float32)[None, :] * freqs[:, None]).astype(np.float32)
    cos, sin = np.cos(ang).astype(np.float32), np.sin(ang).astype(np.float32)
    rope = np.zeros((2, 32, SEQ), np.float32)
    rope[0, :16] = cos
    rope[0, 16:] = cos
    rope[1, :16] = -sin
    rope[1, 16:] = sin
    c["rope"] = rope
    return c


def _pv_layout():
    off = {}
    n = 0

    def add(name, cols):
        nonlocal n
        off[name] = n
        n += cols
    for l in range(DEPTH):
        for i in range(3):
            add(("lng", l, i), 8)
            add(("lnb", l, i), 8)
        add(("qn", l), 2)
        add(("kvn", l), 1)
        add(("cw", l), 8)
        add(("cb", l), 2)
        add(("ba", l), 2)
        add(("bx", l), 2)
        add(("lam", l), 2)
        add(("on", l), 8)
        add(("bi", l), 1)
        add(("bf", l), 1)
    return off, n


PV_OFF, PV_N = _pv_layout()


def _pvec(inp):
    pv = np.zeros((128, PV_N), np.float32)

    def put(name, arr):
        a = np.asarray(arr, np.float32).reshape(-1, 128).T
        pv[:, PV_OFF[name]:PV_OFF[name] + a.shape[1]] = a
    for l in range(DEPTH):
        for i in range(3):
            put(("lng", l, i), inp["ln_g"][l, i])
            put(("lnb", l, i), inp["ln_b"][l, i])
        put(("qn", l), inp["mla_q_norm"][l])
        put(("kvn", l), inp["mla_kv_norm"][l])
        put(("cw", l), inp["lru_conv_w"][l].reshape(-1))
        put(("cb", l), inp["lru_conv_b"][l])
        put(("ba", l), inp["lru_b_a"][l])
        put(("bx", l), inp["lru_b_x"][l])
        put(("lam", l), inp["lru_lambda"][l])
        put(("on", l), inp["out_norm"][l])
        pv[0:4, PV_OFF[("bi", l)]] = inp["ml_b_i"][l]
        pv[0:4, PV_OFF[("bf", l)]] = inp["ml_b_f"][l]
    return pv


class Builder:
    def __init__(self, debug=None):
        self.debug = debug or {}
        self.nc = bass.Bass("TRN2", target_bir_lowering=False)
        self.dbg_out = []

    def dram_in(self, name, shape):
        return self.nc.dram_tensor(name, list(shape), F32, kind="ExternalInput").ap()

    def build(self):
        nc = self.nc
        d = {}
        d["x"] = self.dram_in("x", [SEQ, DM])
        d["w1"] = self.dram_in("ffn_w1", [DEPTH, 2, DM, DFF])
        d["w3"] = self.dram_in("ffn_w3", [DEPTH, 2, DM, DFF])
        d["w2"] = self.dram_in("ffn_w2", [DEPTH, 2, DFF, DM])
        d["win"] = self.dram_in("w_in", [DEPTH, DM, N_IN])
        d["wuq"] = self.dram_in("mla_w_uq", [DEPTH, 256, 384])
        d["wukv"] = self.dram_in("mla_w_ukv", [DEPTH, 128, 512])
        d["wa"] = self.dram_in("lru_w_a", [DEPTH, 4, 64, 64])
        d["wx"] = self.dram_in("lru_w_x", [DEPTH, 4, 64, 64])
        d["wout"] = self.dram_in("w_out", [DEPTH, DM, DM])
        d["pvec"] = self.dram_in("pvec", [128, PV_N])
        for nm, shp in (("ident", [128, 128]), ("ones", [128, 128]), ("tri01", [128, 128]),
                        ("trineg", [128, 128]), ("ctab", [128, SEQ]), ("sel", [128, 128]),
                        ("selh", [128, 512]), ("rope", [2, 32, SEQ])):
            d[nm] = self.dram_in("c_" + nm, shp)
        d["out"] = nc.dram_tensor("out", [SEQ, DM], F32, kind="ExternalOutput").ap()
        d["yscr"] = nc.dram_tensor("yscr", [DM, SEQ], F32).ap()
        self.d = d
        with ExitStack() as st:
            self.st = st
            self.S = Sched(nc, st)
            self._alloc()
            self._program()
            self.S.barrier()
            self.S.emit()
        return nc

    def sbt(self, name, shape):
        return self.st.enter_context(self.nc.sbuf_tensor(name, list(shape), F32))

    def _alloc(self):
        nc, st = self.nc, self.st
        self.XT = self.sbt("XT", [128, KC, SEQ])
        self.XTB = [[Buf("xt%d_%d" % (i, c)) for c in range(KC)] for i in range(NQ)]
        self.YSB = [Buf("yscr%d" % i) for i in range(NQ)]
        self.cst = {}
        for nm, cols in (("ident", 128), ("ones", 128), ("tri01", 128), ("trineg", 128),
                         ("sel", 128), ("selh", 512), ("pvec", PV_N)):
            self.cst[nm] = Tl(self.sbt("k_" + nm, [128, cols])[:], nm)
        for nm, cols in (("identb", 128), ("trinegb", 128), ("onesb", 128), ("ctabb", SEQ)):
            self.cst[nm] = Tl(self.st.enter_context(self.nc.sbuf_tensor("k_" + nm, [128, cols], BF16))[:], nm)
        self.ps = [st.enter_context(nc.psum_tensor("ps%d" % i, [128, 512], F32)) for i in range(8)]
        self.psb = [Buf("ps%d" % i, excl=True) for i in range(8)]
        self._rr = 0
        self._ra = 0
        rem = nc.sbuf_bytes_remaining
        self.AN = (rem - 1024) // 4
        self.arena = self.sbt("arena", [128, self.AN])
        self.aoff = 0
        self.live = []
        self.ghosts = []
        self.leaked = []
        self.attn_filler = None

    def reset_arena(self, soft=False):
        self.rewind(0, soft=soft)

    def rewind(self, base, soft=False):
        keep, dead = [], []
        for ent in self.live:
            (dead if ent[0] >= base else keep).append(ent)
        if soft:
            for off, n, tl in dead:
                users = {}
                for b in tl.b:
                    for src in (b.writers, b.readers):
                        for k, t in src.items():
                            if users.get(k, 0) < t:
                                users[k] = t
                    self.leaked.append(b)
                if users:
                    self.ghosts.append((off, off + n, users))
        else:
            self.S.barrier()
            for off, n, tl in dead:
                self.S.release(tl.b)
            self.S.release(self.leaked)
            self.leaked = []
            self.ghosts = []
        self.live = keep
        self.aoff = base

    def _ghost_users(self, lo, hi):
        users = {}
        for g0, g1, u in self.ghosts:
            if g0 < hi and lo < g1:
                for k, t in u.items():
                    if users.get(k, 0) < t:
                        users[k] = t
        return users

    def track(self, tl, lo, hi):
        u = self._ghost_users(lo, hi)
        if u:
            for b in tl.b:
                b.readers = dict(u)
        self.live.append((lo, hi - lo, tl))
        return tl

    def carve(self, name, shape, nb=1, dt=F32):
        n = int(np.prod(shape[1:]))
        if dt == BF16:
            assert n % 2 == 0
            n //= 2
        assert self.aoff + n <= self.AN, (name, self.aoff, n, self.AN)
        ap = self.arena[0:shape[0], self.aoff:self.aoff + n]
        if dt == BF16:
            ap = ap.bitcast(BF16)
        off0 = self.aoff
        self.aoff += n
        if len(shape) == 3:
            ap = ap.rearrange("p (a b) -> p a b", a=shape[1])
        elif len(shape) == 4:
            ap = ap.rearrange("p (a b c) -> p a b c", a=shape[1], b=shape[2])
        tl = Tl(ap, name, nb)
        u = self._ghost_users(off0, off0 + n)
        if u:
            for b in tl.b:
                b.readers = dict(u)
        self.live.append((off0, n, tl))
        return tl

    def bank(self):
        i = self._rr
        self._rr = (self._rr + 1) % 6
        return i

    def accbank(self):
        i = 6 + self._ra
        self._ra ^= 1
        return i

    def mmg(self, bank, out, pairs, reads, start=True, last=True, extra_w=()):
        S = self.S
        n = len(pairs)
        w = [self.psb[bank]] + list(extra_w)
        for i, (lt, rh) in enumerate(pairs):
            st_ = start and i == 0
            sp_ = last and i == n - 1
            sig = (i == n - 1)
            rr = reads if (i == 0 or i == n - 1) else ()
            ww = w if (i == 0 or i == n - 1) else ()
            S.op("pe", (lambda e, o=out, a=lt, b=rh, s=st_, p=sp_: e.matmul(o, a, b, start=s, stop=p)),
                 reads=rr, writes=ww, signal=sig)

    def act(self, out, in_, func, reads, writes, bias=0.0, scale=1.0):
        self.S.op("act", lambda e: e.activation(out, in_, func, bias=bias, scale=scale), reads=reads, writes=writes)

    def tt(self, eng, out, a, b, op, reads, writes):
        self.S.op(eng, lambda e: e.tensor_tensor(out, a, b, op), reads=reads, writes=writes)

    def ts(self, eng, out, a, s1, s2, op0, op1, reads, writes):
        self.S.op(eng, lambda e: e.tensor_scalar(out, a, s1, s2, op0, op1), reads=reads, writes=writes)

    def stt(self, out, a, sc, b, op0, op1, reads, writes):
        self.S.op("dve", lambda e: e.scalar_tensor_tensor(out, a, sc, b, op0, op1), reads=reads, writes=writes)

    def cp(self, eng, out, in_, reads, writes):
        if eng == "act":
            self.S.op("act", lambda e: e.copy(out, in_), reads=reads, writes=writes)
        else:
            self.S.op(eng, lambda e: e.tensor_copy(out, in_), reads=reads, writes=writes)

    def load(self, tl, src, bi=0, queue=None, ap=None):
        if queue is None:
            queue = "sp"
        self.S.dma(queue, ap if ap is not None else tl.ap, src, tl.b[bi], writes=[tl.b[bi]])

    def pv(self, key, rows=128):
        o = PV_OFF[key]
        return lambda c=0, r0=0, r1=rows: self.cst["pvec"].ap[r0:r1, o + c:o + c + 1]

    def dump(self, name, tl_ap, bufs, shape):
        if name not in self.debug:
            return
        o = self.nc.dram_tensor("dbg_" + name, list(shape), F32, kind="ExternalOutput").ap()
        self.dbg_out.append("dbg_" + name)
        self.S.dma("sp", o, tl_ap, bufs[0], reads=bufs)

    def dump_xt(self, name):
        if name not in self.debug:
            return
        o = self.nc.dram_tensor("dbg_" + name, [DM, SEQ], F32, kind="ExternalOutput").ap()
        self.dbg_out.append("dbg_" + name)
        for q in range(NQ):
            self.S.dma("sp", o.rearrange("(c p) s -> p c s", p=128)[:, :, q * TQ:(q + 1) * TQ],
                       self.XT[:, :, q * TQ:(q + 1) * TQ], self.XTB[q][0], reads=self.XTB[q])

    def _program(self):
        d, S = self.d, self.S
        for nm, t in self.cst.items():
            if not nm.endswith("b"):
                self.load(t, d[nm])
        self.load(self.cst["ctabb"], d["ctab"], queue="pool")
        self.cp("dve", self.cst["identb"].ap, self.cst["ident"].ap, reads=[self.cst["ident"].b0], writes=[self.cst["identb"].b0])
        self.cp("dve", self.cst["trinegb"].ap, self.cst["trineg"].ap, reads=[self.cst["trineg"].b0], writes=[self.cst["trinegb"].b0])
        self.cp("dve", self.cst["onesb"].ap, self.cst["ones"].ap, reads=[self.cst["ones"].b0], writes=[self.cst["onesb"].b0])
        self.phase_load_x()
        self.dump_xt("x0")
        stop = self.debug.get("stop")
        for l in range(DEPTH):
            self.phase_ffn(l, 0)
            self.dump_xt("ffn%d0" % l)
            if stop == ("ffn", l, 0):
                break
            self.phase_mixer(l)
            self.dump_xt("mix%d" % l)
            if stop == ("mix", l):
                break
            self.phase_ffn(l, 1)
            self.dump_xt("ffn%d1" % l)
        self.phase_store()

    def phase_load_x(self):
        d, S = self.d, self.S
        self.reset_arena()
        xin = self.carve("xin", [128, 2, DM], nb=2)
        ident = self.cst["ident"]
        for tt in range(16):
            bi = tt % 2
            self.load(xin, d["x"][tt * 128:(tt + 1) * 128, :], bi=bi, ap=xin.ap[:, bi, :])
            q = tt // 4
            for half in range(2):
                bk = self.bank()
                for j in range(4):
                    dc = half * 4 + j
                    S.op("pe", (lambda e, o=self.ps[bk][:, j * 128:(j + 1) * 128], i=xin.ap[:, bi, dc * 128:(dc + 1) * 128]:
                                e.transpose(o, i, ident.ap)),
                         reads=[xin.b[bi], ident.b0], writes=[self.psb[bk]], signal=(j == 3))
                eng = "act" if half == 0 else "dve"
                self.cp(eng, self.XT[:, half * 4:(half + 1) * 4, tt * 128:(tt + 1) * 128],
                        self.ps[bk][:, :].rearrange("p (a b) -> p a b", a=4),
                        reads=[self.psb[bk]], writes=self.XTB[q][half * 4:(half + 1) * 4])

    def phase_store(self):
        d, S = self.d, self.S
        self.reset_arena()
        xo = self.carve("xout", [128, 2, DM], nb=2)
        ident = self.cst["ident"]
        for tt in range(16):
            bi = tt % 2
            q = tt // 4
            for half in range(2):
                bk = self.bank()
                for j in range(4):
                    dc = half * 4 + j
                    S.op("pe", (lambda e, o=self.ps[bk][:, j * 128:(j + 1) * 128], i=self.XT[:, dc, tt * 128:(tt + 1) * 128]:
                                e.transpose(o, i, ident.ap)),
                         reads=[self.XTB[q][dc], ident.b0], writes=[self.psb[bk]], signal=(j == 3))
                eng = "act" if half == 0 else "dve"
                self.cp(eng, xo.ap[:, bi, half * 512:(half + 1) * 512], self.ps[bk][:, :],
                        reads=[self.psb[bk]], writes=[xo.b[bi]])
            S.dma("sp", d["out"][tt * 128:(tt + 1) * 128, :], xo.ap[:, bi, :], xo.b[bi], reads=[xo.b[bi]])

    def layernorm(self, q, gkey, bkey, eps, tmp):
        S = self.S
        ones = self.cst["ones"]
        sq, mean, var, rstd = tmp
        qs = slice(q * TQ, (q + 1) * TQ)
        XB = self.XTB[q]
        bs, bq = self.bank(), self.bank()
        onesb = self.cst["onesb"]
        for dc in range(KC):
            i = dc % 2
            self.act(sq.ap[:, i, 0:TQ], self.XT[:, dc, qs], AF.Square, reads=[XB[dc]], writes=[sq.b[i]])
            self.act(sq.ap[:, i, TQ:2 * TQ], self.XT[:, dc, qs], AF.Identity, reads=[XB[dc]], writes=[sq.b[i]])
            S.op("pe", (lambda e, o=self.ps[bq][:, :], r=sq.ap[:, i, 0:TQ], s=(dc == 0), p=(dc == KC - 1):
                        e.matmul(o, onesb.ap, r, start=s, stop=p)),
                 reads=[sq.b[i], onesb.b0], writes=[self.psb[bq]], signal=(dc == KC - 1))
            S.op("pe", (lambda e, o=self.ps[bs][:, :], r=sq.ap[:, i, TQ:2 * TQ], s=(dc == 0), p=(dc == KC - 1):
                        e.matmul(o, onesb.ap, r, start=s, stop=p)),
                 reads=[sq.b[i]], writes=[self.psb[bs]], signal=True)
        S.op("act", lambda e: e.mul(mean.ap, self.ps[bs][:, :], 1.0 / DM), reads=[self.psb[bs]], writes=[mean.b0])
        self.tt("dve", var.ap, mean.ap, mean.ap, ALU.mult, reads=[mean.b0], writes=[var.b0])
        self.stt(var.ap, self.ps[bq][:, :], 1.0 / DM, var.ap, ALU.mult, ALU.subtract,
                 reads=[self.psb[bq], var.b0], writes=[var.b0])
        self.act(rstd.ap, var.ap, AF.Ln, reads=[var.b0], writes=[rstd.b0], bias=eps, scale=1.0)
        self.act(rstd.ap, rstd.ap, AF.Exp, reads=[rstd.b0], writes=[rstd.b0], scale=-0.5)
        g, b = self.pv(gkey), self.pv(bkey)
        pvb = self.cst["pvec"].b0
        for dc in range(KC):
            x = self.XT[:, dc, qs]
            self.tt("dve", x, x, mean.ap, ALU.subtract, reads=[XB[dc], mean.b0], writes=[XB[dc]])
            self.tt("dve", x, x, rstd.ap, ALU.mult, reads=[XB[dc], rstd.b0], writes=[XB[dc]])
            self.act(x, x, AF.Identity, reads=[XB[dc], pvb], writes=[XB[dc]], bias=b(dc), scale=g(dc))

    def phase_ffn(self, l, i):
        d, S = self.d, self.S
        self.reset_arena(soft=(i == 1 or l == 0))
        NB13, NB2 = 4, 3
        HT = self.carve("HT", [128, FC, TQ], dt=BF16)
        w13t = [self.carve("w13_%d" % k, [128, 2, KC, 128], dt=BF16) for k in range(NB13)]
        w2t = [self.carve("w2_%d" % k, [128, FC, 128], dt=BF16) for k in range(NB2)]
        XBt = [self.carve("xb%d" % k, [128, KC, TQ], dt=BF16) for k in range(2)]
        sil = self.carve("sil", [128, 2, TQ], nb=2)
        sq = self.carve("sq", [128, 2, 2 * TQ], nb=2, dt=BF16)
        mean = self.carve("mean", [128, TQ])
        var = self.carve("var", [128, TQ])
        rstd = self.carve("rstd", [128, TQ])
        w1v = d["w1"][l, i].rearrange("(kc p) f -> p kc f", p=128)
        w3v = d["w3"][l, i].rearrange("(kc p) f -> p kc f", p=128)
        w2v = d["w2"][l, i].rearrange("(fc p) m -> p fc m", p=128)
        jobs13 = [(q, fc) for q in range(NQ) for fc in range(FC)]
        jobs2 = [(q, dc) for q in range(NQ) for dc in range(KC)]
        st13 = [0]
        st2 = [0]

        def issue13(upto):
            while st13[0] <= upto and st13[0] < len(jobs13):
                n = st13[0]
                st13[0] += 1
                q, fc = jobs13[n]
                t = w13t[n % NB13]
                S.dma("pool", t.ap[:, 0, :, :], w1v[:, :, fc * 128:(fc + 1) * 128], t.b0, writes=[t.b0])
                S.dma("pool", t.ap[:, 1, :, :], w3v[:, :, fc * 128:(fc + 1) * 128], t.b0, writes=[t.b0])

        def issue2(upto):
            while st2[0] <= upto and st2[0] < len(jobs2):
                n = st2[0]
                st2[0] += 1
                q, dc = jobs2[n]
                t = w2t[n % NB2]
                S.dma("pool", t.ap, w2v[:, :, dc * 128:(dc + 1) * 128], t.b0, writes=[t.b0])

        def mkxb(q):
            if q < NQ:
                xb = XBt[q % 2]
                self.cp("dve", xb.ap, self.XT[:, :, q * TQ:(q + 1) * TQ], reads=self.XTB[q], writes=[xb.b0])
        stream_lru = (i == 0) and not self.debug.get("no_lru_stream")
        if stream_lru:
            self.lru_setup(l)
        fill = []

        def filler(k):
            for _ in range(k):
                while fill:
                    try:
                        next(fill[0])
                        break
                    except StopIteration:
                        fill.pop(0)
                if not fill:
                    return
        issue13(NB13 - 2)
        issue2(0)
        mkxb(0)
        n13 = 0
        n2 = 0
        for q in range(NQ):
            qs = slice(q * TQ, (q + 1) * TQ)
            xb = XBt[q % 2]
            for fc in range(FC):
                issue13(n13 + NB13 - 1)
                t = w13t[n13 % NB13]
                n13 += 1
                ba, bb = self.bank(), self.bank()
                self.mmg(ba, self.ps[ba][:, :], [(t.ap[:, 0, kc, :], xb.ap[:, kc, :]) for kc in range(KC)],
                         reads=[t.b0, xb.b0])
                self.mmg(bb, self.ps[bb][:, :], [(t.ap[:, 1, kc, :], xb.ap[:, kc, :]) for kc in range(KC)],
                         reads=[t.b0, xb.b0])
                si = fc % 2
                self.act(sil.ap[:, si, :], self.ps[ba][:, :], AF.Silu, reads=[self.psb[ba]], writes=[sil.b[si]])
                self.tt("dve", HT.ap[:, fc, :], sil.ap[:, si, :], self.ps[bb][:, :], ALU.mult,
                        reads=[sil.b[si], self.psb[bb]], writes=[HT.b0])
                if fc == FC - 4:
                    issue2(n2 + NB2 - 1)
                filler(2)
            mkxb(q + 1)
            for dc in range(KC):
                issue2(n2 + NB2 - 1)
                t = w2t[n2 % NB2]
                n2 += 1
                bk = self.bank()
                self.mmg(bk, self.ps[bk][:, :], [(t.ap[:, fc, :], HT.ap[:, fc, :]) for fc in range(FC)],
                         reads=[t.b0, HT.b0])
                self.stt(self.XT[:, dc, qs], self.XT[:, dc, qs], 2.0 * ALPHA, self.ps[bk][:, :], ALU.mult, ALU.add,
                         reads=[self.XTB[q][dc], self.psb[bk]], writes=[self.XTB[q][dc]])
                if dc < KC - 2:
                    filler(5)
            self.layernorm(q, ("lng", l, 2 * i), ("lnb", l, 2 * i), 4.0e-5, (sq, mean, var, rstd))
            if stream_lru:
                fill.append(self.lru_tile(l, q))
        filler(100000)

    def proj_fm(self, wt, ncols, out_fn, evac="act", c0=0):
        XBF = self.XBF
        for q in range(NQ):
            qs = slice(q * TQ, (q + 1) * TQ)
            bk = self.bank()
            self.mmg(bk, self.ps[bk][0:ncols, :], [(wt.ap[:, kc, c0:c0 + ncols], XBF.ap[:, kc, qs]) for kc in range(KC)],
                     reads=[wt.b0, XBF.b[q]])
            oap, obufs = out_fn(q)
            self.cp(evac, oap, self.ps[bk][0:ncols, :], reads=[self.psb[bk]], writes=obufs)

    def rmsnorm_fm(self, X, Y, nch, rows_key, eps, nfeat, tmp):
        S = self.S
        ones = self.cst["onesb"]
        sq, rstd0 = tmp
        rot = [rstd0, self.ND, self.RD]
        g = self.pv(rows_key)
        pvb = self.cst["pvec"].b0
        for q in range(NQ):
            qs = slice(q * TQ, (q + 1) * TQ)
            rstd = rot[q % 3]
            bk = self.bank()
            for c in range(nch):
                i = c % 2
                src = X.ap[:, c, qs] if nch > 1 else X.ap[:, qs]
                self.act(sq.ap[:, i, :], src, AF.Square, reads=[X.b[q]], writes=[sq.b[i]])
                S.op("pe", (lambda e, o=self.ps[bk][:, :], r=sq.ap[:, i, :], s=(c == 0), p=(c == nch - 1):
                            e.matmul(o, ones.ap, r, start=s, stop=p)),
                     reads=[sq.b[i], ones.b0], writes=[self.psb[bk]], signal=True)
            self.act(rstd.ap, self.ps[bk][:, :], AF.Ln, reads=[self.psb[bk]], writes=[rstd.b0], bias=eps, scale=1.0 / nfeat)
            self.act(rstd.ap, rstd.ap, AF.Exp, reads=[rstd.b0], writes=[rstd.b0], scale=-0.5)
            for c in range(nch):
                src = X.ap[:, c, qs] if nch > 1 else X.ap[:, qs]
                dst = Y.ap[:, c, qs] if nch > 1 else Y.ap[:, qs]
                self.stt(dst, src, g(c), rstd.ap, ALU.mult, ALU.mult, reads=[X.b[q], rstd.b0, pvb], writes=[Y.b[q]])

    def attn_head(self, kfn, qfn, kdim, vfn, mode, scale, fin_fn, aux=None):
        S = self.S
        ident, trineg, tri01, ctab = (self.cst[k] for k in ("identb", "trinegb", "tri01", "ctabb"))
        PT = self.PT
        NPT = len(PT.b)
        LOOK = NPT - 2
        blocks = [(q, j) for q in range(NQ) for j in range(4 * q + 4)]
        accs = {}
        pts = {}
        pending = []

        def stage_a(n):
            q, j = blocks[n]
            if j == 0:
                accs[q] = self.accbank()
            r = j - 4 * q
            c0 = 128 * max(r, 0)
            lo, hi = q * TQ + c0, (q + 1) * TQ
            bk = self.bank()
            kap, kb = kfn(j)
            qap, qb = qfn(lo, hi)
            pso = self.ps[bk][:, c0:TQ]
            diag = (r >= 0)
            pi = self._pti
            self._pti = (self._pti + 1) % NPT
            pt = PT.ap[:, pi, c0:TQ]
            ptb = PT.b[pi]
            pts[n] = (pt, ptb, c0)
            if mode == "mla":
                self.mmg(bk, pso, [(kap, qap)], reads=kb + qb, last=not diag)
                if diag:
                    S.op("pe", (lambda e, o=self.ps[bk][:, c0:c0 + 128]: e.matmul(o, ident.ap, trineg.ap, start=False, stop=True)),
                         reads=[ident.b0, trineg.b0], writes=[self.psb[bk]], signal=True)
                self.act(pt, pso, AF.Exp, reads=[self.psb[bk]], writes=[ptb], scale=scale)
            elif mode == "dil":
                self.mmg(bk, pso, [(kap, qap)], reads=kb + qb)
                self.act(pt, pso, AF.Exp, reads=[self.psb[bk]], writes=[ptb], scale=scale)
                off = q * TQ - 128 * j
                self.tt("dve", pt, pt, ctab.ap[:, off + c0:off + TQ], ALU.mult, reads=[ptb, ctab.b0], writes=[ptb])
            else:
                csrow, biasfn = aux
                self.mmg(bk, pso, [(kap, qap)], reads=kb + qb)
                ei = self._ei
                self._ei = (self._ei + 1) % len(self.ET.b)
                et = self.ET.ap[:, ei, c0:TQ]
                etb = self.ET.b[ei]
                bap, bb = biasfn(j)
                self.act(et, csrow.ap[:, lo:hi], AF.Exp, reads=[csrow.b[q]] + bb, writes=[etb], bias=bap, scale=-1.0)
                if diag:
                    self.tt("pool", self.ET.ap[:, ei, c0:c0 + 128], self.ET.ap[:, ei, c0:c0 + 128], tri01.ap, ALU.mult,
                            reads=[etb, tri01.b0], writes=[etb])
                self.stt(pt, pso, scale, et, ALU.mult, ALU.mult, reads=[self.psb[bk], etb], writes=[ptb])

        def stage_b(n):
            q, j = blocks[n]
            nj = 4 * q + 4
            acc = accs[q]
            pt, ptb, c0 = pts.pop(n)
            vap, vb = vfn(j)
            S.op("pe", (lambda e, o=self.ps[acc][:, c0:TQ], a=vap, b=pt, s=(j == 0), p=(j == nj - 1):
                        e.matmul(o, a, b, start=s, stop=p)),
                 reads=vb + [ptb], writes=[self.psb[acc]], signal=True)
            if j == nj - 1:
                pending.append((n + 2, q, acc))
            while pending and pending[0][0] <= n:
                _, q_, acc_ = pending.pop(0)
                fin_fn(q_, acc_)
        nb = len(blocks)
        for n in range(min(LOOK, nb)):
            stage_a(n)
        for n in range(nb):
            if n + LOOK < nb:
                stage_a(n + LOOK)
            stage_b(n)
            if self.attn_filler is not None:
                self.attn_filler()
        for _, q_, acc_ in pending:
            fin_fn(q_, acc_)

    def finish_softmax(self, q, acc, out_ap, out_bufs, mlstm=False, gate=None, cpeng="act"):
        T, RD = self.ND, self.RD
        pa = self.psb[acc]
        if not mlstm:
            self.act(T.ap[0:64, :], self.ps[acc][64:128, :], AF.Ln, reads=[pa], writes=[T.b0])
            self.act(RD.ap[0:64, :], T.ap[0:64, :], AF.Exp, reads=[T.b0], writes=[RD.b0], scale=-1.0)
        else:
            self.act(T.ap[0:64, :], self.ps[acc][64:128, :], AF.Identity, reads=[pa], writes=[T.b0])
            self.tt("dve", T.ap[0:64, :], T.ap[0:64, :], T.ap[0:64, :], ALU.mult, reads=[T.b0], writes=[T.b0])
            self.ts("dve", T.ap[0:64, :], T.ap[0:64, :], 1.0, 0.0, ALU.max, ALU.add, reads=[T.b0], writes=[T.b0])
            self.act(T.ap[0:64, :], T.ap[0:64, :], AF.Ln, reads=[T.b0], writes=[T.b0])
            self.act(RD.ap[0:64, :], T.ap[0:64, :], AF.Exp, reads=[T.b0], writes=[RD.b0], scale=-0.5)
        if gate is not None:
            gap, gb = gate
            self.tt("pool", RD.ap[0:64, :], RD.ap[0:64, :], gap, ALU.mult, reads=[RD.b0] + gb, writes=[RD.b0])
        self.tt("dve", out_ap, self.ps[acc][0:64, :], RD.ap[0:64, :], ALU.mult, reads=[pa, RD.b0], writes=out_bufs)

    def group_finish(self, l, g, YG):
        S = self.S
        ones = self.cst["onesb"]
        sq = self.gsq
        rot = [self.grstd, self.ND, self.RD]
        on = self.pv(("on", l))
        pvb = self.cst["pvec"].b0
        for q in range(NQ):
            qs = slice(q * TQ, (q + 1) * TQ)
            rstd = rot[q % 3]
            bk = self.bank()
            for c in range(2):
                self.act(sq.ap[:, c, :], YG.ap[:, c, qs], AF.Square, reads=[YG.b[q]], writes=[sq.b[c]])
                S.op("pe", (lambda e, o=self.ps[bk][:, :], r=sq.ap[:, c, :], s=(c == 0), p=(c == 1):
                            e.matmul(o, ones.ap, r, start=s, stop=p)),
                     reads=[sq.b[c], ones.b0], writes=[self.psb[bk]], signal=True)
            self.act(rstd.ap, self.ps[bk][:, :], AF.Ln, reads=[self.psb[bk]], writes=[rstd.b0], bias=1e-6, scale=1.0 / 256)
            self.act(rstd.ap, rstd.ap, AF.Exp, reads=[rstd.b0], writes=[rstd.b0], scale=-0.5)
            for c in range(2):
                self.stt(YG.ap[:, c, qs], YG.ap[:, c, qs], on(g * 2 + c), rstd.ap, ALU.mult, ALU.mult,
                         reads=[YG.b[q], rstd.b0, pvb], writes=[YG.b[q]])
            S.dma("sp", self.d["yscr"].rearrange("(c p) s -> p c s", p=128)[:, g * 2:g * 2 + 2, qs],
                  YG.ap[:, :, qs], YG.b[q], reads=[YG.b[q]], writes=[self.YSB[q]])
        if ("yg%d%d" % (l, g)) in self.debug:
            self.dump("yg%d%d" % (l, g), YG.ap, YG.b, [128, 2, SEQ])

    def phase_mixer(self, l):
        self.reset_arena(soft=True)
        self.XBF = self.carve("XBF", [128, KC, SEQ], nb=NQ, dt=BF16)
        for q in range(NQ):
            self.cp("dve" if q % 2 else "act", self.XBF.ap[:, :, q * TQ:(q + 1) * TQ], self.XT[:, :, q * TQ:(q + 1) * TQ],
                    reads=self.XTB[q], writes=[self.XBF.b[q]])
        self.PT = self.carve("PT", [128, 5, TQ], nb=5, dt=BF16)
        self._pti = 0
        self.ET = self.carve("ET", [128, 4, TQ], nb=4)
        self._ei = 0
        self.ND = self.carve("ND", [128, TQ])
        self.RD = self.carve("RD", [128, TQ])
        self.gsq = self.carve("gsq", [128, 2, TQ], nb=2, dt=BF16)
        self.grstd = self.carve("grstd", [128, TQ])
        base = self.aoff
        for g, fn in enumerate((self.mixer_mla, self.mixer_lru, self.mixer_dil, self.mixer_mlstm)):
            if "only_g" in self.debug and g not in self.debug["only_g"]:
                continue
            if g == 1 and not self.debug.get("no_lru_stream"):
                continue
            self.rewind(base, soft=True)
            fn(l)
        self.phase_outproj(l)

    def win(self, l, c0, n):
        return self.d["win"][l].rearrange("(kc p) f -> p kc f", p=128)[:, :, c0:c0 + n]

    def mixer_mla(self, l):
        d, S = self.d, self.S
        CQ = self.carve("CQ", [128, 2, SEQ], nb=NQ, dt=BF16)
        CKV = self.carve("CKV", [128, SEQ], nb=NQ, dt=BF16)
        KRR = self.carve("KRR", [128, SEQ], nb=NQ, dt=BF16)
        RY = self.carve("RY", [128, 2, SEQ])
        ry0 = self.aoff - 2 * SEQ
        ROPE = self.track(Tl(RY.ap, "rope"), ry0, self.aoff)
        YH = [self.track(Tl(RY.ap[:, k, :], "YHa%d" % k, NQ), ry0, self.aoff) for k in range(2)]
        tmp = self.carve("tmpr", [128, TQ])
        tmp2 = self.carve("tmpr2", [128, TQ])
        base2 = self.aoff
        Wa = self.carve("Wa", [128, KC, 256], dt=BF16)
        Wb = self.carve("Wb", [128, KC, 160], dt=BF16)
        Wsw = self.carve("Wsw", [128, KC, 96], dt=BF16)
        CQr = self.carve("CQr", [128, 2, SEQ], nb=NQ)
        CKVr = self.carve("CKVr", [128, SEQ], nb=NQ)
        self.load(ROPE, d["rope"].rearrange("t r s -> r t s"), ap=ROPE.ap[64:96, :, :])
        self.load(Wa, self.win(l, O_CQ, 256), queue="pool")
        self.load(Wb, self.win(l, O_CKV, 160), queue="pool")
        self.load(Wsw, self.win(l, 320, 64), ap=Wsw.ap[:, :, 0:64], queue="pool")
        self.load(Wsw, self.win(l, O_KR + 16, 16), ap=Wsw.ap[:, :, 64:80], queue="pool")
        self.load(Wsw, self.win(l, O_KR, 16), ap=Wsw.ap[:, :, 80:96], queue="pool")
        for m in range(2):
            self.proj_fm(Wa, 128, lambda q, m=m: (CQr.ap[:, m, q * TQ:(q + 1) * TQ], [CQr.b[q]]), c0=m * 128,
                         evac="act" if m == 0 else "dve")
        self.proj_fm(Wb, 128, lambda q: (CKVr.ap[:, q * TQ:(q + 1) * TQ], [CKVr.b[q]]), evac="dve")
        self.rmsnorm_fm(CQr, CQ, 2, ("qn", l), 1e-6, 256, (self.gsq, self.grstd))
        self.rmsnorm_fm(CKVr, CKV, 1, ("kvn", l), 1e-6, 128, (self.gsq, self.grstd))
        self.dump("cq%d" % l, CQ.ap, CQ.b, [128, 2, SEQ])
        self.dump("ckv%d" % l, CKV.ap, CKV.b, [128, SEQ])
        for q in range(NQ):
            qs = slice(q * TQ, (q + 1) * TQ)
            ba, bb = self.bank(), self.bank()
            self.mmg(ba, self.ps[ba][0:96, :], [(Wb.ap[:, kc, 64:160], self.XBF.ap[:, kc, qs]) for kc in range(KC)],
                     reads=[Wb.b0, self.XBF.b[q]])
            self.mmg(bb, self.ps[bb][0:96, :], [(Wsw.ap[:, kc, :], self.XBF.ap[:, kc, qs]) for kc in range(KC)],
                     reads=[Wsw.b0, self.XBF.b[q]])
            self.tt("dve", tmp2.ap[64:96, :], self.ps[ba][64:96, :], ROPE.ap[64:96, 0, qs], ALU.mult,
                    reads=[self.psb[ba], ROPE.b0], writes=[tmp2.b0])
            self.tt("dve", tmp.ap[64:96, :], self.ps[bb][64:96, :], ROPE.ap[64:96, 1, qs], ALU.mult,
                    reads=[self.psb[bb], ROPE.b0], writes=[tmp.b0])
            self.tt("pool", KRR.ap[64:96, qs], tmp2.ap[64:96, :], tmp.ap[64:96, :], ALU.add,
                    reads=[tmp2.b0, tmp.b0], writes=[KRR.b[q]])
        self.dump("krr%d" % l, KRR.ap, KRR.b, [128, SEQ])
        self.rewind(base2, soft=True)
        YG = self.carve("YGa", [128, 2, SEQ], nb=NQ)
        QH = [self.carve("QH%d" % k, [128, SEQ], nb=NQ, dt=BF16) for k in range(2)]
        KH = [self.carve("KH%d" % k, [128, SEQ], nb=NQ, dt=BF16) for k in range(2)]
        VH = [self.carve("VH%d" % k, [128, 16, 128], dt=BF16) for k in range(2)]
        wuq = [self.carve("wuq%d" % k, [128, 2, 96], dt=BF16) for k in range(2)]
        wuqs = [self.carve("wuqs%d" % k, [128, 2, 96], dt=BF16) for k in range(2)]
        wukv = [self.carve("wukv%d" % k, [128, 128], dt=BF16) for k in range(2)]
        wuqv = d["wuq"][l].rearrange("(kc p) f -> p kc f", p=128)
        def head_proj(h):
            k = h % 2
            self.load(wuq[k], wuqv[:, :, h * 96:(h + 1) * 96], queue="pool")
            self.load(wuqs[k], wuqv[:, :, h * 96:h * 96 + 64], ap=wuqs[k].ap[:, :, 0:64], queue="pool")
            self.load(wuqs[k], wuqv[:, :, h * 96 + 80:h * 96 + 96], ap=wuqs[k].ap[:, :, 64:80], queue="pool")
            self.load(wuqs[k], wuqv[:, :, h * 96 + 64:h * 96 + 80], ap=wuqs[k].ap[:, :, 80:96], queue="pool")
            self.load(wukv[k], d["wukv"][l][:, h * 128:(h + 1) * 128], queue="pool")
            qh, kh, vh, yh = QH[k], KH[k], VH[k], YH[k]
            S.op("pool", lambda e, v=vh: e.memset(v.ap[:, :, 64:128], 1.0), writes=[vh.b0])
            yield
            yield
            yield
            for q in range(NQ):
                qs = slice(q * TQ, (q + 1) * TQ)
                ba, bb, bc = self.bank(), self.bank(), self.bank()
                self.mmg(ba, self.ps[ba][0:96, :], [(wuq[k].ap[:, c, :], CQ.ap[:, c, qs]) for c in range(2)],
                         reads=[wuq[k].b0, CQ.b[q]])
                self.mmg(bb, self.ps[bb][0:96, :], [(wuqs[k].ap[:, c, :], CQ.ap[:, c, qs]) for c in range(2)],
                         reads=[wuqs[k].b0, CQ.b[q]])
                self.cp("dve", qh.ap[0:64, qs], self.ps[ba][0:64, :], reads=[self.psb[ba]], writes=[qh.b[q]])
                self.tt("dve", tmp2.ap[64:96, :], self.ps[ba][64:96, :], ROPE.ap[64:96, 0, qs], ALU.mult,
                        reads=[self.psb[ba], ROPE.b0], writes=[tmp2.b0])
                self.tt("dve", tmp.ap[64:96, :], self.ps[bb][64:96, :], ROPE.ap[64:96, 1, qs], ALU.mult,
                        reads=[self.psb[bb], ROPE.b0], writes=[tmp.b0])
                self.tt("pool", qh.ap[64:96, qs], tmp2.ap[64:96, :], tmp.ap[64:96, :], ALU.add,
                        reads=[tmp2.b0, tmp.b0], writes=[qh.b[q]])
                yield
                self.mmg(bc, self.ps[bc][:, :], [(wukv[k].ap[:, :], CKV.ap[:, qs])], reads=[wukv[k].b0, CKV.b[q]])
                self.cp("dve", kh.ap[0:64, qs], self.ps[bc][0:64, :], reads=[self.psb[bc]], writes=[kh.b[q]])
                self.cp("pool", kh.ap[64:96, qs], KRR.ap[64:96, qs], reads=[KRR.b[q]], writes=[kh.b[q]])
                yield
            for half in range(2):
                bk = self.bank()
                for t8 in range(8):
                    tc = half * 8 + t8
                    S.op("pe", (lambda e, o=self.ps[bk][:, t8 * 64:(t8 + 1) * 64], a=CKV.ap[:, tc * 128:(tc + 1) * 128], b=wukv[k].ap[:, 64:128]:
                                e.matmul(o, a, b, start=True, stop=True)),
                         reads=[CKV.b[tc // 4], wukv[k].b0], writes=[self.psb[bk]], signal=(t8 == 7))
                self.cp("dve", vh.ap[:, half * 8:(half + 1) * 8, 0:64], self.ps[bk][:, :].rearrange("p (a b) -> p a b", a=8),
                        reads=[self.psb[bk]], writes=[vh.b0])
                yield
            if h == 0:
                self.dump("qh%d" % l, qh.ap, qh.b, [128, SEQ])
                self.dump("kh%d" % l, kh.ap, kh.b, [128, SEQ])
                pass


        def head_attn(h):
            k = h % 2
            qh, kh, vh, yh = QH[k], KH[k], VH[k], YH[k]
            def fin(q, acc, h=h, yh=yh):
                qs = slice(q * TQ, (q + 1) * TQ)
                if h % 2 == 0:
                    self.finish_softmax(q, acc, YG.ap[0:64, h // 2, qs], [YG.b[q]], cpeng="dve")
                else:
                    self.finish_softmax(q, acc, yh.ap[0:64, qs], [yh.b[q]], cpeng="dve")
                    S.dma("sp", YG.ap[64:128, h // 2, qs], yh.ap[0:64, qs], yh.b[q], reads=[yh.b[q]], writes=[YG.b[q]])
            self.attn_head(lambda j, kh=kh: (kh.ap[0:96, j * 128:(j + 1) * 128], [kh.b[j // 4]]),
                           lambda lo, hi, qh=qh: (qh.ap[0:96, lo:hi], [qh.b[lo // TQ]]), 96,
                           lambda j, vh=vh: (vh.ap[:, j, :], [vh.b0]), "mla", 96.0 ** -0.5, fin)

        for _ in head_proj(0):
            pass
        for h in range(4):
            gen = head_proj(h + 1) if h + 1 < 4 else iter(())
            self.attn_filler = lambda gen=gen: next(gen, None)
            head_attn(h)
            self.attn_filler = None
            for _ in gen:
                pass
        self.group_finish(l, 0, YG)

    def lru_setup(self, l):
        d, S = self.d, self.S
        L = {}
        L["XL"] = self.carve("XL", [128, KC, TQ], dt=BF16)
        W = L["W"] = self.carve("Wl", [128, KC, 512], dt=BF16)
        L["wbd"] = [[self.carve("wbd%d%d" % (c, k), [128, 128]) for k in range(2)] for c in range(2)]
        sp = L["sp"] = self.carve("spl", [128, 8])
        L["XB"] = [self.carve("XBl%d" % c, [128, TQ + 4]) for c in range(2)]
        L["H"] = self.carve("Hl", [128, 2])
        L["hb"] = self.carve("hbl", [128, 4])
        for nm in ("GT", "XC", "A", "I", "M", "Qp", "Tg"):
            L[nm] = self.carve(nm + "l", [128, TQ])
        L["YGt"] = self.carve("YGt", [128, 2, TQ])
        L["sq"] = self.carve("sql", [128, 2, TQ], nb=2, dt=BF16)
        L["rstd"] = self.carve("rstdl", [128, TQ])
        pvb = self.cst["pvec"].b0
        lam = self.pv(("lam", l))
        for c in range(2):
            self.load(W, self.win(l, O_LX + c * 128, 128), ap=W.ap[:, :, c * 128:(c + 1) * 128], queue="pool")
            self.load(W, self.win(l, O_LG + c * 128, 128), ap=W.ap[:, :, 256 + c * 128:256 + (c + 1) * 128], queue="pool")
            for k, wsrc in enumerate((d["wa"], d["wx"])):
                t = L["wbd"][c][k]
                S.op("dve", lambda e, t=t: e.memset(t.ap, 0.0), writes=[t.b0])
                self.load(t, wsrc[l, 2 * c], ap=t.ap[0:64, 0:64])
                self.load(t, wsrc[l, 2 * c + 1], ap=t.ap[64:128, 64:128])
            x_, t_, nsp = sp.ap[:, 4 * c:4 * c + 1], sp.ap[:, 4 * c + 1:4 * c + 2], sp.ap[:, 4 * c + 2:4 * c + 3]
            self.act(x_, lam(c), AF.Exp, reads=[pvb], writes=[sp.b0], scale=-1.0)
            self.ts("dve", t_, x_, -0.25, 1.0 / 3.0, ALU.mult, ALU.add, reads=[sp.b0], writes=[sp.b0])
            self.tt("dve", t_, t_, x_, ALU.mult, reads=[sp.b0], writes=[sp.b0])
            self.ts("dve", t_, t_, -1.0, 0.5, ALU.mult, ALU.add, reads=[sp.b0], writes=[sp.b0])
            self.tt("dve", t_, t_, x_, ALU.mult, reads=[sp.b0], writes=[sp.b0])
            self.ts("dve", t_, t_, -1.0, 1.0, ALU.mult, ALU.add, reads=[sp.b0], writes=[sp.b0])
            self.tt("dve", t_, t_, x_, ALU.mult, reads=[sp.b0], writes=[sp.b0])
            self.ts("dve", nsp, t_, -8.0, 0.0, ALU.mult, ALU.add, reads=[sp.b0], writes=[sp.b0])
            self.ts("dve", sp.ap[:, 4 * c + 3:4 * c + 4], t_, -4.0, 0.0, ALU.mult, ALU.add, reads=[sp.b0], writes=[sp.b0])
            hb = L["hb"]
            self.ts("dve", hb.ap[:, 2 * c:2 * c + 1], self.pv(("ba", l))(c), 0.5, 0.0, ALU.mult, ALU.add, reads=[pvb], writes=[hb.b0])
            self.ts("dve", hb.ap[:, 2 * c + 1:2 * c + 2], self.pv(("bx", l))(c), 0.5, 0.0, ALU.mult, ALU.add, reads=[pvb], writes=[hb.b0])
            xb = L["XB"][c]
            S.op("dve", lambda e, xb=xb: e.memset(xb.ap[:, 0:3], 0.0), writes=[xb.b0])
        self.L = L

    def lru_tile(self, l, q):
        S, L = self.S, self.L
        qs = slice(q * TQ, (q + 1) * TQ)
        pvb = self.cst["pvec"].b0
        ones = self.cst["onesb"]
        cw, cb, ba_, bx_ = (self.pv((k, l)) for k in ("cw", "cb", "ba", "bx"))
        on = self.pv(("on", l))
        XL, W, sp, H = L["XL"], L["W"], L["sp"], L["H"]
        GT, XC, A, I, M, Qp, Tg, YGt, sq, rstd = (L[k] for k in ("GT", "XC", "A", "I", "M", "Qp", "Tg", "YGt", "sq", "rstd"))
        self.cp("dve", XL.ap, self.XT[:, :, qs], reads=self.XTB[q], writes=[XL.b0])
        yield
        for c in range(2):
            XB = L["XB"][c]
            nsp = sp.ap[:, 4 * c + 2:4 * c + 3]
            bk = self.bank()
            self.mmg(bk, self.ps[bk][:, :], [(W.ap[:, kc, c * 128:(c + 1) * 128], XL.ap[:, kc, :]) for kc in range(KC)],
                     reads=[W.b0, XL.b0])
            self.cp("act", XB.ap[:, 3:3 + TQ], self.ps[bk][:, :], reads=[self.psb[bk]], writes=[XB.b0])
            yield
            bk = self.bank()
            self.mmg(bk, self.ps[bk][:, :], [(W.ap[:, kc, 256 + c * 128:256 + (c + 1) * 128], XL.ap[:, kc, :]) for kc in range(KC)],
                     reads=[W.b0, XL.b0])
            self.cp("dve", GT.ap, self.ps[bk][:, :], reads=[self.psb[bk]], writes=[GT.b0])
            yield
            self.ts("dve", XC.ap, XB.ap[:, 0:TQ], cw(0 * 2 + c), cb(c), ALU.mult, ALU.add, reads=[XB.b0, pvb], writes=[XC.b0])
            self.tt("dve", Tg.ap, GT.ap, GT.ap, ALU.mult, reads=[GT.b0], writes=[Tg.b0])
            yield
            self.stt(XC.ap, XB.ap[:, 1:1 + TQ], cw(1 * 2 + c), XC.ap, ALU.mult, ALU.add, reads=[XB.b0, XC.b0, pvb], writes=[XC.b0])
            self.ts("dve", Tg.ap, Tg.ap, 0.044715, 1.0, ALU.mult, ALU.add, reads=[Tg.b0], writes=[Tg.b0])
            yield
            self.stt(XC.ap, XB.ap[:, 2:2 + TQ], cw(2 * 2 + c), XC.ap, ALU.mult, ALU.add, reads=[XB.b0, XC.b0, pvb], writes=[XC.b0])
            self.tt("dve", Tg.ap, Tg.ap, GT.ap, ALU.mult, reads=[Tg.b0, GT.b0], writes=[Tg.b0])
            yield
            self.stt(XC.ap, XB.ap[:, 3:3 + TQ], cw(3 * 2 + c), XC.ap, ALU.mult, ALU.add, reads=[XB.b0, XC.b0, pvb], writes=[XC.b0])
            self.act(Tg.ap, Tg.ap, AF.Tanh, reads=[Tg.b0], writes=[Tg.b0], scale=0.7978845608028654)
            yield
            self.cp("dve", XB.ap[:, 0:3], XB.ap[:, TQ:TQ + 3], reads=[XB.b0], writes=[XB.b0])
            yield
            wbd = L["wbd"][c]
            b1, b2 = self.bank(), self.bank()
            self.mmg(b1, self.ps[b1][:, :], [(wbd[0].ap, XC.ap)], reads=[wbd[0].b0, XC.b0])
            self.mmg(b2, self.ps[b2][:, :], [(wbd[1].ap, XC.ap)], reads=[wbd[1].b0, XC.b0])
            hb = L["hb"]
            self.act(A.ap, self.ps[b1][:, :], AF.Tanh, reads=[self.psb[b1], hb.b0], writes=[A.b0], bias=hb.ap[:, 2 * c:2 * c + 1], scale=0.5)
            self.act(I.ap, self.ps[b2][:, :], AF.Tanh, reads=[self.psb[b2], hb.b0], writes=[I.b0], bias=hb.ap[:, 2 * c + 1:2 * c + 2], scale=0.5)
            yield
            self.stt(Tg.ap, Tg.ap, 1.0, GT.ap, ALU.add, ALU.mult, reads=[Tg.b0, GT.b0], writes=[Tg.b0])
            yield
            hnsp = sp.ap[:, 4 * c + 3:4 * c + 4]
            self.ts("dve", A.ap, A.ap, hnsp, hnsp, ALU.mult, ALU.add, reads=[A.b0, sp.b0], writes=[A.b0])
            yield
            self.ts("dve", Qp.ap, A.ap, -1.0 / 120.0, 0.0, ALU.mult, ALU.add, reads=[A.b0], writes=[Qp.b0])
            yield
            for cc in (-1.0 / 24.0, -1.0 / 6.0, -0.5, -1.0):
                self.stt(Qp.ap, Qp.ap, cc, A.ap, ALU.add, ALU.mult, reads=[Qp.b0, A.b0], writes=[Qp.b0])
                yield
            self.ts("dve", A.ap, Qp.ap, -1.0, 1.0, ALU.mult, ALU.add, reads=[Qp.b0], writes=[A.b0])
            yield
            self.stt(M.ap, Qp.ap, 2.0, Qp.ap, ALU.subtract, ALU.mult, reads=[Qp.b0], writes=[M.b0])
            yield
            self.act(M.ap, M.ap, AF.Sqrt, reads=[M.b0], writes=[M.b0], scale=-0.25)
            yield
            self.stt(I.ap, I.ap, 1.0, XC.ap, ALU.add, ALU.mult, reads=[I.b0, XC.b0], writes=[I.b0])
            yield
            self.tt("dve", I.ap, I.ap, M.ap, ALU.mult, reads=[I.b0, M.b0], writes=[I.b0])
            yield
            init = 0.0 if q == 0 else H.ap[:, c:c + 1]
            S.op("dve", lambda e, init=init: e.tensor_tensor_scan(M.ap, A.ap, I.ap, init, ALU.mult, ALU.add),
                 reads=[A.b0, I.b0, H.b0], writes=[M.b0])
            yield
            self.cp("dve", H.ap[:, c:c + 1], M.ap[:, TQ - 1:TQ], reads=[M.b0], writes=[H.b0])
            self.stt(YGt.ap[:, c, :], M.ap, 0.5, Tg.ap, ALU.mult, ALU.mult, reads=[M.b0, Tg.b0], writes=[YGt.b0])
            yield
        bk = self.bank()
        for c in range(2):
            self.act(sq.ap[:, c, :], YGt.ap[:, c, :], AF.Square, reads=[YGt.b0], writes=[sq.b[c]])
            S.op("pe", (lambda e, o=self.ps[bk][:, :], r=sq.ap[:, c, :], s=(c == 0), p=(c == 1):
                        e.matmul(o, ones.ap, r, start=s, stop=p)),
                 reads=[sq.b[c], ones.b0], writes=[self.psb[bk]], signal=True)
            yield
        self.act(rstd.ap, self.ps[bk][:, :], AF.Ln, reads=[self.psb[bk]], writes=[rstd.b0], bias=1e-6, scale=1.0 / 256)
        yield
        self.act(rstd.ap, rstd.ap, AF.Exp, reads=[rstd.b0], writes=[rstd.b0], scale=-0.5)
        yield
        for c in range(2):
            self.stt(YGt.ap[:, c, :], YGt.ap[:, c, :], on(2 + c), rstd.ap, ALU.mult, ALU.mult,
                     reads=[YGt.b0, rstd.b0, pvb], writes=[YGt.b0])
            yield
        S.dma("sp", self.d["yscr"].rearrange("(c p) s -> p c s", p=128)[:, 2:4, qs], YGt.ap, YGt.b0,
              reads=[YGt.b0], writes=[self.YSB[q]])
        yield

    def mixer_lru(self, l):
        d, S = self.d, self.S
        YG = self.carve("YGb", [128, 2, SEQ], nb=NQ)
        Wx = self.carve("Wlx", [128, KC, 128], dt=BF16)
        Wg = self.carve("Wlg", [128, KC, 128], dt=BF16)
        wbd = [self.carve("wbd%d" % k, [128, 128]) for k in range(2)]
        XB = self.carve("XB", [128, SEQ + 4], nb=NQ)
        GT = self.carve("GT", [128, SEQ], nb=NQ)
        XC = self.carve("XC", [128, SEQ])
        A = self.carve("A", [128, SEQ])
        I = self.carve("I", [128, SEQ])
        M = self.carve("M", [128, SEQ])
        Qp = self.carve("Qp", [128, SEQ])
        sp = self.carve("sp", [128, 8])
        pvb = self.cst["pvec"].b0
        cw, cb, ba_, bx_, lam = (self.pv((k, l)) for k in ("cw", "cb", "ba", "bx", "lam"))
        for c in range(2):
            self.load(Wx, self.win(l, O_LX + c * 128, 128), queue="pool")
            self.load(Wg, self.win(l, O_LG + c * 128, 128), queue="pool")
            for k, wsrc in enumerate((d["wa"], d["wx"])):
                S.op("pool", lambda e, t=wbd[k]: e.memset(t.ap, 0.0), writes=[wbd[k].b0])
                self.load(wbd[k], wsrc[l, 2 * c], ap=wbd[k].ap[0:64, 0:64])
                self.load(wbd[k], wsrc[l, 2 * c + 1], ap=wbd[k].ap[64:128, 64:128])
            x_, t_, nsp = sp.ap[:, 0:1], sp.ap[:, 1:2], sp.ap[:, 2:3]
            self.act(x_, lam(c), AF.Exp, reads=[pvb], writes=[sp.b0], scale=-1.0)
            self.ts("dve", t_, x_, -0.25, 1.0 / 3.0, ALU.mult, ALU.add, reads=[sp.b0], writes=[sp.b0])
            self.tt("dve", t_, t_, x_, ALU.mult, reads=[sp.b0], writes=[sp.b0])
            self.ts("dve", t_, t_, -1.0, 0.5, ALU.mult, ALU.add, reads=[sp.b0], writes=[sp.b0])
            self.tt("dve", t_, t_, x_, ALU.mult, reads=[sp.b0], writes=[sp.b0])
            self.ts("dve", t_, t_, -1.0, 1.0, ALU.mult, ALU.add, reads=[sp.b0], writes=[sp.b0])
            self.tt("dve", t_, t_, x_, ALU.mult, reads=[sp.b0], writes=[sp.b0])
            self.ts("dve", nsp, t_, -8.0, 0.0, ALU.mult, ALU.add, reads=[sp.b0], writes=[sp.b0])
            S.op("pool", lambda e: e.memset(XB.ap[:, 0:3], 0.0), writes=[XB.b[0]])
            self.proj_fm(Wx, 128, lambda q: (XB.ap[:, 3 + q * TQ:3 + (q + 1) * TQ], [XB.b[q]]), evac="act")
            self.proj_fm(Wg, 128, lambda q: (GT.ap[:, q * TQ:(q + 1) * TQ], [GT.b[q]]), evac="dve")
            xb_all = XB.b
            self.ts("dve", XC.ap, XB.ap[:, 0:SEQ], cw(0 * 2 + c), cb(c), ALU.mult, ALU.add, reads=xb_all + [pvb], writes=[XC.b0])
            for j in range(1, 4):
                self.stt(XC.ap, XB.ap[:, j:j + SEQ], cw(j * 2 + c), XC.ap, ALU.mult, ALU.add, reads=xb_all + [XC.b0, pvb], writes=[XC.b0])
            for q in range(NQ):
                qs = slice(q * TQ, (q + 1) * TQ)
                b1, b2 = self.bank(), self.bank()
                self.mmg(b1, self.ps[b1][:, :], [(wbd[0].ap, XC.ap[:, qs])], reads=[wbd[0].b0, XC.b0])
                self.mmg(b2, self.ps[b2][:, :], [(wbd[1].ap, XC.ap[:, qs])], reads=[wbd[1].b0, XC.b0])
                self.act(A.ap[:, qs], self.ps[b1][:, :], AF.Sigmoid, reads=[self.psb[b1], pvb], writes=[A.b0], bias=ba_(c))
                self.act(I.ap[:, qs], self.ps[b2][:, :], AF.Sigmoid, reads=[self.psb[b2], pvb], writes=[I.b0], bias=bx_(c))
            gall = GT.b
            Tg = XB.ap[:, 0:SEQ]
            tb = XB.b
            self.tt("pool", Tg, GT.ap, GT.ap, ALU.mult, reads=gall + [XC.b0], writes=tb)
            self.ts("pool", Tg, Tg, 0.044715, 1.0, ALU.mult, ALU.add, reads=tb, writes=tb)
            self.tt("pool", Tg, Tg, GT.ap, ALU.mult, reads=tb + gall, writes=tb)
            self.act(Tg, Tg, AF.Sigmoid, reads=tb, writes=tb, scale=1.5957691216057308)
            self.tt("pool", Tg, Tg, GT.ap, ALU.mult, reads=tb + gall, writes=tb)
            self.ts("dve", A.ap, A.ap, nsp, 0.0, ALU.mult, ALU.add, reads=[A.b0, sp.b0], writes=[A.b0])
            self.ts("dve", Qp.ap, A.ap, -1.0 / 120.0, 0.0, ALU.mult, ALU.add, reads=[A.b0], writes=[Qp.b0])
            for cc in (-1.0 / 24.0, -1.0 / 6.0, -0.5, -1.0):
                self.stt(Qp.ap, Qp.ap, cc, A.ap, ALU.add, ALU.mult, reads=[Qp.b0, A.b0], writes=[Qp.b0])
            self.ts("dve", A.ap, Qp.ap, -1.0, 1.0, ALU.mult, ALU.add, reads=[Qp.b0], writes=[A.b0])
            self.stt(M.ap, Qp.ap, 2.0, Qp.ap, ALU.subtract, ALU.mult, reads=[Qp.b0], writes=[M.b0])
            self.act(M.ap, M.ap, AF.Sqrt, reads=[M.b0], writes=[M.b0], scale=-1.0)
            self.tt("dve", I.ap, I.ap, XC.ap, ALU.mult, reads=[I.b0, XC.b0], writes=[I.b0])
            self.tt("dve", I.ap, I.ap, M.ap, ALU.mult, reads=[I.b0, M.b0], writes=[I.b0])
            S.op("dve", lambda e: e.tensor_tensor_scan(M.ap, A.ap, I.ap, 0.0, ALU.mult, ALU.add),
                 reads=[A.b0, I.b0], writes=[M.b0])
            self.tt("dve", YG.ap[:, c, :], M.ap, Tg, ALU.mult, reads=[M.b0] + tb, writes=YG.b)
        self.group_finish(l, 1, YG)

    def proj_qz(self, Wq, QZ):
        S = self.S
        S.op("pool", lambda e: e.memset(QZ[0].ap[64:128, :], 0.0), writes=QZ[0].b)
        S.op("pool", lambda e: e.memset(QZ[1].ap[0:64, :], 0.0), writes=QZ[1].b)
        XBF = self.XBF
        for q in range(NQ):
            qs = slice(q * TQ, (q + 1) * TQ)
            bk = self.bank()
            self.mmg(bk, self.ps[bk][:, :], [(Wq.ap[:, kc, :], XBF.ap[:, kc, qs]) for kc in range(KC)],
                     reads=[Wq.b0, XBF.b[q]])
            self.cp("act", QZ[0].ap[0:64, qs], self.ps[bk][0:64, :], reads=[self.psb[bk]], writes=[QZ[0].b[q]])
            self.cp("dve", QZ[1].ap[64:128, qs], self.ps[bk][64:128, :], reads=[self.psb[bk]], writes=[QZ[1].b[q]])

    def pair_proj(self, l, offs, hp, names):
        out = []
        for nm, o in zip(names, offs):
            t = self.carve(nm, [128, KC, 128], dt=BF16)
            self.load(t, self.win(l, o + hp * 128, 128), queue="pool")
            out.append(t)
        return out

    def v_tokmajor(self, Wv, VP):
        S = self.S
        S.op("pool", lambda e: e.memset(VP.ap[:, :, :, 64:128], 1.0), writes=[VP.b0])
        for g4 in range(4):
            bk = self.bank()
            for t4 in range(4):
                tc = g4 * 4 + t4
                self.mmg(bk, self.ps[bk][:, t4 * 128:(t4 + 1) * 128],
                         [(self.XBF.ap[:, kc, tc * 128:(tc + 1) * 128], Wv.ap[:, kc, :]) for kc in range(KC)],
                         reads=[self.XBF.b[g4], Wv.b0])
            self.cp("dve", VP.ap[:, g4 * 4:(g4 + 1) * 4, :, 0:64],
                    self.ps[bk][:, :].rearrange("p (a b c) -> p a b c", a=4, b=2),
                    reads=[self.psb[bk]], writes=[VP.b0])

    def mixer_dil(self, l):
        S = self.S
        YG = self.carve("YGc", [128, 2, SEQ], nb=NQ)
        YH = self.carve("YHc", [128, SEQ], nb=NQ)
        sets = []
        for hp in range(2):
            Wq, Wk, Wv = self.pair_proj(l, (O_SQ, O_SK, O_SV), hp, ("Wq%d" % hp, "Wk%d" % hp, "Wv%d" % hp))
            QZ = [self.carve("QZ%d_%d" % (hp, k), [128, SEQ], nb=NQ, dt=BF16) for k in range(2)]
            KP = self.carve("KP%d" % hp, [128, SEQ], nb=NQ, dt=BF16)
            VP = self.carve("VP%d" % hp, [128, 16, 2, 128], dt=BF16)
            sets.append((Wq, Wk, Wv, QZ, KP, VP))

        def proj(hp):
            Wq, Wk, Wv, QZ, KP, VP = sets[hp]
            self.proj_qz(Wq, QZ)
            self.proj_fm(Wk, 128, lambda q: (KP.ap[:, q * TQ:(q + 1) * TQ], [KP.b[q]]), evac="dve")
            self.v_tokmajor(Wv, VP)

        def attn(hp):
            Wq, Wk, Wv, QZ, KP, VP = sets[hp]
            for h2 in range(2):
                def fin(q, acc, h2=h2, hp=hp):
                    qs = slice(q * TQ, (q + 1) * TQ)
                    if h2 == 0:
                        self.finish_softmax(q, acc, YG.ap[0:64, hp, qs], [YG.b[q]])
                    else:
                        self.finish_softmax(q, acc, YH.ap[0:64, qs], [YH.b[q]])
                        S.dma("sp", YG.ap[64:128, hp, qs], YH.ap[0:64, qs], YH.b[q], reads=[YH.b[q]], writes=[YG.b[q]])
                self.attn_head(lambda j: (KP.ap[:, j * 128:(j + 1) * 128], [KP.b[j // 4]]),
                               lambda lo, hi, h2=h2: (QZ[h2].ap[:, lo:hi], [QZ[h2].b[lo // TQ]]), 128,
                               lambda j, h2=h2: (VP.ap[:, j, h2, :], [VP.b0]), "dil", 0.125, fin)
        proj(0)
        proj(1)
        attn(0)
        attn(1)
        self.group_finish(l, 2, YG)

    def mixer_mlstm(self, l):
        d, S = self.d, self.S
        YG = self.carve("YGd", [128, 2, SEQ], nb=NQ)
        YH = self.carve("YHd", [128, SEQ], nb=NQ)
        CS = self.carve("CS", [128, SEQ], nb=NQ)
        BIAS = self.carve("BIAS", [128, 64])
        small = self.carve("small", [128, 4])
        base1 = self.aoff
        Wg = self.carve("Wgt", [128, KC, 8], dt=BF16)
        DB = self.carve("DB", [128, SEQ], nb=NQ)
        E1 = self.carve("E1", [128, SEQ])
        ONE4 = self.carve("one4", [128, SEQ])
        pvb = self.cst["pvec"].b0
        ident, selh = self.cst["ident"], self.cst["selh"]
        self.load(Wg, self.win(l, O_MI, 8), queue="pool")
        bi, bf = self.pv(("bi", l)), self.pv(("bf", l))
        nbf = small.ap[0:4, 0:1]
        self.ts("dve", nbf, bf(0, 0, 4), -1.0, 0.0, ALU.mult, ALU.add, reads=[pvb], writes=[small.b0])
        S.op("pool", lambda e: e.memset(ONE4.ap[0:4, :], 1.0), writes=[ONE4.b0])
        S.op("pool", lambda e: e.memset(CS.ap, 0.0), writes=CS.b)
        for q in range(NQ):
            qs = slice(q * TQ, (q + 1) * TQ)
            b1, b2 = self.bank(), self.bank()
            self.mmg(b1, self.ps[b1][0:4, :], [(Wg.ap[:, kc, 0:4], self.XBF.ap[:, kc, qs]) for kc in range(KC)], reads=[Wg.b0, self.XBF.b[q]])
            self.mmg(b2, self.ps[b2][0:4, :], [(Wg.ap[:, kc, 4:8], self.XBF.ap[:, kc, qs]) for kc in range(KC)], reads=[Wg.b0, self.XBF.b[q]])
            self.act(E1.ap[0:4, qs], self.ps[b2][0:4, :], AF.Exp, reads=[self.psb[b2], small.b0], writes=[E1.b0], bias=nbf, scale=-1.0)
            self.act(E1.ap[0:4, qs], E1.ap[0:4, qs], AF.Ln, reads=[E1.b0], writes=[E1.b0], bias=1.0, scale=1.0)
            self.ts("dve", DB.ap[0:4, qs], self.ps[b1][0:4, :], bi(0, 0, 4), 0.0, ALU.add, ALU.add,
                    reads=[self.psb[b1], pvb], writes=[DB.b[q]])
        S.op("dve", lambda e: e.tensor_tensor_scan(CS.ap[0:4, :], ONE4.ap[0:4, :], E1.ap[0:4, :], 0.0, ALU.mult, ALU.add),
             reads=[ONE4.b0, E1.b0], writes=CS.b)
        self.tt("dve", DB.ap[0:4, :], DB.ap[0:4, :], CS.ap[0:4, :], ALU.add, reads=DB.b + CS.b, writes=DB.b)
        bk = self.bank()
        for tc in range(16):
            S.op("pe", (lambda e, o=self.ps[bk][:, tc * 4:(tc + 1) * 4], i=DB.ap[0:4, tc * 128:(tc + 1) * 128]:
                        e.transpose(o, i, ident.ap[0:4, 0:4])),
                 reads=DB.b + [ident.b0], writes=[self.psb[bk]], signal=(tc == 15))
        self.cp("dve", BIAS.ap[:, 0:64], self.ps[bk][:, 0:64], reads=[self.psb[bk]], writes=[BIAS.b0])
        self.dump("cs%d" % l, CS.ap, CS.b, [128, SEQ])
        self.dump("bias%d" % l, BIAS.ap, BIAS.b, [128, 64])
        self.rewind(base1, soft=True)
        CSR = [self.carve("CSR0", [128, SEQ], nb=NQ)] * 2
        OG = self.carve("OG", [128, SEQ], nb=NQ)
        Wo = self.carve("Wo", [128, KC, 64], dt=BF16)
        base = self.aoff
        for hp in range(2):
            self.rewind(base, soft=True)
            Wq, Wk, Wv = self.pair_proj(l, (O_MQ, O_MK, O_MV), hp, ("Wq", "Wk", "Wv"))
            QZ = [self.carve("QZ%d" % k, [128, SEQ], nb=NQ, dt=BF16) for k in range(2)]
            KP = self.carve("KP", [128, SEQ], nb=NQ, dt=BF16)
            VP = self.carve("VP", [128, 16, 2, 128], dt=BF16)
            self.proj_qz(Wq, QZ)
            self.proj_fm(Wk, 128, lambda q: (KP.ap[:, q * TQ:(q + 1) * TQ], [KP.b[q]]), evac="dve")
            self.v_tokmajor(Wv, VP)
            for h2 in range(2):
                h = hp * 2 + h2
                pb = 64 * h2
                csr = CSR[h2]
                for q in range(NQ):
                    qs = slice(q * TQ, (q + 1) * TQ)
                    bk = self.bank()
                    self.mmg(bk, self.ps[bk][:, :], [(selh.ap[:, h * 128:(h + 1) * 128], CS.ap[:, qs])], reads=[selh.b0, CS.b[q]])
                    self.cp("act", csr.ap[:, qs], self.ps[bk][:, :], reads=[self.psb[bk]], writes=[csr.b[q]])
                self.load(Wo, self.win(l, O_MO + h * 64, 64), queue="pool")
                for q in range(NQ):
                    qs = slice(q * TQ, (q + 1) * TQ)
                    bk = self.bank()
                    self.mmg(bk, self.ps[bk][0:64, :], [(Wo.ap[:, kc, :], self.XBF.ap[:, kc, qs]) for kc in range(KC)], reads=[Wo.b0, self.XBF.b[q]])
                    self.act(OG.ap[0:64, qs], self.ps[bk][0:64, :], AF.Sigmoid, reads=[self.psb[bk]], writes=[OG.b[q]])

                def fin(q, acc, h2=h2, hp=hp):
                    qs = slice(q * TQ, (q + 1) * TQ)
                    gate = (OG.ap[0:64, qs], [OG.b[q]])
                    if h2 == 0:
                        self.finish_softmax(q, acc, YG.ap[0:64, hp, qs], [YG.b[q]], mlstm=True, gate=gate)
                    else:
                        self.finish_softmax(q, acc, YH.ap[0:64, qs], [YH.b[q]], mlstm=True, gate=gate)
                        S.dma("sp", YG.ap[64:128, hp, qs], YH.ap[0:64, qs], YH.b[q], reads=[YH.b[q]], writes=[YG.b[q]])
                self.attn_head(lambda j: (KP.ap[:, j * 128:(j + 1) * 128], [KP.b[j // 4]]),
                               lambda lo, hi, h2=h2: (QZ[h2].ap[:, lo:hi], [QZ[h2].b[lo // TQ]]), 128,
                               lambda j, h2=h2: (VP.ap[:, j, h2, :], [VP.b0]), "mlstm", 0.125, fin,
                               aux=(csr, lambda j, h=h: (BIAS.ap[:, j * 4 + h:j * 4 + h + 1], [BIAS.b0])))
        self.group_finish(l, 3, YG)

    def phase_outproj(self, l):
        d, S = self.d, self.S
        self.reset_arena(soft=True)
        NBO = 3
        YT = self.carve("YT", [128, 2, KC, TQ], nb=2, dt=BF16)
        wo = [self.carve("wo%d" % k, [128, KC, 128], dt=BF16) for k in range(NBO)]
        sq = self.carve("sq", [128, 2, 2 * TQ], nb=2, dt=BF16)
        mean = self.carve("mean", [128, TQ])
        var = self.carve("var", [128, TQ])
        rstd = self.carve("rstd", [128, TQ])
        wov = d["wout"][l].rearrange("(kc p) m -> p kc m", p=128)
        ysv = d["yscr"].rearrange("(c p) s -> p c s", p=128)
        jobs = [(q, dc) for q in range(NQ) for dc in range(KC)]
        st = [0]

        def issue(upto):
            while st[0] <= upto and st[0] < len(jobs):
                n = st[0]
                st[0] += 1
                q, dc = jobs[n]
                if dc == 0:
                    yi = q % 2
                    S.dma("pool", YT.ap[:, yi, :, :], ysv[:, :, q * TQ:(q + 1) * TQ], YT.b[yi], reads=[self.YSB[q]], writes=[YT.b[yi]])
                t = wo[n % NBO]
                S.dma("pool", t.ap, wov[:, :, dc * 128:(dc + 1) * 128], t.b0, writes=[t.b0])
        n = 0
        for q in range(NQ):
            qs = slice(q * TQ, (q + 1) * TQ)
            yi = q % 2
            for dc in range(KC):
                issue(n + NBO - 1)
                t = wo[n % NBO]
                n += 1
                bk = self.bank()
                self.mmg(bk, self.ps[bk][:, :], [(t.ap[:, kc, :], YT.ap[:, yi, kc, :]) for kc in range(KC)], reads=[t.b0, YT.b[yi]])
                self.stt(self.XT[:, dc, qs], self.XT[:, dc, qs], ALPHA, self.ps[bk][:, :], ALU.mult, ALU.add,
                         reads=[self.XTB[q][dc], self.psb[bk]], writes=[self.XTB[q][dc]])
            self.layernorm(q, ("lng", l, 1), ("lnb", l, 1), 1.0e-5, (sq, mean, var, rstd))


_CACHE = {}


def _inmaps(inputs):
    cs = _consts()
    pv = _pvec(inputs)
    maps = []
    shared = {
        "ffn_w1": np.ascontiguousarray(inputs["ffn_w1"], np.float32),
        "ffn_w3": np.ascontiguousarray(inputs["ffn_w3"], np.float32),
        "ffn_w2": np.ascontiguousarray(inputs["ffn_w2"], np.float32),
        "w_in": np.ascontiguousarray(inputs["w_in"], np.float32),
        "mla_w_uq": np.ascontiguousarray(inputs["mla_w_uq"], np.float32),
        "mla_w_ukv": np.ascontiguousarray(inputs["mla_w_ukv"], np.float32),
        "lru_w_a": np.ascontiguousarray(inputs["lru_w_a"], np.float32),
        "lru_w_x": np.ascontiguousarray(inputs["lru_w_x"], np.float32),
        "w_out": np.ascontiguousarray(inputs["w_out"], np.float32),
        "pvec": pv,
    }
    for k, v in cs.items():
        shared["c_" + k] = v
    return shared


def kernel(**inputs):
    inputs = {k: np.asarray(v) for k, v in inputs.items()}
    if "nc" not in _CACHE:
        _CACHE["nc"] = Builder().build()
    nc = _CACHE["nc"]
    shared = _inmaps(inputs)
    x = np.ascontiguousarray(inputs["x"], np.float32)
    in_maps = []
    for b in range(8):
        m = dict(shared)
        m["x"] = x[b]
        in_maps.append(m)
    res = run_bass_kernel_spmd(nc, in_maps, core_ids=list(range(8)))
    return np.stack([r["out"] for r in res.results], axis=0).astype(np.float32)
```

```python
import numpy as np
from contextlib import ExitStack
import concourse.bass as bass
import concourse.mybir as mybir
from concourse.bass_utils import run_bass_kernel_spmd

F32 = mybir.dt.float32
BF16 = mybir.dt.bfloat16
AF = mybir.ActivationFunctionType
ALU = mybir.AluOpType

SEQ = 2048
DM = 1024
DFF = 2816
DEPTH = 2
NQ = 4
TQ = 512
KC = 8
FC = 22
ALPHA = (2.0 * DEPTH) ** 0.25
N_IN = 2728
O_CQ, O_CKV, O_KR, O_LX, O_LG, O_SQ, O_SK, O_SV, O_MQ, O_MK, O_MV, O_MO, O_MI, O_MF = (
    0, 256, 384, 416, 672, 928, 1184, 1440, 1696, 1952, 2208, 2464, 2720, 2724)
NEG = -30000.0


class Buf:
    __slots__ = ("name", "excl", "writers", "readers", "sem", "cnt")

    def __init__(self, name, excl=False):
        self.name = name
        self.excl = excl
        self.writers = {}
        self.readers = {}
        self.sem = None
        self.cnt = 0


class DSem:
    __slots__ = ("sem", "cnt")

    def __init__(self, sem):
        self.sem = sem
        self.cnt = 0


class Sched:
    ENG = ("pe", "act", "dve", "pool")

    def __init__(self, nc, stack):
        self.nc = nc
        self.stack = stack
        self.q = {k: [] for k in ("pe", "act", "dve", "pool", "sp")}
        self.tick = {k: 0 for k in self.ENG}
        self.sem = {k: stack.enter_context(nc.semaphore("s_" + k)) for k in self.ENG}
        self.seen = {k: {} for k in self.q}
        self.dsems = []
        self.dfree = []

    def _semof(self, key):
        return self.sem[key] if isinstance(key, str) else key.sem

    def _dsem(self):
        if self.dfree:
            return self.dfree.pop()
        ds = DSem(self.stack.enter_context(self.nc.semaphore("d%d" % len(self.dsems))))
        self.dsems.append(ds)
        return ds

    def release(self, bufs):
        for b in bufs:
            if b.sem is not None:
                self.dfree.append(b.sem)
                b.sem = None
            b.writers = {}
            b.readers = {}

    def _collect(self, queue, reads, writes, ownkey=None):
        need = {}

        def add(k, t):
            if k is ownkey:
                return
            if queue == "pe" and k == "pe":
                return
            if need.get(k, 0) < t:
                need[k] = t
        for b in reads:
            for k, t in b.writers.items():
                add(k, t)
            if b.excl:
                for k, t in b.readers.items():
                    add(k, t)
        for b in writes:
            for k, t in b.writers.items():
                add(k, t)
            for k, t in b.readers.items():
                add(k, t)
        waits = []
        seen = self.seen[queue]
        for k, t in need.items():
            if seen.get(k, 0) >= t:
                continue
            seen[k] = t
            waits.append((self._semof(k), t))
        return waits

    def _update(self, key, tick, reads, writes):
        for b in reads:
            if b.excl:
                b.writers = {key: tick}
                b.readers = {}
            elif b.readers.get(key, 0) < tick:
                b.readers[key] = tick
        for b in writes:
            b.writers = {key: tick}
            b.readers = {}

    def op(self, queue, fn, reads=(), writes=(), signal=True):
        waits = self._collect(queue, reads, writes)
        tick = self.tick[queue] + 1
        if signal:
            self.tick[queue] = tick
        self._update(queue, tick, reads, writes)
        self.q[queue].append((waits, fn, self.sem[queue] if signal else None, 1))

    def dma(self, queue, out, in_, sembuf, reads=(), writes=()):
        if sembuf.sem is None:
            sembuf.sem = self._dsem()
        ds = sembuf.sem
        waits = self._collect(queue, reads, writes, ownkey=ds)
        ds.cnt += 1
        tick = 16 * ds.cnt
        self._update(ds, tick, reads, writes)
        self.q[queue].append((waits, lambda e: e.dma_start(out=out, in_=in_), ds.sem, 16))

    def barrier(self):
        for queue in self.q:
            seen = self.seen[queue]
            waits = []
            for k in self.ENG:
                t = self.tick[k]
                if t > 0 and seen.get(k, 0) < t:
                    seen[k] = t
                    waits.append((self.sem[k], t))
            for b in self.dsems:
                t = 16 * b.cnt
                if seen.get(b, 0) < t:
                    seen[b] = t
                    waits.append((b.sem, t))
            if waits:
                self.q[queue].append((waits, None, None, 0))

    def emit(self):
        qs = self.q

        def replay(lst, e):
            for waits, fn, sem, val in lst:
                for s, t in waits:
                    e.wait_ge(s, t)
                if fn is None:
                    continue
                ins = fn(e)
                if sem is not None:
                    ins.then_inc(sem, val)
        with self.nc.Block() as block:
            @block.tensor
            def _(e):
                replay(qs["pe"], e)

            @block.scalar
            def _(e):
                replay(qs["act"], e)

            @block.vector
            def _(e):
                replay(qs["dve"], e)

            @block.gpsimd
            def _(e):
                replay(qs["pool"], e)

            @block.sync
            def _(e):
                replay(qs["sp"], e)


class Tl:
    def __init__(self, ap, name, nb=1):
        self.ap = ap
        self.b = [Buf("%s%d" % (name, i)) for i in range(nb)]

    @property
    def b0(self):
        return self.b[0]


def _consts():
    c = {}
    c["ident"] = np.eye(128, dtype=np.float32)
    c["ones"] = np.ones((128, 128), np.float32)
    k = np.arange(128)[:, None]
    q = np.arange(128)[None, :]
    c["tri01"] = (q >= k).astype(np.float32)
    c["trineg"] = np.where(q >= k, 0.0, NEG).astype(np.float32)
    x = np.arange(SEQ)[None, :]
    d = x - k
    cnt = ((d >= 0) & (d <= 128)).astype(np.float32)
    cnt += ((d >= 0) & (d % 4 == 0) & (d <= 512)).astype(np.float32)
    cnt += ((d >= 0) & (d % 16 == 0) & (d <= 2048)).astype(np.float32)
    c["ctab"] = cnt.astype(np.float32)
    sel = np.zeros((128, 128), np.float32)
    sel[64, :] = 1.0
    c["sel"] = sel
    selh = np.zeros((128, 4, 128), np.float32)
    for h in range(4):
        selh[h, h, :] = 1.0
    c["selh"] = np.ascontiguousarray(selh.reshape(128, 512))
    half = 16
    freqs = (np.float32(10000.0) ** (-np.arange(half, dtype=np.float32) / np.float32(half))).astype(np.float32)
    ang = (np.arange(SEQ, dtype=np.float32)[None, :] * freqs[:, None]).astype(np.float32)
    cos, sin = np.cos(ang).astype(np.float32), np.sin(ang).astype(np.float32)
    rope = np.zeros((2, 32, SEQ), np.float32)
    rope[0, :16] = cos
    rope[0, 16:] = cos
    rope[1, :16] = -sin
    rope[1, 16:] = sin
    c["rope"] = rope
    return c


def _pv_layout():
    off = {}
    n = 0

    def add(name, cols):
        nonlocal n
        off[name] = n
        n += cols
    for l in range(DEPTH):
        for i in range(3):
            add(("lng", l, i), 8)
            add(("lnb", l, i), 8)
        add(("qn", l), 2)
        add(("kvn", l), 1)
        add(("cw", l), 8)
        add(("cb", l), 2)
        add(("ba", l), 2)
        add(("bx", l), 2)
        add(("lam", l), 2)
        add(("on", l), 8)
        add(("bi", l), 1)
        add(("bf", l), 1)
    return off, n


PV_OFF, PV_N = _pv_layout()


def _pvec(inp):
    pv = np.zeros((128, PV_N), np.float32)

    def put(name, arr):
        a = np.asarray(arr, np.float32).reshape(-1, 128).T
        pv[:, PV_OFF[name]:PV_OFF[name] + a.shape[1]] = a
    for l in range(DEPTH):
        for i in range(3):
            put(("lng", l, i), inp["ln_g"][l, i])
            put(("lnb", l, i), inp["ln_b"][l, i])
        put(("qn", l), inp["mla_q_norm"][l])
        put(("kvn", l), inp["mla_kv_norm"][l])
        put(("cw", l), inp["lru_conv_w"][l].reshape(-1))
        put(("cb", l), inp["lru_conv_b"][l])
        put(("ba", l), inp["lru_b_a"][l])
        put(("bx", l), inp["lru_b_x"][l])
        put(("lam", l), inp["lru_lambda"][l])
        put(("on", l), inp["out_norm"][l])
        pv[0:4, PV_OFF[("bi", l)]] = inp["ml_b_i"][l]
        pv[0:4, PV_OFF[("bf", l)]] = inp["ml_b_f"][l]
    return pv


class Builder:
    def __init__(self, debug=None):
        self.debug = debug or {}
        self.nc = bass.Bass("TRN2", target_bir_lowering=False)
        self.dbg_out = []

    def dram_in(self, name, shape):
        return self.nc.dram_tensor(name, list(shape), F32, kind="ExternalInput").ap()

    def build(self):
        nc = self.nc
        d = {}
        d["x"] = self.dram_in("x", [SEQ, DM])
        d["w1"] = self.dram_in("ffn_w1", [DEPTH, 2, DM, DFF])
        d["w3"] = self.dram_in("ffn_w3", [DEPTH, 2, DM, DFF])
        d["w2"] = self.dram_in("ffn_w2", [DEPTH, 2, DFF, DM])
        d["win"] = self.dram_in("w_in", [DEPTH, DM, N_IN])
        d["wuq"] = self.dram_in("mla_w_uq", [DEPTH, 256, 384])
        d["wukv"] = self.dram_in("mla_w_ukv", [DEPTH, 128, 512])
        d["wa"] = self.dram_in("lru_w_a", [DEPTH, 4, 64, 64])
        d["wx"] = self.dram_in("lru_w_x", [DEPTH, 4, 64, 64])
        d["wout"] = self.dram_in("w_out", [DEPTH, DM, DM])
        d["pvec"] = self.dram_in("pvec", [128, PV_N])
        for nm, shp in (("ident", [128, 128]), ("ones", [128, 128]), ("tri01", [128, 128]),
                        ("trineg", [128, 128]), ("ctab", [128, SEQ]), ("sel", [128, 128]),
                        ("selh", [128, 512]), ("rope", [2, 32, SEQ])):
            d[nm] = self.dram_in("c_" + nm, shp)
        d["out"] = nc.dram_tensor("out", [SEQ, DM], F32, kind="ExternalOutput").ap()
        d["yscr"] = nc.dram_tensor("yscr", [DM, SEQ], F32).ap()
        self.d = d
        with ExitStack() as st:
            self.st = st
            self.S = Sched(nc, st)
            self._alloc()
            self._program()
            self.S.barrier()
            self.S.emit()
        return nc

    def sbt(self, name, shape):
        return self.st.enter_context(self.nc.sbuf_tensor(name, list(shape), F32))

    def _alloc(self):
        nc, st = self.nc, self.st
        self.XT = self.sbt("XT", [128, KC, SEQ])
        self.XTB = [[Buf("xt%d_%d" % (i, c)) for c in range(KC)] for i in range(NQ)]
        self.YSB = [Buf("yscr%d" % i) for i in range(NQ)]
        self.cst = {}
        for nm, cols in (("ident", 128), ("ones", 128), ("tri01", 128), ("trineg", 128),
                         ("sel", 128), ("selh", 512), ("pvec", PV_N)):
            self.cst[nm] = Tl(self.sbt("k_" + nm, [128, cols])[:], nm)
        for nm, cols in (("identb", 128), ("trinegb", 128), ("onesb", 128), ("ctabb", SEQ)):
            self.cst[nm] = Tl(self.st.enter_context(self.nc.sbuf_tensor("k_" + nm, [128, cols], BF16))[:], nm)
        self.ps = [st.enter_context(nc.psum_tensor("ps%d" % i, [128, 512], F32)) for i in range(8)]
        self.psb = [Buf("ps%d" % i, excl=True) for i in range(8)]
        self._rr = 0
        self._ra = 0
        rem = nc.sbuf_bytes_remaining
        self.AN = (rem - 1024) // 4
        self.arena = self.sbt("arena", [128, self.AN])
        self.aoff = 0
        self.live = []
        self.ghosts = []
        self.leaked = []
        self.attn_filler = None

    def reset_arena(self, soft=False):
        self.rewind(0, soft=soft)

    def rewind(self, base, soft=False):
        keep, dead = [], []
        for ent in self.live:
            (dead if ent[0] >= base else keep).append(ent)
        if soft:
            for off, n, tl in dead:
                users = {}
                for b in tl.b:
                    for src in (b.writers, b.readers):
                        for k, t in src.items():
                            if users.get(k, 0) < t:
                                users[k] = t
                    self.leaked.append(b)
                if users:
                    self.ghosts.append((off, off + n, users))
        else:
            self.S.barrier()
            for off, n, tl in dead:
                self.S.release(tl.b)
            self.S.release(self.leaked)
            self.leaked = []
            self.ghosts = []
        self.live = keep
        self.aoff = base

    def _ghost_users(self, lo, hi):
        users = {}
        for g0, g1, u in self.ghosts:
            if g0 < hi and lo < g1:
                for k, t in u.items():
                    if users.get(k, 0) < t:
                        users[k] = t
        return users

    def track(self, tl, lo, hi):
        u = self._ghost_users(lo, hi)
        if u:
            for b in tl.b:
                b.readers = dict(u)
        self.live.append((lo, hi - lo, tl))
        return tl

    def carve(self, name, shape, nb=1, dt=F32):
        n = int(np.prod(shape[1:]))
        if dt == BF16:
            assert n % 2 == 0
            n //= 2
        assert self.aoff + n <= self.AN, (name, self.aoff, n, self.AN)
        ap = self.arena[0:shape[0], self.aoff:self.aoff + n]
        if dt == BF16:
            ap = ap.bitcast(BF16)
        off0 = self.aoff
        self.aoff += n
        if len(shape) == 3:
            ap = ap.rearrange("p (a b) -> p a b", a=shape[1])
        elif len(shape) == 4:
            ap = ap.rearrange("p (a b c) -> p a b c", a=shape[1], b=shape[2])
        tl = Tl(ap, name, nb)
        u = self._ghost_users(off0, off0 + n)
        if u:
            for b in tl.b:
                b.readers = dict(u)
        self.live.append((off0, n, tl))
        return tl

    def bank(self):
        i = self._rr
        self._rr = (self._rr + 1) % 6
        return i

    def accbank(self):
        i = 6 + self._ra
        self._ra ^= 1
        return i

    def mmg(self, bank, out, pairs, reads, start=True, last=True, extra_w=()):
        S = self.S
        n = len(pairs)
        w = [self.psb[bank]] + list(extra_w)
        for i, (lt, rh) in enumerate(pairs):
            st_ = start and i == 0
            sp_ = last and i == n - 1
            sig = (i == n - 1)
            rr = reads if (i == 0 or i == n - 1) else ()
            ww = w if (i == 0 or i == n - 1) else ()
            S.op("pe", (lambda e, o=out, a=lt, b=rh, s=st_, p=sp_: e.matmul(o, a, b, start=s, stop=p)),
                 reads=rr, writes=ww, signal=sig)

    def act(self, out, in_, func, reads, writes, bias=0.0, scale=1.0):
        self.S.op("act", lambda e: e.activation(out, in_, func, bias=bias, scale=scale), reads=reads, writes=writes)

    def tt(self, eng, out, a, b, op, reads, writes):
        self.S.op(eng, lambda e: e.tensor_tensor(out, a, b, op), reads=reads, writes=writes)

    def ts(self, eng, out, a, s1, s2, op0, op1, reads, writes):
        self.S.op(eng, lambda e: e.tensor_scalar(out, a, s1, s2, op0, op1), reads=reads, writes=writes)

    def stt(self, out, a, sc, b, op0, op1, reads, writes):
        self.S.op("dve", lambda e: e.scalar_tensor_tensor(out, a, sc, b, op0, op1), reads=reads, writes=writes)

    def cp(self, eng, out, in_, reads, writes):
        if eng == "act":
            self.S.op("act", lambda e: e.copy(out, in_), reads=reads, writes=writes)
        else:
            self.S.op(eng, lambda e: e.tensor_copy(out, in_), reads=reads, writes=writes)

    def load(self, tl, src, bi=0, queue=None, ap=None):
        if queue is None:
            queue = "sp"
        self.S.dma(queue, ap if ap is not None else tl.ap, src, tl.b[bi], writes=[tl.b[bi]])

    def pv(self, key, rows=128):
        o = PV_OFF[key]
        return lambda c=0, r0=0, r1=rows: self.cst["pvec"].ap[r0:r1, o + c:o + c + 1]

    def dump(self, name, tl_ap, bufs, shape):
        if name not in self.debug:
            return
        o = self.nc.dram_tensor("dbg_" + name, list(shape), F32, kind="ExternalOutput").ap()
        self.dbg_out.append("dbg_" + name)
        self.S.dma("sp", o, tl_ap, bufs[0], reads=bufs)

    def dump_xt(self, name):
        if name not in self.debug:
            return
        o = self.nc.dram_tensor("dbg_" + name, [DM, SEQ], F32, kind="ExternalOutput").ap()
        self.dbg_out.append("dbg_" + name)
        for q in range(NQ):
            self.S.dma("sp", o.rearrange("(c p) s -> p c s", p=128)[:, :, q * TQ:(q + 1) * TQ],
                       self.XT[:, :, q * TQ:(q + 1) * TQ], self.XTB[q][0], reads=self.XTB[q])

    def _program(self):
        d, S = self.d, self.S
        for nm, t in self.cst.items():
            if not nm.endswith("b"):
                self.load(t, d[nm])
        self.load(self.cst["ctabb"], d["ctab"], queue="pool")
        self.cp("dve", self.cst["identb"].ap, self.cst["ident"].ap, reads=[self.cst["ident"].b0], writes=[self.cst["identb"].b0])
        self.cp("dve", self.cst["trinegb"].ap, self.cst["trineg"].ap, reads=[self.cst["trineg"].b0], writes=[self.cst["trinegb"].b0])
        self.cp("dve", self.cst["onesb"].ap, self.cst["ones"].ap, reads=[self.cst["ones"].b0], writes=[self.cst["onesb"].b0])
        self.phase_load_x()
        self.dump_xt("x0")
        stop = self.debug.get("stop")
        for l in range(DEPTH):
            self.phase_ffn(l, 0)
            self.dump_xt("ffn%d0" % l)
            if stop == ("ffn", l, 0):
                break
            self.phase_mixer(l)
            self.dump_xt("mix%d" % l)
            if stop == ("mix", l):
                break
            self.phase_ffn_wide(l, 1)
            self.dump_xt("ffn%d1" % l)
        self.phase_store()

    def phase_load_x(self):
        d, S = self.d, self.S
        self.reset_arena()
        xin = self.carve("xin", [128, 2, DM], nb=2)
        ident = self.cst["ident"]
        for tt in range(16):
            bi = tt % 2
            self.load(xin, d["x"][tt * 128:(tt + 1) * 128, :], bi=bi, ap=xin.ap[:, bi, :])
            q = tt // 4
            for half in range(2):
                bk = self.bank()
                for j in range(4):
                    dc = half * 4 + j
                    S.op("pe", (lambda e, o=self.ps[bk][:, j * 128:(j + 1) * 128], i=xin.ap[:, bi, dc * 128:(dc + 1) * 128]:
                                e.transpose(o, i, ident.ap)),
                         reads=[xin.b[bi], ident.b0], writes=[self.psb[bk]], signal=(j == 3))
                eng = "act" if half == 0 else "dve"
                self.cp(eng, self.XT[:, half * 4:(half + 1) * 4, tt * 128:(tt + 1) * 128],
                        self.ps[bk][:, :].rearrange("p (a b) -> p a b", a=4),
                        reads=[self.psb[bk]], writes=self.XTB[q][half * 4:(half + 1) * 4])

    def phase_store(self):
        d, S = self.d, self.S
        self.reset_arena()
        xo = self.carve("xout", [128, 2, DM], nb=2)
        ident = self.cst["ident"]
        for tt in range(16):
            bi = tt % 2
            q = tt // 4
            for half in range(2):
                bk = self.bank()
                for j in range(4):
                    dc = half * 4 + j
                    S.op("pe", (lambda e, o=self.ps[bk][:, j * 128:(j + 1) * 128], i=self.XT[:, dc, tt * 128:(tt + 1) * 128]:
                                e.transpose(o, i, ident.ap)),
                         reads=[self.XTB[q][dc], ident.b0], writes=[self.psb[bk]], signal=(j == 3))
                eng = "act" if half == 0 else "dve"
                self.cp(eng, xo.ap[:, bi, half * 512:(half + 1) * 512], self.ps[bk][:, :],
                        reads=[self.psb[bk]], writes=[xo.b[bi]])
            S.dma("sp", d["out"][tt * 128:(tt + 1) * 128, :], xo.ap[:, bi, :], xo.b[bi], reads=[xo.b[bi]])

    def layernorm(self, q, gkey, bkey, eps, tmp):
        S = self.S
        ones = self.cst["ones"]
        sq, mean, var, rstd = tmp
        qs = slice(q * TQ, (q + 1) * TQ)
        XB = self.XTB[q]
        bs, bq = self.bank(), self.bank()
        onesb = self.cst["onesb"]
        for dc in range(KC):
            i = dc % 2
            self.act(sq.ap[:, i, 0:TQ], self.XT[:, dc, qs], AF.Square, reads=[XB[dc]], writes=[sq.b[i]])
            self.act(sq.ap[:, i, TQ:2 * TQ], self.XT[:, dc, qs], AF.Identity, reads=[XB[dc]], writes=[sq.b[i]])
            S.op("pe", (lambda e, o=self.ps[bq][:, :], r=sq.ap[:, i, 0:TQ], s=(dc == 0), p=(dc == KC - 1):
                        e.matmul(o, onesb.ap, r, start=s, stop=p)),
                 reads=[sq.b[i], onesb.b0], writes=[self.psb[bq]], signal=(dc == KC - 1))
            S.op("pe", (lambda e, o=self.ps[bs][:, :], r=sq.ap[:, i, TQ:2 * TQ], s=(dc == 0), p=(dc == KC - 1):
                        e.matmul(o, onesb.ap, r, start=s, stop=p)),
                 reads=[sq.b[i]], writes=[self.psb[bs]], signal=True)
        S.op("act", lambda e: e.mul(mean.ap, self.ps[bs][:, :], 1.0 / DM), reads=[self.psb[bs]], writes=[mean.b0])
        self.tt("dve", var.ap, mean.ap, mean.ap, ALU.mult, reads=[mean.b0], writes=[var.b0])
        self.stt(var.ap, self.ps[bq][:, :], 1.0 / DM, var.ap, ALU.mult, ALU.subtract,
                 reads=[self.psb[bq], var.b0], writes=[var.b0])
        self.act(rstd.ap, var.ap, AF.Ln, reads=[var.b0], writes=[rstd.b0], bias=eps, scale=1.0)
        self.act(rstd.ap, rstd.ap, AF.Exp, reads=[rstd.b0], writes=[rstd.b0], scale=-0.5)
        g, b = self.pv(gkey), self.pv(bkey)
        pvb = self.cst["pvec"].b0
        for dc in range(KC):
            x = self.XT[:, dc, qs]
            self.tt("dve", x, x, mean.ap, ALU.subtract, reads=[XB[dc], mean.b0], writes=[XB[dc]])
            self.tt("dve", x, x, rstd.ap, ALU.mult, reads=[XB[dc], rstd.b0], writes=[XB[dc]])
            self.act(x, x, AF.Identity, reads=[XB[dc], pvb], writes=[XB[dc]], bias=b(dc), scale=g(dc))

    def phase_ffn(self, l, i):
        d, S = self.d, self.S
        self.reset_arena(soft=(i == 1 or l == 0))
        NB13, NB2 = 4, 3
        HT = self.carve("HT", [128, FC, TQ], dt=BF16)
        w13t = [self.carve("w13_%d" % k, [128, 2, KC, 128], dt=BF16) for k in range(NB13)]
        w2t = [self.carve("w2_%d" % k, [128, FC, 128], dt=BF16) for k in range(NB2)]
        XBt = [self.carve("xb%d" % k, [128, KC, TQ], dt=BF16) for k in range(2)]
        sil = self.carve("sil", [128, 2, TQ], nb=2)
        sq = self.carve("sq", [128, 2, 2 * TQ], nb=2, dt=BF16)
        mean = self.carve("mean", [128, TQ])
        var = self.carve("var", [128, TQ])
        rstd = self.carve("rstd", [128, TQ])
        w1v = d["w1"][l, i].rearrange("(kc p) f -> p kc f", p=128)
        w3v = d["w3"][l, i].rearrange("(kc p) f -> p kc f", p=128)
        w2v = d["w2"][l, i].rearrange("(fc p) m -> p fc m", p=128)
        jobs13 = [(q, fc) for q in range(NQ) for fc in range(FC)]
        jobs2 = [(q, dc) for q in range(NQ) for dc in range(KC)]
        st13 = [0]
        st2 = [0]

        def issue13(upto):
            while st13[0] <= upto and st13[0] < len(jobs13):
                n = st13[0]
                st13[0] += 1
                q, fc = jobs13[n]
                t = w13t[n % NB13]
                S.dma("pool", t.ap[:, 0, :, :], w1v[:, :, fc * 128:(fc + 1) * 128], t.b0, writes=[t.b0])
                S.dma("pool", t.ap[:, 1, :, :], w3v[:, :, fc * 128:(fc + 1) * 128], t.b0, writes=[t.b0])

        def issue2(upto):
            while st2[0] <= upto and st2[0] < len(jobs2):
                n = st2[0]
                st2[0] += 1
                q, dc = jobs2[n]
                t = w2t[n % NB2]
                S.dma("pool", t.ap, w2v[:, :, dc * 128:(dc + 1) * 128], t.b0, writes=[t.b0])

        def mkxb(q):
            if q < NQ:
                xb = XBt[q % 2]
                self.cp("dve", xb.ap, self.XT[:, :, q * TQ:(q + 1) * TQ], reads=self.XTB[q], writes=[xb.b0])
        stream_lru = (i == 0) and not self.debug.get("no_lru_stream")
        if stream_lru:
            self.lru_setup(l)
        fill = []

        def filler(k):
            for _ in range(k):
                while fill:
                    try:
                        next(fill[0])
                        break
                    except StopIteration:
                        fill.pop(0)
                if not fill:
                    return
        issue13(NB13 - 2)
        issue2(0)
        mkxb(0)
        n13 = 0
        n2 = 0
        for q in range(NQ):
            qs = slice(q * TQ, (q + 1) * TQ)
            xb = XBt[q % 2]
            for fc in range(FC):
                issue13(n13 + NB13 - 1)
                t = w13t[n13 % NB13]
                n13 += 1
                ba, bb = self.bank(), self.bank()
                self.mmg(ba, self.ps[ba][:, :], [(t.ap[:, 0, kc, :], xb.ap[:, kc, :]) for kc in range(KC)],
                         reads=[t.b0, xb.b0])
                self.mmg(bb, self.ps[bb][:, :], [(t.ap[:, 1, kc, :], xb.ap[:, kc, :]) for kc in range(KC)],
                         reads=[t.b0, xb.b0])
                si = fc % 2
                self.act(sil.ap[:, si, :], self.ps[ba][:, :], AF.Silu, reads=[self.psb[ba]], writes=[sil.b[si]])
                self.tt("dve", HT.ap[:, fc, :], sil.ap[:, si, :], self.ps[bb][:, :], ALU.mult,
                        reads=[sil.b[si], self.psb[bb]], writes=[HT.b0])
                if fc == FC - 4:
                    issue2(n2 + NB2 - 1)
                filler(2)
            mkxb(q + 1)
            for dc in range(KC):
                issue2(n2 + NB2 - 1)
                t = w2t[n2 % NB2]
                n2 += 1
                bk = self.bank()
                self.mmg(bk, self.ps[bk][:, :], [(t.ap[:, fc, :], HT.ap[:, fc, :]) for fc in range(FC)],
                         reads=[t.b0, HT.b0])
                self.stt(self.XT[:, dc, qs], self.XT[:, dc, qs], 2.0 * ALPHA, self.ps[bk][:, :], ALU.mult, ALU.add,
                         reads=[self.XTB[q][dc], self.psb[bk]], writes=[self.XTB[q][dc]])
                if dc < KC - 2:
                    filler(5)
            self.layernorm(q, ("lng", l, 2 * i), ("lnb", l, 2 * i), 4.0e-5, (sq, mean, var, rstd))
            if stream_lru:
                fill.append(self.lru_tile(l, q))
        filler(100000)

    def phase_ffn_wide(self, l, i):
        d, S = self.d, self.S
        self.reset_arena(soft=True)
        NB13, NB2, NH = 4, 3, 2
        TW = NH * TQ
        NQQ = NQ // NH
        HT = self.carve("HT", [128, FC, TW], dt=BF16)
        w13t = [self.carve("w13_%d" % k, [128, 2, KC, 128], dt=BF16) for k in range(NB13)]
        w2t = [self.carve("w2_%d" % k, [128, FC, 128], dt=BF16) for k in range(NB2)]
        XBt = [self.carve("xb%d" % k, [128, KC, TW], dt=BF16) for k in range(2)]
        sil = self.carve("sil", [128, 2, TQ], nb=2)
        sq = self.carve("sq", [128, 2, 2 * TQ], nb=2, dt=BF16)
        mean = self.carve("mean", [128, TQ])
        var = self.carve("var", [128, TQ])
        rstd = self.carve("rstd", [128, TQ])
        w1v = d["w1"][l, i].rearrange("(kc p) f -> p kc f", p=128)
        w3v = d["w3"][l, i].rearrange("(kc p) f -> p kc f", p=128)
        w2v = d["w2"][l, i].rearrange("(fc p) m -> p fc m", p=128)
        jobs13 = [(qq, fc) for qq in range(NQQ) for fc in range(FC)]
        jobs2 = [(qq, dc) for qq in range(NQQ) for dc in range(KC)]
        st13 = [0]
        st2 = [0]

        def issue13(upto):
            while st13[0] <= upto and st13[0] < len(jobs13):
                n = st13[0]
                st13[0] += 1
                qq, fc = jobs13[n]
                t = w13t[n % NB13]
                S.dma("pool", t.ap[:, 0, :, :], w1v[:, :, fc * 128:(fc + 1) * 128], t.b0, writes=[t.b0])
                S.dma("pool", t.ap[:, 1, :, :], w3v[:, :, fc * 128:(fc + 1) * 128], t.b0, writes=[t.b0])

        def issue2(upto):
            while st2[0] <= upto and st2[0] < len(jobs2):
                n = st2[0]
                st2[0] += 1
                qq, dc = jobs2[n]
                t = w2t[n % NB2]
                S.dma("pool", t.ap, w2v[:, :, dc * 128:(dc + 1) * 128], t.b0, writes=[t.b0])

        def mkxb(qq):
            if qq < NQQ:
                xb = XBt[qq % 2]
                for h in range(NH):
                    q = qq * NH + h
                    self.cp("dve", xb.ap[:, :, h * TQ:(h + 1) * TQ], self.XT[:, :, q * TQ:(q + 1) * TQ],
                            reads=self.XTB[q], writes=[xb.b0])
        issue13(NB13 - 2)
        issue2(0)
        mkxb(0)
        n13 = 0
        n2 = 0
        for qq in range(NQQ):
            xb = XBt[qq % 2]
            for fc in range(FC):
                issue13(n13 + NB13 - 1)
                t = w13t[n13 % NB13]
                n13 += 1
                for h in range(NH):
                    hs = slice(h * TQ, (h + 1) * TQ)
                    ba, bb = self.bank(), self.bank()
                    self.mmg(ba, self.ps[ba][:, :], [(t.ap[:, 0, kc, :], xb.ap[:, kc, hs]) for kc in range(KC)],
                             reads=[t.b0, xb.b0])
                    self.mmg(bb, self.ps[bb][:, :], [(t.ap[:, 1, kc, :], xb.ap[:, kc, hs]) for kc in range(KC)],
                             reads=[t.b0, xb.b0])
                    si = (fc * NH + h) % 2
                    self.act(sil.ap[:, si, :], self.ps[ba][:, :], AF.Silu, reads=[self.psb[ba]], writes=[sil.b[si]])
                    self.tt("dve", HT.ap[:, fc, hs], sil.ap[:, si, :], self.ps[bb][:, :], ALU.mult,
                            reads=[sil.b[si], self.psb[bb]], writes=[HT.b0])
                if fc == FC - 4:
                    issue2(n2 + NB2 - 1)
            mkxb(qq + 1)
            for dc in range(KC):
                issue2(n2 + NB2 - 1)
                t = w2t[n2 % NB2]
                n2 += 1
                for h in range(NH):
                    q = qq * NH + h
                    qs = slice(q * TQ, (q + 1) * TQ)
                    hs = slice(h * TQ, (h + 1) * TQ)
                    bk = self.bank()
                    self.mmg(bk, self.ps[bk][:, :], [(t.ap[:, fc, :], HT.ap[:, fc, hs]) for fc in range(FC)],
                             reads=[t.b0, HT.b0])
                    self.stt(self.XT[:, dc, qs], self.XT[:, dc, qs], 2.0 * ALPHA, self.ps[bk][:, :], ALU.mult, ALU.add,
                             reads=[self.XTB[q][dc], self.psb[bk]], writes=[self.XTB[q][dc]])
            for h in range(NH):
                self.layernorm(qq * NH + h, ("lng", l, 2 * i), ("lnb", l, 2 * i), 4.0e-5, (sq, mean, var, rstd))

    def proj_fm(self, wt, ncols, out_fn, evac="act", c0=0):
        XBF = self.XBF
        for q in range(NQ):
            qs = slice(q * TQ, (q + 1) * TQ)
            bk = self.bank()
            self.mmg(bk, self.ps[bk][0:ncols, :], [(wt.ap[:, kc, c0:c0 + ncols], XBF.ap[:, kc, qs]) for kc in range(KC)],
                     reads=[wt.b0, XBF.b[q]])
            oap, obufs = out_fn(q)
            self.cp(evac, oap, self.ps[bk][0:ncols, :], reads=[self.psb[bk]], writes=obufs)

    def rmsnorm_fm(self, X, Y, nch, rows_key, eps, nfeat, tmp):
        S = self.S
        ones = self.cst["onesb"]
        sq, rstd0 = tmp
        rot = [rstd0, self.ND, self.RD]
        g = self.pv(rows_key)
        pvb = self.cst["pvec"].b0
        for q in range(NQ):
            qs = slice(q * TQ, (q + 1) * TQ)
            rstd = rot[q % 3]
            bk = self.bank()
            for c in range(nch):
                i = c % 2
                src = X.ap[:, c, qs] if nch > 1 else X.ap[:, qs]
                self.act(sq.ap[:, i, :], src, AF.Square, reads=[X.b[q]], writes=[sq.b[i]])
                S.op("pe", (lambda e, o=self.ps[bk][:, :], r=sq.ap[:, i, :], s=(c == 0), p=(c == nch - 1):
                            e.matmul(o, ones.ap, r, start=s, stop=p)),
                     reads=[sq.b[i], ones.b0], writes=[self.psb[bk]], signal=True)
            self.act(rstd.ap, self.ps[bk][:, :], AF.Ln, reads=[self.psb[bk]], writes=[rstd.b0], bias=eps, scale=1.0 / nfeat)
            self.act(rstd.ap, rstd.ap, AF.Exp, reads=[rstd.b0], writes=[rstd.b0], scale=-0.5)
            for c in range(nch):
                src = X.ap[:, c, qs] if nch > 1 else X.ap[:, qs]
                dst = Y.ap[:, c, qs] if nch > 1 else Y.ap[:, qs]
                self.stt(dst, src, g(c), rstd.ap, ALU.mult, ALU.mult, reads=[X.b[q], rstd.b0, pvb], writes=[Y.b[q]])

    def attn_head(self, kfn, qfn, kdim, vfn, mode, scale, fin_fn, aux=None):
        S = self.S
        ident, trineg, tri01, ctab = (self.cst[k] for k in ("identb", "trinegb", "tri01", "ctabb"))
        PT = self.PT
        NPT = len(PT.b)
        LOOK = NPT - 2
        blocks = [(q, j) for q in range(NQ) for j in range(4 * q + 4)]
        accs = {}
        pts = {}
        pending = []

        def stage_a(n):
            q, j = blocks[n]
            if j == 0:
                accs[q] = self.accbank()
            r = j - 4 * q
            c0 = 128 * max(r, 0)
            lo, hi = q * TQ + c0, (q + 1) * TQ
            bk = self.bank()
            kap, kb = kfn(j)
            qap, qb = qfn(lo, hi)
            pso = self.ps[bk][:, c0:TQ]
            diag = (r >= 0)
            pi = self._pti
            self._pti = (self._pti + 1) % NPT
            pt = PT.ap[:, pi, c0:TQ]
            ptb = PT.b[pi]
            pts[n] = (pt, ptb, c0)
            if mode == "mla":
                self.mmg(bk, pso, [(kap, qap)], reads=kb + qb, last=not diag)
                if diag:
                    S.op("pe", (lambda e, o=self.ps[bk][:, c0:c0 + 128]: e.matmul(o, ident.ap, trineg.ap, start=False, stop=True)),
                         reads=[ident.b0, trineg.b0], writes=[self.psb[bk]], signal=True)
                self.act(pt, pso, AF.Exp, reads=[self.psb[bk]], writes=[ptb], scale=scale)
            elif mode == "dil":
                self.mmg(bk, pso, [(kap, qap)], reads=kb + qb)
                self.act(pt, pso, AF.Exp, reads=[self.psb[bk]], writes=[ptb], scale=scale)
                off = q * TQ - 128 * j
                self.tt("dve", pt, pt, ctab.ap[:, off + c0:off + TQ], ALU.mult, reads=[ptb, ctab.b0], writes=[ptb])
            else:
                csrow, biasfn = aux
                self.mmg(bk, pso, [(kap, qap)], reads=kb + qb)
                ei = self._ei
                self._ei = (self._ei + 1) % len(self.ET.b)
                et = self.ET.ap[:, ei, c0:TQ]
                etb = self.ET.b[ei]
                bap, bb = biasfn(j)
                self.act(et, csrow.ap[:, lo:hi], AF.Exp, reads=[csrow.b[q]] + bb, writes=[etb], bias=bap, scale=-1.0)
                if diag:
                    self.tt("pool", self.ET.ap[:, ei, c0:c0 + 128], self.ET.ap[:, ei, c0:c0 + 128], tri01.ap, ALU.mult,
                            reads=[etb, tri01.b0], writes=[etb])
                self.stt(pt, pso, scale, et, ALU.mult, ALU.mult, reads=[self.psb[bk], etb], writes=[ptb])

        def stage_b(n):
            q, j = blocks[n]
            nj = 4 * q + 4
            acc = accs[q]
            pt, ptb, c0 = pts.pop(n)
            vap, vb = vfn(j)
            S.op("pe", (lambda e, o=self.ps[acc][:, c0:TQ], a=vap, b=pt, s=(j == 0), p=(j == nj - 1):
                        e.matmul(o, a, b, start=s, stop=p)),
                 reads=vb + [ptb], writes=[self.psb[acc]], signal=True)
            if j == nj - 1:
                pending.append((n + 2, q, acc))
            while pending and pending[0][0] <= n:
                _, q_, acc_ = pending.pop(0)
                fin_fn(q_, acc_)
        nb = len(blocks)
        for n in range(min(LOOK, nb)):
            stage_a(n)
        for n in range(nb):
            if n + LOOK < nb:
                stage_a(n + LOOK)
            stage_b(n)
            if self.attn_filler is not None:
                self.attn_filler()
        for _, q_, acc_ in pending:
            fin_fn(q_, acc_)

    def finish_softmax(self, q, acc, out_ap, out_bufs, mlstm=False, gate=None, cpeng="act"):
        T, RD = self.ND, self.RD
        pa = self.psb[acc]
        if not mlstm:
            self.act(T.ap[0:64, :], self.ps[acc][64:128, :], AF.Ln, reads=[pa], writes=[T.b0])
            self.act(RD.ap[0:64, :], T.ap[0:64, :], AF.Exp, reads=[T.b0], writes=[RD.b0], scale=-1.0)
        else:
            self.act(T.ap[0:64, :], self.ps[acc][64:128, :], AF.Identity, reads=[pa], writes=[T.b0])
            self.tt("dve", T.ap[0:64, :], T.ap[0:64, :], T.ap[0:64, :], ALU.mult, reads=[T.b0], writes=[T.b0])
            self.ts("dve", T.ap[0:64, :], T.ap[0:64, :], 1.0, 0.0, ALU.max, ALU.add, reads=[T.b0], writes=[T.b0])
            self.act(T.ap[0:64, :], T.ap[0:64, :], AF.Ln, reads=[T.b0], writes=[T.b0])
            self.act(RD.ap[0:64, :], T.ap[0:64, :], AF.Exp, reads=[T.b0], writes=[RD.b0], scale=-0.5)
        if gate is not None:
            gap, gb = gate
            self.tt("pool", RD.ap[0:64, :], RD.ap[0:64, :], gap, ALU.mult, reads=[RD.b0] + gb, writes=[RD.b0])
        self.tt("dve", out_ap, self.ps[acc][0:64, :], RD.ap[0:64, :], ALU.mult, reads=[pa, RD.b0], writes=out_bufs)

    def group_finish(self, l, g, YG):
        S = self.S
        ones = self.cst["onesb"]
        sq = self.gsq
        rot = [self.grstd, self.ND, self.RD]
        on = self.pv(("on", l))
        pvb = self.cst["pvec"].b0
        for q in range(NQ):
            qs = slice(q * TQ, (q + 1) * TQ)
            rstd = rot[q % 3]
            bk = self.bank()
            for c in range(2):
                self.act(sq.ap[:, c, :], YG.ap[:, c, qs], AF.Square, reads=[YG.b[q]], writes=[sq.b[c]])
                S.op("pe", (lambda e, o=self.ps[bk][:, :], r=sq.ap[:, c, :], s=(c == 0), p=(c == 1):
                            e.matmul(o, ones.ap, r, start=s, stop=p)),
                     reads=[sq.b[c], ones.b0], writes=[self.psb[bk]], signal=True)
            self.act(rstd.ap, self.ps[bk][:, :], AF.Ln, reads=[self.psb[bk]], writes=[rstd.b0], bias=1e-6, scale=1.0 / 256)
            self.act(rstd.ap, rstd.ap, AF.Exp, reads=[rstd.b0], writes=[rstd.b0], scale=-0.5)
            for c in range(2):
                self.stt(YG.ap[:, c, qs], YG.ap[:, c, qs], on(g * 2 + c), rstd.ap, ALU.mult, ALU.mult,
                         reads=[YG.b[q], rstd.b0, pvb], writes=[YG.b[q]])
            S.dma("sp", self.d["yscr"].rearrange("(c p) s -> p c s", p=128)[:, g * 2:g * 2 + 2, qs],
                  YG.ap[:, :, qs], YG.b[q], reads=[YG.b[q]], writes=[self.YSB[q]])
        if ("yg%d%d" % (l, g)) in self.debug:
            self.dump("yg%d%d" % (l, g), YG.ap, YG.b, [128, 2, SEQ])

    def phase_mixer(self, l):
        self.reset_arena(soft=True)
        self.XBF = self.carve("XBF", [128, KC, SEQ], nb=NQ, dt=BF16)
        for q in range(NQ):
            self.cp("dve" if q % 2 else "act", self.XBF.ap[:, :, q * TQ:(q + 1) * TQ], self.XT[:, :, q * TQ:(q + 1) * TQ],
                    reads=self.XTB[q], writes=[self.XBF.b[q]])
        self.PT = self.carve("PT", [128, 5, TQ], nb=5, dt=BF16)
        self._pti = 0
        self.ET = self.carve("ET", [128, 4, TQ], nb=4)
        self._ei = 0
        self.ND = self.carve("ND", [128, TQ])
        self.RD = self.carve("RD", [128, TQ])
        self.gsq = self.carve("gsq", [128, 2, TQ], nb=2, dt=BF16)
        self.grstd = self.carve("grstd", [128, TQ])
        base = self.aoff
        for g, fn in enumerate((self.mixer_mla, self.mixer_lru, self.mixer_dil, self.mixer_mlstm)):
            if "only_g" in self.debug and g not in self.debug["only_g"]:
                continue
            if g == 1 and not self.debug.get("no_lru_stream"):
                continue
            self.rewind(base, soft=True)
            fn(l)
        self.phase_outproj(l)

    def win(self, l, c0, n):
        return self.d["win"][l].rearrange("(kc p) f -> p kc f", p=128)[:, :, c0:c0 + n]

    def mixer_mla(self, l):
        d, S = self.d, self.S
        CQ = self.carve("CQ", [128, 2, SEQ], nb=NQ, dt=BF16)
        CKV = self.carve("CKV", [128, SEQ], nb=NQ, dt=BF16)
        KRR = self.carve("KRR", [128, SEQ], nb=NQ, dt=BF16)
        RY = self.carve("RY", [128, 2, SEQ])
        ry0 = self.aoff - 2 * SEQ
        ROPE = self.track(Tl(RY.ap, "rope"), ry0, self.aoff)
        YH = [self.track(Tl(RY.ap[:, k, :], "YHa%d" % k, NQ), ry0, self.aoff) for k in range(2)]
        tmp = self.carve("tmpr", [128, TQ])
        tmp2 = self.carve("tmpr2", [128, TQ])
        base2 = self.aoff
        Wa = self.carve("Wa", [128, KC, 256], dt=BF16)
        Wb = self.carve("Wb", [128, KC, 160], dt=BF16)
        Wsw = self.carve("Wsw", [128, KC, 96], dt=BF16)
        CQr = self.carve("CQr", [128, 2, SEQ], nb=NQ)
        CKVr = self.carve("CKVr", [128, SEQ], nb=NQ)
        self.load(ROPE, d["rope"].rearrange("t r s -> r t s"), ap=ROPE.ap[64:96, :, :])
        self.load(Wa, self.win(l, O_CQ, 256), queue="pool")
        self.load(Wb, self.win(l, O_CKV, 160), queue="pool")
        self.load(Wsw, self.win(l, 320, 64), ap=Wsw.ap[:, :, 0:64], queue="pool")
        self.load(Wsw, self.win(l, O_KR + 16, 16), ap=Wsw.ap[:, :, 64:80], queue="pool")
        self.load(Wsw, self.win(l, O_KR, 16), ap=Wsw.ap[:, :, 80:96], queue="pool")
        for m in range(2):
            self.proj_fm(Wa, 128, lambda q, m=m: (CQr.ap[:, m, q * TQ:(q + 1) * TQ], [CQr.b[q]]), c0=m * 128,
                         evac="act" if m == 0 else "dve")
        self.proj_fm(Wb, 128, lambda q: (CKVr.ap[:, q * TQ:(q + 1) * TQ], [CKVr.b[q]]), evac="dve")
        self.rmsnorm_fm(CQr, CQ, 2, ("qn", l), 1e-6, 256, (self.gsq, self.grstd))
        self.rmsnorm_fm(CKVr, CKV, 1, ("kvn", l), 1e-6, 128, (self.gsq, self.grstd))
        self.dump("cq%d" % l, CQ.ap, CQ.b, [128, 2, SEQ])
        self.dump("ckv%d" % l, CKV.ap, CKV.b, [128, SEQ])
        for q in range(NQ):
            qs = slice(q * TQ, (q + 1) * TQ)
            ba, bb = self.bank(), self.bank()
            self.mmg(ba, self.ps[ba][0:96, :], [(Wb.ap[:, kc, 64:160], self.XBF.ap[:, kc, qs]) for kc in range(KC)],
                     reads=[Wb.b0, self.XBF.b[q]])
            self.mmg(bb, self.ps[bb][0:96, :], [(Wsw.ap[:, kc, :], self.XBF.ap[:, kc, qs]) for kc in range(KC)],
                     reads=[Wsw.b0, self.XBF.b[q]])
            self.tt("dve", tmp2.ap[64:96, :], self.ps[ba][64:96, :], ROPE.ap[64:96, 0, qs], ALU.mult,
                    reads=[self.psb[ba], ROPE.b0], writes=[tmp2.b0])
            self.tt("dve", tmp.ap[64:96, :], self.ps[bb][64:96, :], ROPE.ap[64:96, 1, qs], ALU.mult,
                    reads=[self.psb[bb], ROPE.b0], writes=[tmp.b0])
            self.tt("pool", KRR.ap[64:96, qs], tmp2.ap[64:96, :], tmp.ap[64:96, :], ALU.add,
                    reads=[tmp2.b0, tmp.b0], writes=[KRR.b[q]])
        self.dump("krr%d" % l, KRR.ap, KRR.b, [128, SEQ])
        self.rewind(base2, soft=True)
        YG = self.carve("YGa", [128, 2, SEQ], nb=NQ)
        QH = [self.carve("QH%d" % k, [128, SEQ], nb=NQ, dt=BF16) for k in range(2)]
        KH = [self.carve("KH%d" % k, [128, SEQ], nb=NQ, dt=BF16) for k in range(2)]
        VH = [self.carve("VH%d" % k, [128, 16, 128], dt=BF16) for k in range(2)]
        wuq = [self.carve("wuq%d" % k, [128, 2, 96], dt=BF16) for k in range(2)]
        wuqs = [self.carve("wuqs%d" % k, [128, 2, 96], dt=BF16) for k in range(2)]
        wukv = [self.carve("wukv%d" % k, [128, 128], dt=BF16) for k in range(2)]
        wuqv = d["wuq"][l].rearrange("(kc p) f -> p kc f", p=128)
        def head_proj(h):
            k = h % 2
            self.load(wuq[k], wuqv[:, :, h * 96:(h + 1) * 96], queue="pool")
            self.load(wuqs[k], wuqv[:, :, h * 96:h * 96 + 64], ap=wuqs[k].ap[:, :, 0:64], queue="pool")
            self.load(wuqs[k], wuqv[:, :, h * 96 + 80:h * 96 + 96], ap=wuqs[k].ap[:, :, 64:80], queue="pool")
            self.load(wuqs[k], wuqv[:, :, h * 96 + 64:h * 96 + 80], ap=wuqs[k].ap[:, :, 80:96], queue="pool")
            self.load(wukv[k], d["wukv"][l][:, h * 128:(h + 1) * 128], queue="pool")
            qh, kh, vh, yh = QH[k], KH[k], VH[k], YH[k]
            S.op("pool", lambda e, v=vh: e.memset(v.ap[:, :, 64:128], 1.0), writes=[vh.b0])
            yield
            yield
            yield
            for q in range(NQ):
                qs = slice(q * TQ, (q + 1) * TQ)
                ba, bb, bc = self.bank(), self.bank(), self.bank()
                self.mmg(ba, self.ps[ba][0:96, :], [(wuq[k].ap[:, c, :], CQ.ap[:, c, qs]) for c in range(2)],
                         reads=[wuq[k].b0, CQ.b[q]])
                self.mmg(bb, self.ps[bb][0:96, :], [(wuqs[k].ap[:, c, :], CQ.ap[:, c, qs]) for c in range(2)],
                         reads=[wuqs[k].b0, CQ.b[q]])
                self.cp("dve", qh.ap[0:64, qs], self.ps[ba][0:64, :], reads=[self.psb[ba]], writes=[qh.b[q]])
                self.tt("dve", tmp2.ap[64:96, :], self.ps[ba][64:96, :], ROPE.ap[64:96, 0, qs], ALU.mult,
                        reads=[self.psb[ba], ROPE.b0], writes=[tmp2.b0])
                self.tt("dve", tmp.ap[64:96, :], self.ps[bb][64:96, :], ROPE.ap[64:96, 1, qs], ALU.mult,
                        reads=[self.psb[bb], ROPE.b0], writes=[tmp.b0])
                self.tt("pool", qh.ap[64:96, qs], tmp2.ap[64:96, :], tmp.ap[64:96, :], ALU.add,
                        reads=[tmp2.b0, tmp.b0], writes=[qh.b[q]])
                yield
                self.mmg(bc, self.ps[bc][:, :], [(wukv[k].ap[:, :], CKV.ap[:, qs])], reads=[wukv[k].b0, CKV.b[q]])
                self.cp("dve", kh.ap[0:64, qs], self.ps[bc][0:64, :], reads=[self.psb[bc]], writes=[kh.b[q]])
                self.cp("pool", kh.ap[64:96, qs], KRR.ap[64:96, qs], reads=[KRR.b[q]], writes=[kh.b[q]])
                yield
            for half in range(2):
                bk = self.bank()
                for t8 in range(8):
                    tc = half * 8 + t8
                    S.op("pe", (lambda e, o=self.ps[bk][:, t8 * 64:(t8 + 1) * 64], a=CKV.ap[:, tc * 128:(tc + 1) * 128], b=wukv[k].ap[:, 64:128]:
                                e.matmul(o, a, b, start=True, stop=True)),
                         reads=[CKV.b[tc // 4], wukv[k].b0], writes=[self.psb[bk]], signal=(t8 == 7))
                self.cp("dve", vh.ap[:, half * 8:(half + 1) * 8, 0:64], self.ps[bk][:, :].rearrange("p (a b) -> p a b", a=8),
                        reads=[self.psb[bk]], writes=[vh.b0])
                yield
            if h == 0:
                self.dump("qh%d" % l, qh.ap, qh.b, [128, SEQ])
                self.dump("kh%d" % l, kh.ap, kh.b, [128, SEQ])
                pass


        def head_attn(h):
            k = h % 2
            qh, kh, vh, yh = QH[k], KH[k], VH[k], YH[k]
            def fin(q, acc, h=h, yh=yh):
                qs = slice(q * TQ, (q + 1) * TQ)
                if h % 2 == 0:
                    self.finish_softmax(q, acc, YG.ap[0:64, h // 2, qs], [YG.b[q]], cpeng="dve")
                else:
                    self.finish_softmax(q, acc, yh.ap[0:64, qs], [yh.b[q]], cpeng="dve")
                    S.dma("sp", YG.ap[64:128, h // 2, qs], yh.ap[0:64, qs], yh.b[q], reads=[yh.b[q]], writes=[YG.b[q]])
            self.attn_head(lambda j, kh=kh: (kh.ap[0:96, j * 128:(j + 1) * 128], [kh.b[j // 4]]),
                           lambda lo, hi, qh=qh: (qh.ap[0:96, lo:hi], [qh.b[lo // TQ]]), 96,
                           lambda j, vh=vh: (vh.ap[:, j, :], [vh.b0]), "mla", 96.0 ** -0.5, fin)

        for _ in head_proj(0):
            pass
        for h in range(4):
            gen = head_proj(h + 1) if h + 1 < 4 else iter(())
            self.attn_filler = lambda gen=gen: next(gen, None)
            head_attn(h)
            self.attn_filler = None
            for _ in gen:
                pass
        self.group_finish(l, 0, YG)

    def lru_setup(self, l):
        d, S = self.d, self.S
        L = {}
        L["XL"] = self.carve("XL", [128, KC, TQ], dt=BF16)
        W = L["W"] = self.carve("Wl", [128, KC, 512], dt=BF16)
        L["wbd"] = [[self.carve("wbd%d%d" % (c, k), [128, 128]) for k in range(2)] for c in range(2)]
        sp = L["sp"] = self.carve("spl", [128, 8])
        L["XB"] = [self.carve("XBl%d" % c, [128, TQ + 4]) for c in range(2)]
        L["H"] = self.carve("Hl", [128, 2])
        L["hb"] = self.carve("hbl", [128, 4])
        for nm in ("GT", "XC", "A", "I", "M", "Qp", "Tg"):
            L[nm] = self.carve(nm + "l", [128, TQ])
        L["YGt"] = self.carve("YGt", [128, 2, TQ])
        L["sq"] = self.carve("sql", [128, 2, TQ], nb=2, dt=BF16)
        L["rstd"] = self.carve("rstdl", [128, TQ])
        pvb = self.cst["pvec"].b0
        lam = self.pv(("lam", l))
        for c in range(2):
            self.load(W, self.win(l, O_LX + c * 128, 128), ap=W.ap[:, :, c * 128:(c + 1) * 128], queue="pool")
            self.load(W, self.win(l, O_LG + c * 128, 128), ap=W.ap[:, :, 256 + c * 128:256 + (c + 1) * 128], queue="pool")
            for k, wsrc in enumerate((d["wa"], d["wx"])):
                t = L["wbd"][c][k]
                S.op("dve", lambda e, t=t: e.memset(t.ap, 0.0), writes=[t.b0])
                self.load(t, wsrc[l, 2 * c], ap=t.ap[0:64, 0:64])
                self.load(t, wsrc[l, 2 * c + 1], ap=t.ap[64:128, 64:128])
            x_, t_, nsp = sp.ap[:, 4 * c:4 * c + 1], sp.ap[:, 4 * c + 1:4 * c + 2], sp.ap[:, 4 * c + 2:4 * c + 3]
            self.act(x_, lam(c), AF.Exp, reads=[pvb], writes=[sp.b0], scale=-1.0)
            self.ts("dve", t_, x_, -0.25, 1.0 / 3.0, ALU.mult, ALU.add, reads=[sp.b0], writes=[sp.b0])
            self.tt("dve", t_, t_, x_, ALU.mult, reads=[sp.b0], writes=[sp.b0])
            self.ts("dve", t_, t_, -1.0, 0.5, ALU.mult, ALU.add, reads=[sp.b0], writes=[sp.b0])
            self.tt("dve", t_, t_, x_, ALU.mult, reads=[sp.b0], writes=[sp.b0])
            self.ts("dve", t_, t_, -1.0, 1.0, ALU.mult, ALU.add, reads=[sp.b0], writes=[sp.b0])
            self.tt("dve", t_, t_, x_, ALU.mult, reads=[sp.b0], writes=[sp.b0])
            self.ts("dve", nsp, t_, -8.0, 0.0, ALU.mult, ALU.add, reads=[sp.b0], writes=[sp.b0])
            self.ts("dve", sp.ap[:, 4 * c + 3:4 * c + 4], t_, -4.0, 0.0, ALU.mult, ALU.add, reads=[sp.b0], writes=[sp.b0])
            hb = L["hb"]
            self.ts("dve", hb.ap[:, 2 * c:2 * c + 1], self.pv(("ba", l))(c), 0.5, 0.0, ALU.mult, ALU.add, reads=[pvb], writes=[hb.b0])
            self.ts("dve", hb.ap[:, 2 * c + 1:2 * c + 2], self.pv(("bx", l))(c), 0.5, 0.0, ALU.mult, ALU.add, reads=[pvb], writes=[hb.b0])
            xb = L["XB"][c]
            S.op("dve", lambda e, xb=xb: e.memset(xb.ap[:, 0:3], 0.0), writes=[xb.b0])
        self.L = L

    def lru_tile(self, l, q):
        S, L = self.S, self.L
        qs = slice(q * TQ, (q + 1) * TQ)
        pvb = self.cst["pvec"].b0
        ones = self.cst["onesb"]
        cw, cb, ba_, bx_ = (self.pv((k, l)) for k in ("cw", "cb", "ba", "bx"))
        on = self.pv(("on", l))
        XL, W, sp, H = L["XL"], L["W"], L["sp"], L["H"]
        GT, XC, A, I, M, Qp, Tg, YGt, sq, rstd = (L[k] for k in ("GT", "XC", "A", "I", "M", "Qp", "Tg", "YGt", "sq", "rstd"))
        self.cp("dve", XL.ap, self.XT[:, :, qs], reads=self.XTB[q], writes=[XL.b0])
        yield
        for c in range(2):
            XB = L["XB"][c]
            nsp = sp.ap[:, 4 * c + 2:4 * c + 3]
            bk = self.bank()
            self.mmg(bk, self.ps[bk][:, :], [(W.ap[:, kc, c * 128:(c + 1) * 128], XL.ap[:, kc, :]) for kc in range(KC)],
                     reads=[W.b0, XL.b0])
            self.cp("act", XB.ap[:, 3:3 + TQ], self.ps[bk][:, :], reads=[self.psb[bk]], writes=[XB.b0])
            yield
            bk = self.bank()
            self.mmg(bk, self.ps[bk][:, :], [(W.ap[:, kc, 256 + c * 128:256 + (c + 1) * 128], XL.ap[:, kc, :]) for kc in range(KC)],
                     reads=[W.b0, XL.b0])
            self.cp("dve", GT.ap, self.ps[bk][:, :], reads=[self.psb[bk]], writes=[GT.b0])
            yield
            self.ts("dve", XC.ap, XB.ap[:, 0:TQ], cw(0 * 2 + c), cb(c), ALU.mult, ALU.add, reads=[XB.b0, pvb], writes=[XC.b0])
            self.tt("dve", Tg.ap, GT.ap, GT.ap, ALU.mult, reads=[GT.b0], writes=[Tg.b0])
            yield
            self.stt(XC.ap, XB.ap[:, 1:1 + TQ], cw(1 * 2 + c), XC.ap, ALU.mult, ALU.add, reads=[XB.b0, XC.b0, pvb], writes=[XC.b0])
            self.ts("dve", Tg.ap, Tg.ap, 0.044715, 1.0, ALU.mult, ALU.add, reads=[Tg.b0], writes=[Tg.b0])
            yield
            self.stt(XC.ap, XB.ap[:, 2:2 + TQ], cw(2 * 2 + c), XC.ap, ALU.mult, ALU.add, reads=[XB.b0, XC.b0, pvb], writes=[XC.b0])
            self.tt("dve", Tg.ap, Tg.ap, GT.ap, ALU.mult, reads=[Tg.b0, GT.b0], writes=[Tg.b0])
            yield
            self.stt(XC.ap, XB.ap[:, 3:3 + TQ], cw(3 * 2 + c), XC.ap, ALU.mult, ALU.add, reads=[XB.b0, XC.b0, pvb], writes=[XC.b0])
            self.act(Tg.ap, Tg.ap, AF.Tanh, reads=[Tg.b0], writes=[Tg.b0], scale=0.7978845608028654)
            yield
            self.cp("dve", XB.ap[:, 0:3], XB.ap[:, TQ:TQ + 3], reads=[XB.b0], writes=[XB.b0])
            yield
            wbd = L["wbd"][c]
            b1, b2 = self.bank(), self.bank()
            self.mmg(b1, self.ps[b1][:, :], [(wbd[0].ap, XC.ap)], reads=[wbd[0].b0, XC.b0])
            self.mmg(b2, self.ps[b2][:, :], [(wbd[1].ap, XC.ap)], reads=[wbd[1].b0, XC.b0])
            hb = L["hb"]
            self.act(A.ap, self.ps[b1][:, :], AF.Tanh, reads=[self.psb[b1], hb.b0], writes=[A.b0], bias=hb.ap[:, 2 * c:2 * c + 1], scale=0.5)
            self.act(I.ap, self.ps[b2][:, :], AF.Tanh, reads=[self.psb[b2], hb.b0], writes=[I.b0], bias=hb.ap[:, 2 * c + 1:2 * c + 2], scale=0.5)
            yield
            self.stt(Tg.ap, Tg.ap, 1.0, GT.ap, ALU.add, ALU.mult, reads=[Tg.b0, GT.b0], writes=[Tg.b0])
            yield
            hnsp = sp.ap[:, 4 * c + 3:4 * c + 4]
            self.ts("dve", A.ap, A.ap, hnsp, hnsp, ALU.mult, ALU.add, reads=[A.b0, sp.b0], writes=[A.b0])
            yield
            self.ts("dve", Qp.ap, A.ap, -1.0 / 120.0, 0.0, ALU.mult, ALU.add, reads=[A.b0], writes=[Qp.b0])
            yield
            for cc in (-1.0 / 24.0, -1.0 / 6.0, -0.5, -1.0):
                self.stt(Qp.ap, Qp.ap, cc, A.ap, ALU.add, ALU.mult, reads=[Qp.b0, A.b0], writes=[Qp.b0])
                yield
            self.ts("dve", A.ap, Qp.ap, -1.0, 1.0, ALU.mult, ALU.add, reads=[Qp.b0], writes=[A.b0])
            yield
            self.stt(M.ap, Qp.ap, 2.0, Qp.ap, ALU.subtract, ALU.mult, reads=[Qp.b0], writes=[M.b0])
            yield
            self.act(M.ap, M.ap, AF.Sqrt, reads=[M.b0], writes=[M.b0], scale=-0.25)
            yield
            self.stt(I.ap, I.ap, 1.0, XC.ap, ALU.add, ALU.mult, reads=[I.b0, XC.b0], writes=[I.b0])
            yield
            self.tt("dve", I.ap, I.ap, M.ap, ALU.mult, reads=[I.b0, M.b0], writes=[I.b0])
            yield
            init = 0.0 if q == 0 else H.ap[:, c:c + 1]
            S.op("dve", lambda e, init=init: e.tensor_tensor_scan(M.ap, A.ap, I.ap, init, ALU.mult, ALU.add),
                 reads=[A.b0, I.b0, H.b0], writes=[M.b0])
            yield
            self.cp("dve", H.ap[:, c:c + 1], M.ap[:, TQ - 1:TQ], reads=[M.b0], writes=[H.b0])
            self.stt(YGt.ap[:, c, :], M.ap, 0.5, Tg.ap, ALU.mult, ALU.mult, reads=[M.b0, Tg.b0], writes=[YGt.b0])
            yield
        bk = self.bank()
        for c in range(2):
            self.act(sq.ap[:, c, :], YGt.ap[:, c, :], AF.Square, reads=[YGt.b0], writes=[sq.b[c]])
            S.op("pe", (lambda e, o=self.ps[bk][:, :], r=sq.ap[:, c, :], s=(c == 0), p=(c == 1):
                        e.matmul(o, ones.ap, r, start=s, stop=p)),
                 reads=[sq.b[c], ones.b0], writes=[self.psb[bk]], signal=True)
            yield
        self.act(rstd.ap, self.ps[bk][:, :], AF.Ln, reads=[self.psb[bk]], writes=[rstd.b0], bias=1e-6, scale=1.0 / 256)
        yield
        self.act(rstd.ap, rstd.ap, AF.Exp, reads=[rstd.b0], writes=[rstd.b0], scale=-0.5)
        yield
        for c in range(2):
            self.stt(YGt.ap[:, c, :], YGt.ap[:, c, :], on(2 + c), rstd.ap, ALU.mult, ALU.mult,
                     reads=[YGt.b0, rstd.b0, pvb], writes=[YGt.b0])
            yield
        S.dma("sp", self.d["yscr"].rearrange("(c p) s -> p c s", p=128)[:, 2:4, qs], YGt.ap, YGt.b0,
              reads=[YGt.b0], writes=[self.YSB[q]])
        yield

    def mixer_lru(self, l):
        d, S = self.d, self.S
        YG = self.carve("YGb", [128, 2, SEQ], nb=NQ)
        Wx = self.carve("Wlx", [128, KC, 128], dt=BF16)
        Wg = self.carve("Wlg", [128, KC, 128], dt=BF16)
        wbd = [self.carve("wbd%d" % k, [128, 128]) for k in range(2)]
        XB = self.carve("XB", [128, SEQ + 4], nb=NQ)
        GT = self.carve("GT", [128, SEQ], nb=NQ)
        XC = self.carve("XC", [128, SEQ])
        A = self.carve("A", [128, SEQ])
        I = self.carve("I", [128, SEQ])
        M = self.carve("M", [128, SEQ])
        Qp = self.carve("Qp", [128, SEQ])
        sp = self.carve("sp", [128, 8])
        pvb = self.cst["pvec"].b0
        cw, cb, ba_, bx_, lam = (self.pv((k, l)) for k in ("cw", "cb", "ba", "bx", "lam"))
        for c in range(2):
            self.load(Wx, self.win(l, O_LX + c * 128, 128), queue="pool")
            self.load(Wg, self.win(l, O_LG + c * 128, 128), queue="pool")
            for k, wsrc in enumerate((d["wa"], d["wx"])):
                S.op("pool", lambda e, t=wbd[k]: e.memset(t.ap, 0.0), writes=[wbd[k].b0])
                self.load(wbd[k], wsrc[l, 2 * c], ap=wbd[k].ap[0:64, 0:64])
                self.load(wbd[k], wsrc[l, 2 * c + 1], ap=wbd[k].ap[64:128, 64:128])
            x_, t_, nsp = sp.ap[:, 0:1], sp.ap[:, 1:2], sp.ap[:, 2:3]
            self.act(x_, lam(c), AF.Exp, reads=[pvb], writes=[sp.b0], scale=-1.0)
            self.ts("dve", t_, x_, -0.25, 1.0 / 3.0, ALU.mult, ALU.add, reads=[sp.b0], writes=[sp.b0])
            self.tt("dve", t_, t_, x_, ALU.mult, reads=[sp.b0], writes=[sp.b0])
            self.ts("dve", t_, t_, -1.0, 0.5, ALU.mult, ALU.add, reads=[sp.b0], writes=[sp.b0])
            self.tt("dve", t_, t_, x_, ALU.mult, reads=[sp.b0], writes=[sp.b0])
            self.ts("dve", t_, t_, -1.0, 1.0, ALU.mult, ALU.add, reads=[sp.b0], writes=[sp.b0])
            self.tt("dve", t_, t_, x_, ALU.mult, reads=[sp.b0], writes=[sp.b0])
            self.ts("dve", nsp, t_, -8.0, 0.0, ALU.mult, ALU.add, reads=[sp.b0], writes=[sp.b0])
            S.op("pool", lambda e: e.memset(XB.ap[:, 0:3], 0.0), writes=[XB.b[0]])
            self.proj_fm(Wx, 128, lambda q: (XB.ap[:, 3 + q * TQ:3 + (q + 1) * TQ], [XB.b[q]]), evac="act")
            self.proj_fm(Wg, 128, lambda q: (GT.ap[:, q * TQ:(q + 1) * TQ], [GT.b[q]]), evac="dve")
            xb_all = XB.b
            self.ts("dve", XC.ap, XB.ap[:, 0:SEQ], cw(0 * 2 + c), cb(c), ALU.mult, ALU.add, reads=xb_all + [pvb], writes=[XC.b0])
            for j in range(1, 4):
                self.stt(XC.ap, XB.ap[:, j:j + SEQ], cw(j * 2 + c), XC.ap, ALU.mult, ALU.add, reads=xb_all + [XC.b0, pvb], writes=[XC.b0])
            for q in range(NQ):
                qs = slice(q * TQ, (q + 1) * TQ)
                b1, b2 = self.bank(), self.bank()
                self.mmg(b1, self.ps[b1][:, :], [(wbd[0].ap, XC.ap[:, qs])], reads=[wbd[0].b0, XC.b0])
                self.mmg(b2, self.ps[b2][:, :], [(wbd[1].ap, XC.ap[:, qs])], reads=[wbd[1].b0, XC.b0])
                self.act(A.ap[:, qs], self.ps[b1][:, :], AF.Sigmoid, reads=[self.psb[b1], pvb], writes=[A.b0], bias=ba_(c))
                self.act(I.ap[:, qs], self.ps[b2][:, :], AF.Sigmoid, reads=[self.psb[b2], pvb], writes=[I.b0], bias=bx_(c))
            gall = GT.b
            Tg = XB.ap[:, 0:SEQ]
            tb = XB.b
            self.tt("pool", Tg, GT.ap, GT.ap, ALU.mult, reads=gall + [XC.b0], writes=tb)
            self.ts("pool", Tg, Tg, 0.044715, 1.0, ALU.mult, ALU.add, reads=tb, writes=tb)
            self.tt("pool", Tg, Tg, GT.ap, ALU.mult, reads=tb + gall, writes=tb)
            self.act(Tg, Tg, AF.Sigmoid, reads=tb, writes=tb, scale=1.5957691216057308)
            self.tt("pool", Tg, Tg, GT.ap, ALU.mult, reads=tb + gall, writes=tb)
            self.ts("dve", A.ap, A.ap, nsp, 0.0, ALU.mult, ALU.add, reads=[A.b0, sp.b0], writes=[A.b0])
            self.ts("dve", Qp.ap, A.ap, -1.0 / 120.0, 0.0, ALU.mult, ALU.add, reads=[A.b0], writes=[Qp.b0])
            for cc in (-1.0 / 24.0, -1.0 / 6.0, -0.5, -1.0):
                self.stt(Qp.ap, Qp.ap, cc, A.ap, ALU.add, ALU.mult, reads=[Qp.b0, A.b0], writes=[Qp.b0])
            self.ts("dve", A.ap, Qp.ap, -1.0, 1.0, ALU.mult, ALU.add, reads=[Qp.b0], writes=[A.b0])
            self.stt(M.ap, Qp.ap, 2.0, Qp.ap, ALU.subtract, ALU.mult, reads=[Qp.b0], writes=[M.b0])
            self.act(M.ap, M.ap, AF.Sqrt, reads=[M.b0], writes=[M.b0], scale=-1.0)
            self.tt("dve", I.ap, I.ap, XC.ap, ALU.mult, reads=[I.b0, XC.b0], writes=[I.b0])
            self.tt("dve", I.ap, I.ap, M.ap, ALU.mult, reads=[I.b0, M.b0], writes=[I.b0])
            S.op("dve", lambda e: e.tensor_tensor_scan(M.ap, A.ap, I.ap, 0.0, ALU.mult, ALU.add),
                 reads=[A.b0, I.b0], writes=[M.b0])
            self.tt("dve", YG.ap[:, c, :], M.ap, Tg, ALU.mult, reads=[M.b0] + tb, writes=YG.b)
        self.group_finish(l, 1, YG)

    def proj_qz(self, Wq, QZ):
        S = self.S
        S.op("pool", lambda e: e.memset(QZ[0].ap[64:128, :], 0.0), writes=QZ[0].b)
        S.op("pool", lambda e: e.memset(QZ[1].ap[0:64, :], 0.0), writes=QZ[1].b)
        XBF = self.XBF
        for q in range(NQ):
            qs = slice(q * TQ, (q + 1) * TQ)
            bk = self.bank()
            self.mmg(bk, self.ps[bk][:, :], [(Wq.ap[:, kc, :], XBF.ap[:, kc, qs]) for kc in range(KC)],
                     reads=[Wq.b0, XBF.b[q]])
            self.cp("act", QZ[0].ap[0:64, qs], self.ps[bk][0:64, :], reads=[self.psb[bk]], writes=[QZ[0].b[q]])
            self.cp("dve", QZ[1].ap[64:128, qs], self.ps[bk][64:128, :], reads=[self.psb[bk]], writes=[QZ[1].b[q]])

    def pair_proj(self, l, offs, hp, names):
        out = []
        for nm, o in zip(names, offs):
            t = self.carve(nm, [128, KC, 128], dt=BF16)
            self.load(t, self.win(l, o + hp * 128, 128), queue="pool")
            out.append(t)
        return out

    def v_tokmajor(self, Wv, VP):
        S = self.S
        S.op("pool", lambda e: e.memset(VP.ap[:, :, :, 64:128], 1.0), writes=[VP.b0])
        for g4 in range(4):
            bk = self.bank()
            for t4 in range(4):
                tc = g4 * 4 + t4
                self.mmg(bk, self.ps[bk][:, t4 * 128:(t4 + 1) * 128],
                         [(self.XBF.ap[:, kc, tc * 128:(tc + 1) * 128], Wv.ap[:, kc, :]) for kc in range(KC)],
                         reads=[self.XBF.b[g4], Wv.b0])
            self.cp("dve", VP.ap[:, g4 * 4:(g4 + 1) * 4, :, 0:64],
                    self.ps[bk][:, :].rearrange("p (a b c) -> p a b c", a=4, b=2),
                    reads=[self.psb[bk]], writes=[VP.b0])

    def mixer_dil(self, l):
        S = self.S
        YG = self.carve("YGc", [128, 2, SEQ], nb=NQ)
        YH = self.carve("YHc", [128, SEQ], nb=NQ)
        sets = []
        for hp in range(2):
            Wq, Wk, Wv = self.pair_proj(l, (O_SQ, O_SK, O_SV), hp, ("Wq%d" % hp, "Wk%d" % hp, "Wv%d" % hp))
            QZ = [self.carve("QZ%d_%d" % (hp, k), [128, SEQ], nb=NQ, dt=BF16) for k in range(2)]
            KP = self.carve("KP%d" % hp, [128, SEQ], nb=NQ, dt=BF16)
            VP = self.carve("VP%d" % hp, [128, 16, 2, 128], dt=BF16)
            sets.append((Wq, Wk, Wv, QZ, KP, VP))

        def proj(hp):
            Wq, Wk, Wv, QZ, KP, VP = sets[hp]
            self.proj_qz(Wq, QZ)
            self.proj_fm(Wk, 128, lambda q: (KP.ap[:, q * TQ:(q + 1) * TQ], [KP.b[q]]), evac="dve")
            self.v_tokmajor(Wv, VP)

        def attn(hp):
            Wq, Wk, Wv, QZ, KP, VP = sets[hp]
            for h2 in range(2):
                def fin(q, acc, h2=h2, hp=hp):
                    qs = slice(q * TQ, (q + 1) * TQ)
                    if h2 == 0:
                        self.finish_softmax(q, acc, YG.ap[0:64, hp, qs], [YG.b[q]])
                    else:
                        self.finish_softmax(q, acc, YH.ap[0:64, qs], [YH.b[q]])
                        S.dma("sp", YG.ap[64:128, hp, qs], YH.ap[0:64, qs], YH.b[q], reads=[YH.b[q]], writes=[YG.b[q]])
                self.attn_head(lambda j: (KP.ap[:, j * 128:(j + 1) * 128], [KP.b[j // 4]]),
                               lambda lo, hi, h2=h2: (QZ[h2].ap[:, lo:hi], [QZ[h2].b[lo // TQ]]), 128,
                               lambda j, h2=h2: (VP.ap[:, j, h2, :], [VP.b0]), "dil", 0.125, fin)
        proj(0)
        proj(1)
        attn(0)
        attn(1)
        self.group_finish(l, 2, YG)

    def mixer_mlstm(self, l):
        d, S = self.d, self.S
        YG = self.carve("YGd", [128, 2, SEQ], nb=NQ)
        YH = self.carve("YHd", [128, SEQ], nb=NQ)
        CS = self.carve("CS", [128, SEQ], nb=NQ)
        BIAS = self.carve("BIAS", [128, 64])
        small = self.carve("small", [128, 4])
        base1 = self.aoff
        Wg = self.carve("Wgt", [128, KC, 8], dt=BF16)
        DB = self.carve("DB", [128, SEQ], nb=NQ)
        E1 = self.carve("E1", [128, SEQ])
        ONE4 = self.carve("one4", [128, SEQ])
        pvb = self.cst["pvec"].b0
        ident, selh = self.cst["ident"], self.cst["selh"]
        self.load(Wg, self.win(l, O_MI, 8), queue="pool")
        bi, bf = self.pv(("bi", l)), self.pv(("bf", l))
        nbf = small.ap[0:4, 0:1]
        self.ts("dve", nbf, bf(0, 0, 4), -1.0, 0.0, ALU.mult, ALU.add, reads=[pvb], writes=[small.b0])
        S.op("pool", lambda e: e.memset(ONE4.ap[0:4, :], 1.0), writes=[ONE4.b0])
        S.op("pool", lambda e: e.memset(CS.ap, 0.0), writes=CS.b)
        for q in range(NQ):
            qs = slice(q * TQ, (q + 1) * TQ)
            b1, b2 = self.bank(), self.bank()
            self.mmg(b1, self.ps[b1][0:4, :], [(Wg.ap[:, kc, 0:4], self.XBF.ap[:, kc, qs]) for kc in range(KC)], reads=[Wg.b0, self.XBF.b[q]])
            self.mmg(b2, self.ps[b2][0:4, :], [(Wg.ap[:, kc, 4:8], self.XBF.ap[:, kc, qs]) for kc in range(KC)], reads=[Wg.b0, self.XBF.b[q]])
            self.act(E1.ap[0:4, qs], self.ps[b2][0:4, :], AF.Exp, reads=[self.psb[b2], small.b0], writes=[E1.b0], bias=nbf, scale=-1.0)
            self.act(E1.ap[0:4, qs], E1.ap[0:4, qs], AF.Ln, reads=[E1.b0], writes=[E1.b0], bias=1.0, scale=1.0)
            self.ts("dve", DB.ap[0:4, qs], self.ps[b1][0:4, :], bi(0, 0, 4), 0.0, ALU.add, ALU.add,
                    reads=[self.psb[b1], pvb], writes=[DB.b[q]])
        S.op("dve", lambda e: e.tensor_tensor_scan(CS.ap[0:4, :], ONE4.ap[0:4, :], E1.ap[0:4, :], 0.0, ALU.mult, ALU.add),
             reads=[ONE4.b0, E1.b0], writes=CS.b)
        self.tt("dve", DB.ap[0:4, :], DB.ap[0:4, :], CS.ap[0:4, :], ALU.add, reads=DB.b + CS.b, writes=DB.b)
        bk = self.bank()
        for tc in range(16):
            S.op("pe", (lambda e, o=self.ps[bk][:, tc * 4:(tc + 1) * 4], i=DB.ap[0:4, tc * 128:(tc + 1) * 128]:
                        e.transpose(o, i, ident.ap[0:4, 0:4])),
                 reads=DB.b + [ident.b0], writes=[self.psb[bk]], signal=(tc == 15))
        self.cp("dve", BIAS.ap[:, 0:64], self.ps[bk][:, 0:64], reads=[self.psb[bk]], writes=[BIAS.b0])
        self.dump("cs%d" % l, CS.ap, CS.b, [128, SEQ])
        self.dump("bias%d" % l, BIAS.ap, BIAS.b, [128, 64])
        self.rewind(base1, soft=True)
        CSR = [self.carve("CSR0", [128, SEQ], nb=NQ)] * 2
        OG = self.carve("OG", [128, SEQ], nb=NQ)
        Wo = self.carve("Wo", [128, KC, 64], dt=BF16)
        base = self.aoff
        for hp in range(2):
            self.rewind(base, soft=True)
            Wq, Wk, Wv = self.pair_proj(l, (O_MQ, O_MK, O_MV), hp, ("Wq", "Wk", "Wv"))
            QZ = [self.carve("QZ%d" % k, [128, SEQ], nb=NQ, dt=BF16) for k in range(2)]
            KP = self.carve("KP", [128, SEQ], nb=NQ, dt=BF16)
            VP = self.carve("VP", [128, 16, 2, 128], dt=BF16)
            self.proj_qz(Wq, QZ)
            self.proj_fm(Wk, 128, lambda q: (KP.ap[:, q * TQ:(q + 1) * TQ], [KP.b[q]]), evac="dve")
            self.v_tokmajor(Wv, VP)
            for h2 in range(2):
                h = hp * 2 + h2
                pb = 64 * h2
                csr = CSR[h2]
                for q in range(NQ):
                    qs = slice(q * TQ, (q + 1) * TQ)
                    bk = self.bank()
                    self.mmg(bk, self.ps[bk][:, :], [(selh.ap[:, h * 128:(h + 1) * 128], CS.ap[:, qs])], reads=[selh.b0, CS.b[q]])
                    self.cp("act", csr.ap[:, qs], self.ps[bk][:, :], reads=[self.psb[bk]], writes=[csr.b[q]])
                self.load(Wo, self.win(l, O_MO + h * 64, 64), queue="pool")
                for q in range(NQ):
                    qs = slice(q * TQ, (q + 1) * TQ)
                    bk = self.bank()
                    self.mmg(bk, self.ps[bk][0:64, :], [(Wo.ap[:, kc, :], self.XBF.ap[:, kc, qs]) for kc in range(KC)], reads=[Wo.b0, self.XBF.b[q]])
                    self.act(OG.ap[0:64, qs], self.ps[bk][0:64, :], AF.Sigmoid, reads=[self.psb[bk]], writes=[OG.b[q]])

                def fin(q, acc, h2=h2, hp=hp):
                    qs = slice(q * TQ, (q + 1) * TQ)
                    gate = (OG.ap[0:64, qs], [OG.b[q]])
                    if h2 == 0:
                        self.finish_softmax(q, acc, YG.ap[0:64, hp, qs], [YG.b[q]], mlstm=True, gate=gate)
                    else:
                        self.finish_softmax(q, acc, YH.ap[0:64, qs], [YH.b[q]], mlstm=True, gate=gate)
                        S.dma("sp", YG.ap[64:128, hp, qs], YH.ap[0:64, qs], YH.b[q], reads=[YH.b[q]], writes=[YG.b[q]])
                self.attn_head(lambda j: (KP.ap[:, j * 128:(j + 1) * 128], [KP.b[j // 4]]),
                               lambda lo, hi, h2=h2: (QZ[h2].ap[:, lo:hi], [QZ[h2].b[lo // TQ]]), 128,
                               lambda j, h2=h2: (VP.ap[:, j, h2, :], [VP.b0]), "mlstm", 0.125, fin,
                               aux=(csr, lambda j, h=h: (BIAS.ap[:, j * 4 + h:j * 4 + h + 1], [BIAS.b0])))
        self.group_finish(l, 3, YG)

    def phase_outproj(self, l):
        d, S = self.d, self.S
        self.reset_arena(soft=True)
        NBO = 3
        YT = self.carve("YT", [128, 2, KC, TQ], nb=2, dt=BF16)
        wo = [self.carve("wo%d" % k, [128, KC, 128], dt=BF16) for k in range(NBO)]
        sq = self.carve("sq", [128, 2, 2 * TQ], nb=2, dt=BF16)
        mean = self.carve("mean", [128, TQ])
        var = self.carve("var", [128, TQ])
        rstd = self.carve("rstd", [128, TQ])
        wov = d["wout"][l].rearrange("(kc p) m -> p kc m", p=128)
        ysv = d["yscr"].rearrange("(c p) s -> p c s", p=128)
        jobs = [(q, dc) for q in range(NQ) for dc in range(KC)]
        st = [0]

        def issue(upto):
            while st[0] <= upto and st[0] < len(jobs):
                n = st[0]
                st[0] += 1
                q, dc = jobs[n]
                if dc == 0:
                    yi = q % 2
                    S.dma("pool", YT.ap[:, yi, :, :], ysv[:, :, q * TQ:(q + 1) * TQ], YT.b[yi], reads=[self.YSB[q]], writes=[YT.b[yi]])
                t = wo[n % NBO]
                S.dma("pool", t.ap, wov[:, :, dc * 128:(dc + 1) * 128], t.b0, writes=[t.b0])
        n = 0
        for q in range(NQ):
            qs = slice(q * TQ, (q + 1) * TQ)
            yi = q % 2
            for dc in range(KC):
                issue(n + NBO - 1)
                t = wo[n % NBO]
                n += 1
                bk = self.bank()
                self.mmg(bk, self.ps[bk][:, :], [(t.ap[:, kc, :], YT.ap[:, yi, kc, :]) for kc in range(KC)], reads=[t.b0, YT.b[yi]])
                self.stt(self.XT[:, dc, qs], self.XT[:, dc, qs], ALPHA, self.ps[bk][:, :], ALU.mult, ALU.add,
                         reads=[self.XTB[q][dc], self.psb[bk]], writes=[self.XTB[q][dc]])
            self.layernorm(q, ("lng", l, 1), ("lnb", l, 1), 1.0e-5, (sq, mean, var, rstd))


_CACHE = {}


def _inmaps(inputs):
    cs = _consts()
    pv = _pvec(inputs)
    maps = []
    shared = {
        "ffn_w1": np.ascontiguousarray(inputs["ffn_w1"], np.float32),
        "ffn_w3": np.ascontiguousarray(inputs["ffn_w3"], np.float32),
        "ffn_w2": np.ascontiguousarray(inputs["ffn_w2"], np.float32),
        "w_in": np.ascontiguousarray(inputs["w_in"], np.float32),
        "mla_w_uq": np.ascontiguousarray(inputs["mla_w_uq"], np.float32),
        "mla_w_ukv": np.ascontiguousarray(inputs["mla_w_ukv"], np.float32),
        "lru_w_a": np.ascontiguousarray(inputs["lru_w_a"], np.float32),
        "lru_w_x": np.ascontiguousarray(inputs["lru_w_x"], np.float32),
        "w_out": np.ascontiguousarray(inputs["w_out"], np.float32),
        "pvec": pv,
    }
    for k, v in cs.items():
        shared["c_" + k] = v
    return shared


def kernel(**inputs):
    inputs = {k: np.asarray(v) for k, v in inputs.items()}
    if "nc" not in _CACHE:
        _CACHE["nc"] = Builder().build()
    nc = _CACHE["nc"]
    shared = _inmaps(inputs)
    x = np.ascontiguousarray(inputs["x"], np.float32)
    in_maps = []
    for b in range(8):
        m = dict(shared)
        m["x"] = x[b]
        in_maps.append(m)
    res = run_bass_kernel_spmd(nc, in_maps, core_ids=list(range(8)))
    return np.stack([r["out"] for r in res.results], axis=0).astype(np.float32)
```
